# Optimizing a Trainium2 kernel written in Bass

```python
import numpy as np
import jax, jax.numpy as jnp
from jax import lax

D_MODEL = 1024
BATCH = 8
SEQ = 4096
DEPTH = 1

N_HEADS = 8
N_KV = 2
GQA = N_HEADS // N_KV
HEAD_DIM = 64
ATTN_WIDTH = N_HEADS * HEAD_DIM
KV_COLS = N_KV * HEAD_DIM
N_GATE = 3 * N_HEADS
CMP_LEN = 32
CMP_STRIDE = 16
CMP_HIDDEN = 4 * HEAD_DIM
SEL_BLOCK = 64
SEL_TOPK = 16
WINDOW = 512
Q_BLOCK = 128
FORCE_SCORE = 1.0e4
CONV_GROUPS = 8
CONV_WIDTH = D_MODEL - ATTN_WIDTH
CONV_K = 3
MIX_WIDTH = ATTN_WIDTH + CONV_WIDTH
N_IN = ATTN_WIDTH + 6 * KV_COLS + N_GATE + 3 * CONV_WIDTH
D_FF = 2816
EPS = 1e-6

kernel_name = 'hymba_nsa_shortconv_convffn_adaln'


def rms_norm(x, g):
    xf = x.astype(jnp.float32)
    y = xf * lax.rsqrt(jnp.mean(xf * xf, axis=-1, keepdims=True) + EPS)
    return (y * g.astype(jnp.float32)).astype(x.dtype)


def group_rms_norm(y, g, n_groups):
    shp = y.shape
    yg = y.reshape(shp[:-1] + (n_groups, shp[-1] // n_groups))
    return rms_norm(yg, g.reshape(n_groups, -1)).reshape(shp)


def causal_dwconv(u, w):
    k_taps, s_len = w.shape[0], u.shape[1]
    up = jnp.pad(u, ((0, 0), (k_taps - 1, 0), (0, 0)))
    return sum(up[:, j:j + s_len] * w[j] for j in range(k_taps))


def masked_softmax(s, mask):
    s = jnp.where(mask, s.astype(jnp.float32), -jnp.inf)
    m = jnp.max(s, axis=-1, keepdims=True)
    m = jnp.where(jnp.isfinite(m), m, 0.0)
    e = jnp.where(mask, jnp.exp(s - m), 0.0)
    return e / jnp.maximum(jnp.sum(e, axis=-1, keepdims=True), jnp.finfo(jnp.float32).tiny)


def alibi_slopes(n):
    return np.array([2.0 ** (-8.0 * (h + 1) / n) for h in range(n)], dtype=np.float32)


def compress(k, pos, w1, w2):
    s_len = k.shape[2]
    n_cmp = (s_len - CMP_LEN) // CMP_STRIDE + 1
    idx = np.arange(n_cmp)[:, None] * CMP_STRIDE + np.arange(CMP_LEN)[None, :]
    blocks = k[:, :, idx] + pos
    flat = blocks.reshape(blocks.shape[:3] + (CMP_LEN * HEAD_DIM,))
    return jax.nn.gelu(flat @ w1) @ w2


def nsa_mixer(q, kc, vc, ks, vs, kw, vw, gates):
    b_sz, s_len = q.shape[0], q.shape[1]
    n_cmp = kc.shape[2]
    n_blk = s_len // SEL_BLOCK
    n_sel = min(SEL_TOPK, n_blk)
    scale = HEAD_DIM ** -0.5
    slopes = jnp.asarray(alibi_slopes(N_HEADS).reshape(N_KV, GQA))
    c_end = jnp.arange(n_cmp) * CMP_STRIDE + CMP_LEN - 1
    cs = np.arange(n_cmp) * CMP_STRIDE
    bs = np.arange(n_blk) * SEL_BLOCK
    overlap = jnp.asarray(((cs[None, :] < bs[:, None] + SEL_BLOCK) &
                           (cs[None, :] + CMP_LEN > bs[:, None])).astype(np.float32))
    ks_blk = ks.reshape(b_sz, N_KV, n_blk, SEL_BLOCK, HEAD_DIM)
    vs_blk = vs.reshape(b_sz, N_KV, n_blk, SEL_BLOCK, HEAD_DIM)
    kw_pad = jnp.pad(kw, ((0, 0), (0, 0), (WINDOW, 0), (0, 0)))
    vw_pad = jnp.pad(vw, ((0, 0), (0, 0), (WINDOW, 0), (0, 0)))
    bi = jnp.arange(b_sz)[:, None, None, None]
    gi = jnp.arange(N_KV)[None, :, None, None]
    blk_ids = jnp.arange(n_blk)

    def block(i):
        t0 = i * Q_BLOCK
        tpos = t0 + jnp.arange(Q_BLOCK)
        qb = lax.dynamic_slice_in_dim(q, t0, Q_BLOCK, axis=1)
        gb = lax.dynamic_slice_in_dim(gates, t0, Q_BLOCK, axis=1)
        s = jnp.einsum('btghd,bgnd->bgthn', qb, kc).astype(jnp.float32) * scale
        dist = (tpos[:, None] - c_end[None, :]).astype(jnp.float32)
        s = s - slopes[None, :, None, :, None] * dist[None, None, :, None, :]
        p_c = masked_softmax(s, (c_end[None, :] <= tpos[:, None])[None, None, :, None, :])
        o_c = jnp.einsum('bgthn,bgnd->btghd', p_c.astype(vc.dtype), vc)
        imp = jnp.einsum('bgtn,jn->bgtj', p_c.sum(axis=3), overlap)
        cur = tpos // SEL_BLOCK
        valid = blk_ids[None, :] <= cur[:, None]
        forced = ((blk_ids[None, :] == 0) | (blk_ids[None, :] == cur[:, None]) |
                  (blk_ids[None, :] == cur[:, None] - 1))
        score = jnp.where(valid, imp, -1.0)
        score = jnp.where(forced, FORCE_SCORE, score)
        _, idx = lax.top_k(score, n_sel)
        k_sel = ks_blk[bi, gi, idx]
        v_sel = vs_blk[bi, gi, idx]
        kpos = idx[..., None] * SEL_BLOCK + jnp.arange(SEL_BLOCK)
        s = jnp.einsum('btghd,bgtnkd->bgthnk', qb, k_sel).astype(jnp.float32) * scale
        dist = (tpos[None, None, :, None, None] - kpos).astype(jnp.float32)
        s = s - slopes[None, :, None, :, None, None] * dist[:, :, :, None]
        sh = s.shape
        mask_s = (dist >= 0).reshape(sh[:3] + (-1,))[:, :, :, None, :]
        p_s = masked_softmax(s.reshape(sh[:4] + (-1,)), mask_s).reshape(sh)
        o_s = jnp.einsum('bgthnk,bgtnkd->btghd', p_s.astype(v_sel.dtype), v_sel)
        kwb = lax.dynamic_slice_in_dim(kw_pad, t0, WINDOW + Q_BLOCK, axis=2)
        vwb = lax.dynamic_slice_in_dim(vw_pad, t0, WINDOW + Q_BLOCK, axis=2)
        kpos_w = t0 - WINDOW + jnp.arange(WINDOW + Q_BLOCK)
        dist_w = tpos[:, None] - kpos_w[None, :]
        mask_w = (dist_w >= 0) & (dist_w < WINDOW) & (kpos_w[None, :] >= 0)
        s = jnp.einsum('btghd,bgkd->bgthk', qb, kwb).astype(jnp.float32) * scale
        s = s - slopes[None, :, None, :, None] * dist_w.astype(jnp.float32)[None, None, :, None, :]
        p_w = masked_softmax(s, mask_w[None, None, :, None, :])
        o_w = jnp.einsum('bgthk,bgkd->btghd', p_w.astype(vwb.dtype), vwb)
        return gb[..., 0:1] * o_c + gb[..., 1:2] * o_s + gb[..., 2:3] * o_w

    out = lax.map(block, jnp.arange(s_len // Q_BLOCK))
    return jnp.moveaxis(out, 0, 1).reshape(b_sz, s_len, ATTN_WIDTH)


def setup_inputs(seed: int = 0) -> dict:
    key = jax.random.key(seed)
    k = jax.random.split(key, 32)
    L = DEPTH

    def nrm(kk, shape, std):
        return jax.random.normal(kk, shape, jnp.float32) * std

    return {
        'x': nrm(k[0], (BATCH, SEQ, D_MODEL), 1.0),
        'c': nrm(k[1], (BATCH, D_MODEL), 1.0),
        'w_ada': nrm(k[2], (L, D_MODEL, 6 * D_MODEL), 0.5 * D_MODEL ** -0.5),
        'b_ada': nrm(k[3], (L, 6 * D_MODEL), 0.01),
        'norm1_g': 1.0 + nrm(k[4], (L, D_MODEL), 0.02),
        'w_in': nrm(k[5], (L, D_MODEL, N_IN), D_MODEL ** -0.5),
        'b_gate': nrm(k[6], (L, N_GATE), 0.01),
        'q_norm_g': 1.0 + nrm(k[7], (L, HEAD_DIM), 0.02),
        'k_norm_cmp_g': 1.0 + nrm(k[8], (L, HEAD_DIM), 0.02),
        'k_norm_slc_g': 1.0 + nrm(k[9], (L, HEAD_DIM), 0.02),
        'k_norm_win_g': 1.0 + nrm(k[10], (L, HEAD_DIM), 0.02),
        'pos_cmp_k': nrm(k[11], (L, CMP_LEN, HEAD_DIM), 0.1),
        'pos_cmp_v': nrm(k[12], (L, CMP_LEN, HEAD_DIM), 0.1),
        'w_cmp_k1': nrm(k[13], (L, CMP_LEN * HEAD_DIM, CMP_HIDDEN), (CMP_LEN * HEAD_DIM) ** -0.5),
        'w_cmp_k2': nrm(k[14], (L, CMP_HIDDEN, HEAD_DIM), CMP_HIDDEN ** -0.5),
        'w_cmp_v1': nrm(k[15], (L, CMP_LEN * HEAD_DIM, CMP_HIDDEN), (CMP_LEN * HEAD_DIM) ** -0.5),
        'w_cmp_v2': nrm(k[16], (L, CMP_HIDDEN, HEAD_DIM), CMP_HIDDEN ** -0.5),
        'conv_mix_w': nrm(k[17], (L, CONV_K, CONV_WIDTH), CONV_K ** -0.5),
        'attn_out_g': 1.0 + nrm(k[18], (L, ATTN_WIDTH), 0.02),
        'conv_out_g': 1.0 + nrm(k[19], (L, CONV_WIDTH), 0.02),
        'w_out': nrm(k[20], (L, MIX_WIDTH, D_MODEL), MIX_WIDTH ** -0.5),
        'norm2_g': 1.0 + nrm(k[21], (L, D_MODEL), 0.02),
        'w_ffn_gate': nrm(k[22], (L, D_MODEL, D_FF), D_MODEL ** -0.5),
        'w_ffn_up': nrm(k[23], (L, D_MODEL, D_FF), D_MODEL ** -0.5),
        'conv_ffn_w': nrm(k[24], (L, CONV_K, D_FF), CONV_K ** -0.5),
        'w_ffn_down': nrm(k[25], (L, D_FF, D_MODEL), D_FF ** -0.5),
    }


def reference(x, c, w_ada, b_ada, norm1_g, w_in, b_gate, q_norm_g, k_norm_cmp_g, k_norm_slc_g,
              k_norm_win_g, pos_cmp_k, pos_cmp_v, w_cmp_k1, w_cmp_k2, w_cmp_v1, w_cmp_v2,
              conv_mix_w, attn_out_g, conv_out_g, w_out, norm2_g, w_ffn_gate, w_ffn_up,
              conv_ffn_w, w_ffn_down):
    b_sz, s_len, _ = x.shape
    split_at = list(np.cumsum([ATTN_WIDTH] + [KV_COLS] * 6 + [N_GATE, CONV_WIDTH, CONV_WIDTH]))

    def to_kv(t):
        return t.reshape(b_sz, s_len, N_KV, HEAD_DIM).transpose(0, 2, 1, 3)

    for l in range(DEPTH):
        mod = jax.nn.silu(c) @ w_ada[l] + b_ada[l]
        sh1, sc1, g1, sh2, sc2, g2 = jnp.split(mod[:, None, :], 6, axis=-1)
        h = rms_norm(x, norm1_g[l]) * (1.0 + sc1) + sh1
        proj = h @ w_in[l]
        q, kc_raw, vc_raw, ks, vs, kw, vw, g_logit, gate_b, gate_c, xin = jnp.split(proj, split_at, axis=-1)
        q = rms_norm(q.reshape(b_sz, s_len, N_KV, GQA, HEAD_DIM), q_norm_g[l])
        kc = rms_norm(compress(to_kv(kc_raw), pos_cmp_k[l], w_cmp_k1[l], w_cmp_k2[l]), k_norm_cmp_g[l])
        vc = compress(to_kv(vc_raw), pos_cmp_v[l], w_cmp_v1[l], w_cmp_v2[l])
        ks = rms_norm(to_kv(ks), k_norm_slc_g[l])
        kw = rms_norm(to_kv(kw), k_norm_win_g[l])
        gates = jax.nn.sigmoid(g_logit + b_gate[l]).reshape(b_sz, s_len, N_KV, GQA, 3)
        attn = nsa_mixer(q, kc, vc, ks, to_kv(vs), kw, to_kv(vw), gates)
        conv = gate_b * causal_dwconv(gate_c * xin, conv_mix_w[l])
        mixed = jnp.concatenate([group_rms_norm(attn, attn_out_g[l], N_HEADS),
                                 group_rms_norm(conv, conv_out_g[l], CONV_GROUPS)], axis=-1)
        x = x + g1 * (mixed @ w_out[l])
        h2 = rms_norm(x, norm2_g[l]) * (1.0 + sc2) + sh2
        g_pre = causal_dwconv(h2 @ w_ffn_gate[l], conv_ffn_w[l])
        x = x + g2 * ((jax.nn.silu(g_pre) * (h2 @ w_ffn_up[l])) @ w_ffn_down[l])
    return x
```

```python
import numpy as np
from contextlib import ExitStack
import concourse.bass as bass
import concourse.mybir as mybir
from concourse.bass_utils import run_bass_kernel_spmd

F32 = mybir.dt.float32
BF16 = mybir.dt.bfloat16
ALU = mybir.AluOpType
AF = mybir.ActivationFunctionType
AX = mybir.AxisListType

S = 4096
D = 1024
NIN = 2840
DFF = 2816
NFF = 22
EPS = 1e-6
NEG = -30000.0
TB = 512
NTB = S // TB
TBF = 256
ENGINES = ("pe", "act", "dve", "pool", "sp")
SEM_CAP = 30000
DEBUG = False
import os
N1 = int(os.environ.get('K_N1', NTB))
N2 = int(os.environ.get('K_N2', NTB))
NB = int(os.environ.get('K_NB', S // TBF))
CUT = int(os.environ.get('K_CUT', 99))


class Buf:
    __slots__ = ("name", "w", "r")

    def __init__(self, name):
        self.name = name
        self.w = None
        self.r = {}


class Prog:
    def __init__(self, nc, stack, n_dma_sems=24):
        self.nc = nc
        self.stack = stack
        self.q = {e: [] for e in ENGINES}
        self.sem = {}
        self.cnt = {}
        self.nsem = 0
        for e in ENGINES:
            self._new_sem(e)
        self.waited = {}
        self.dma_sems = {e: [stack.enter_context(nc.semaphore(f"dma_{e}{i}")) for i in range(n_dma_sems)] for e in ("sp", "pool")}
        self.dma_cnt = {e: [0] * n_dma_sems for e in ("sp", "pool")}
        self.dma_rr = {"sp": 0, "pool": 0}
        self.ninstr = {e: 0 for e in ENGINES}

    def _new_sem(self, e):
        self.sem[e] = self.stack.enter_context(self.nc.semaphore(f"s_{e}_{self.nsem}"))
        self.nsem += 1
        self.cnt[e] = 0
        self.own = getattr(self, "own", {})
        self.own.setdefault(e, set()).add(id(self.sem[e]))

    def _wait(self, eng, tok):
        if tok is None:
            return
        sem, val = tok
        key = (eng, id(sem))
        if self.waited.get(key, 0) >= val:
            return
        self.waited[key] = val
        self.q[eng].append(lambda E, sem=sem, val=val: E.wait_ge(sem, val))

    def _deps(self, eng, reads, writes):
        for b in reads:
            self._wait(eng, b.w)
        for b in writes:
            self._wait(eng, b.w)
            for t in b.r.values():
                self._wait(eng, t)

    def _mark(self, tok, reads, writes):
        for b in reads:
            b.r[id(tok[0])] = tok
        for b in writes:
            b.w = tok
            b.r = {}

    def op(self, eng, fn, reads=(), writes=(), signal=True, relaxed=()):
        for b in relaxed:
            if b.w is not None and id(b.w[0]) not in self.own[eng]:
                self._wait(eng, b.w)
            for t in b.r.values():
                self._wait(eng, t)
        writes = list(writes) + list(relaxed) if relaxed else writes
        self._deps(eng, reads, [b for b in writes if b not in relaxed] if relaxed else writes)
        self.ninstr[eng] += 1
        if not signal:
            self.q[eng].append(lambda E, fn=fn: fn(E))
            return None
        if self.cnt[eng] >= SEM_CAP:
            self._new_sem(eng)
        self.cnt[eng] += 1
        sem, val = self.sem[eng], self.cnt[eng]
        self.q[eng].append(lambda E, fn=fn, sem=sem: fn(E).then_inc(sem, 1))
        tok = (sem, val)
        self._mark(tok, reads, writes)
        return tok

    def dma(self, eng, fn, reads=(), writes=(), ndesc=256):
        if eng == "pool":
            self.outst = getattr(self, "outst", [])
            while self.outst and sum(n for _, n in self.outst) + ndesc > 5000:
                tok0, _ = self.outst.pop(0)
                self._wait("pool", tok0)
        i = self.dma_rr[eng]
        self.dma_rr[eng] = (i + 1) % len(self.dma_sems[eng])
        sem = self.dma_sems[eng][i]
        if self.dma_cnt[eng][i] > 0:
            self._wait(eng, (sem, self.dma_cnt[eng][i]))
        self._deps(eng, reads, writes)
        self.dma_cnt[eng][i] += 16
        val = self.dma_cnt[eng][i]
        self.q[eng].append(lambda E, fn=fn, sem=sem: fn(E).then_inc(sem, 16))
        tok = (sem, val)
        self._mark(tok, reads, writes)
        self.ninstr[eng] += 1
        if eng == "pool":
            self.outst.append((tok, ndesc))
        return tok

    def wait(self, eng, tok):
        self._wait(eng, tok)

    def run(self):
        nc = self.nc
        for e in ("sp", "pool"):
            for i, sem in enumerate(self.dma_sems[e]):
                if self.dma_cnt[e][i] > 0:
                    self._wait(e, (sem, self.dma_cnt[e][i]))
        q = self.q
        with nc.Block() as block:
            @block.tensor
            def _(E):
                for f in q["pe"]:
                    f(E)

            @block.scalar
            def _(E):
                for f in q["act"]:
                    f(E)

            @block.vector
            def _(E):
                for f in q["dve"]:
                    f(E)

            @block.gpsimd
            def _(E):
                for f in q["pool"]:
                    f(E)

            @block.sync
            def _(E):
                for f in q["sp"]:
                    f(E)
        self.q = {e: [] for e in ENGINES}


def bc(ap, shape):
    return ap.unsqueeze(len(ap.shape)).broadcast_to(list(ap.shape) + [shape])


def build_program(debug=False):
    nc = bass.Bass("TRN2", target_bir_lowering=False)
    dt = lambda n, s, k="ExternalInput": nc.dram_tensor(n, list(s), F32, kind=k).ap()
    x_d = dt("x", [S, D])
    wada_d = dt("w_ada", [D, 6 * D])
    pv_d = dt("pvecs", [128, 163])
    rv_d = dt("rowvecs", [128, 728])
    win_d = dt("w_in", [D, NIN])
    pos_d = dt("posT", [128, 64])
    w1k_d = dt("w_cmp_k1", [2048, 256])
    w1v_d = dt("w_cmp_v1", [2048, 256])
    w2k_d = dt("w_cmp_k2", [256, 64])
    w2v_d = dt("w_cmp_v2", [256, 64])
    wout_d = dt("w_out", [D, D])
    wg_d = dt("w_ffn_gate", [D, DFF])
    wu_d = dt("w_ffn_up", [D, DFF])
    wd_d = dt("w_ffn_down", [DFF, D])
    qaug_d = dt("qaug", [4, 8, S])
    kaug_d = dt("kaug", [4, S])
    caug_d = dt("caug", [4, 256])
    ov_d = dt("ovT", [128, 2, 64])
    out_d = dt("out", [S, D], "ExternalOutput")
    dbg = {}
    if debug:
        for n, s in [("d_mod", [128, 48]), ("d_hT", [128, 8 * TB]), ("d_kcT", [68, 512]), ("d_vcx", [128, 4 * 129]),
                     ("d_qT", [68, 8 * TB]), ("d_gates", [128, 96]), ("d_attnb", [128, 4 * 512]), ("d_mixT", [128, 8 * TB]),
                     ("d_ksT", [68, 2 * TB]), ("d_imp", [128, 256]), ("d_selb", [128, 256]), ("d_acc", [128, 1024])]:
            dbg[n] = nc.dram_tensor(n, s, F32, kind="ExternalOutput").ap()

    x_t = x_d.rearrange("(t p) d -> t p d", p=128)
    out_t = out_d.rearrange("(t p) d -> t p d", p=128)
    obuf = [Buf(f"out{i}") for i in range(32)]
    final_toks = []

    with ExitStack() as st0:
        P = Prog(nc, st0)
        sb0 = lambda n, s, d=F32: st0.enter_context(nc.sbuf_tensor(n, list(s), d))
        banks = [st0.enter_context(nc.psum_tensor(f"bank{i}", [128, 512], F32)) for i in range(8)]
        bb = [Buf(f"bank{i}") for i in range(8)]

        def O(eng, method, reads=(), writes=(), signal=True, relaxed=(), **kw):
            return P.op(eng, lambda E: getattr(E, method)(**kw), reads, writes, signal, relaxed)

        def mmg(mms, reads, writes):
            n = len(mms)
            tok = None
            for i, m in enumerate(mms):
                kw = dict(out=m["out"], lhsT=m["lhsT"], rhs=m["rhs"], start=m["start"], stop=m["stop"])
                if m.get("skip"):
                    kw["skip_group_check"] = True
                tok = O("pe", "matmul", reads, writes, signal=(i == n - 1), **kw)
            return tok

        def dbg_out(name, tile_ap, b, eng="pool", dst=None):
            if debug and name in dbg:
                d_ = dbg[name] if dst is None else dst
                t = P.dma(eng, lambda E: E.dma_start(out=d_, in_=tile_ap), reads=[b])
                final_toks.append(t)

        ident_bf = sb0("ident_bf", [128, 128], BF16); ident_f = sb0("ident_f", [128, 128])
        ones_bf = sb0("ones_bf", [128, 128], BF16); ones_f = sb0("ones_f", [128, 128])
        G64 = sb0("G64", [128, 128], BF16)
        pv = sb0("pv", [128, 163]); rv = sb0("rv", [128, 728])
        mod = sb0("mod", [128, 48]); ab = sb0("ab", [128, 16]); epsT = sb0("epsT", [128, 1])
        qg8 = sb0("qg8", [128, 64])
        stW = ExitStack()
        sbw = lambda n, s, d=F32: stW.enter_context(nc.sbuf_tensor("w_" + n, list(s), d))
        kcT = sbw("kcT", [68, 2, 256], BF16); vcx = sbw("vcx", [128, 2, 2, 129], BF16)
        Lc = sbw("Lc", [128, 2560], BF16); Ldiag = sbw("Ldiag", [128, 128], BF16); Lfar = sbw("Lfar", [128, 128], BF16)
        Eall = sbw("Eall", [128, S], BF16)
        wA = sbw("wA", [128, 8, 512], BF16)
        wB = sbw("wB", [128, 8, NIN - 768], BF16)
        bwin = Buf("win")
        wout = sbw("wout", [128, 8, D], BF16); bwout = Buf("wout")
        bconst = Buf("const"); bpv = Buf("pv"); brv = Buf("rv"); bmod = Buf("mod"); bkcT = Buf("kcT"); bvcx = Buf("vcx")

        PV_BADA, PV_N1G, PV_N2G, PV_CMW, PV_COG, PV_KCG, PV_CT, PV_CFW = 0, 48, 56, 64, 76, 80, 81, 97
        RV_BG, RV_QG, RV_KSG, RV_KWG, RV_AOG = 0, 24, 88, 152, 216

        P.dma("sp", lambda E: E.dma_start(out=pv[:], in_=pv_d), writes=[bpv])
        P.dma("sp", lambda E: E.dma_start(out=rv[:], in_=rv_d), writes=[brv])
        O("pool", "memset", writes=[bconst], ap=ident_bf[:], constant=1.0)
        O("pool", "affine_select", reads=[bconst], writes=[bconst], out=ident_bf[:], in_=ident_bf[:], pattern=[[1, 128]],
          compare_op=ALU.is_equal, fill=0.0, base=0, channel_multiplier=-1)
        O("pool", "tensor_copy", reads=[bconst], writes=[bconst], out=ident_f[:], in_=ident_bf[:])
        O("pool", "memset", writes=[bconst], ap=ones_bf[:], constant=1.0)
        O("pool", "memset", writes=[bconst], ap=ones_f[:], constant=1.0)
        O("pool", "memset", writes=[bconst], ap=epsT[:], constant=EPS)
        O("pool", "memset", writes=[bconst], ap=G64[:], constant=0.0)
        O("pool", "memset", writes=[bconst], ap=G64[0:64, 0:64], constant=1.0)
        O("pool", "memset", writes=[bconst], ap=G64[64:128, 64:128], constant=1.0)
        O("pool", "memset", writes=[bconst], ap=Lc[:], constant=0.0)
        O("pool", "affine_select", reads=[bconst], writes=[bconst], out=Lc[:], in_=Lc[:], pattern=[[1, 2560]],
          compare_op=ALU.is_ge, fill=NEG, base=-15, channel_multiplier=-16)
        O("pool", "memset", writes=[bconst], ap=Ldiag[:], constant=0.0)
        O("pool", "affine_select", reads=[bconst], writes=[bconst], out=Ldiag[:], in_=Ldiag[:], pattern=[[1, 128]],
          compare_op=ALU.is_ge, fill=NEG, base=0, channel_multiplier=-1)
        O("pool", "memset", writes=[bconst], ap=Lfar[:], constant=0.0)
        O("pool", "affine_select", reads=[bconst], writes=[bconst], out=Lfar[:], in_=Lfar[:], pattern=[[-1, 128]],
          compare_op=ALU.is_ge, fill=NEG, base=-1, channel_multiplier=1)
        O("pool", "memset", writes=[bconst], ap=Eall[:], constant=1.0)
        O("pool", "affine_select", reads=[bconst], writes=[bconst], out=Eall[:], in_=Eall[:], pattern=[[1, S]],
          compare_op=ALU.is_ge, fill=0.0, base=0, channel_multiplier=-64)
        O("pool", "affine_select", reads=[bconst], writes=[bconst], out=Eall[:], in_=Eall[:], pattern=[[-1, S]],
          compare_op=ALU.is_ge, fill=0.0, base=63, channel_multiplier=64)
        O("pool", "memset", writes=[bkcT], ap=kcT[0:64, :, :], constant=0.0)
        P.dma("pool", lambda E: E.dma_start(out=kcT[64:68, 0, :], in_=caug_d), writes=[bkcT])
        P.dma("pool", lambda E: E.dma_start(out=kcT[64:68, 1, :], in_=caug_d), writes=[bkcT])
        O("pool", "memset", writes=[bvcx], ap=vcx[:], constant=0.0)
        O("pool", "memset", writes=[bvcx], ap=vcx[:, :, :, 64:65], constant=1.0)
        for g in range(2):
            P.dma("pool", lambda E, g=g: E.dma_start(out=vcx[:, :, g, 65:129], in_=ov_d), writes=[bvcx])
        O("dve", "tensor_scalar", reads=[brv], writes=[bconst], out=qg8[:], in0=rv[:, RV_QG:RV_QG + 64], scalar1=0.125,
          scalar2=None, op0=ALU.mult)

        def make_row(st, name, col0):
            row = st.enter_context(nc.sbuf_tensor(name, [128, D], F32))
            brow = Buf(name)
            Dt = [st.enter_context(nc.sbuf_tensor(f"{name}_D{i}", [128, 128], F32)) for i in range(2)]
            bD = [Buf("D0"), Buf("D1")]
            for c in range(8):
                i = c % 2
                bk = 5 + (c // 4)
                O("dve", "tensor_scalar", reads=[bmod, bconst], writes=[bD[i]], out=Dt[i][:], in0=ident_f[:],
                  scalar1=mod[:, col0 + c:col0 + c + 1], scalar2=None, op0=ALU.mult)
                mmg([dict(out=banks[bk][:, (c % 4) * 128:(c % 4 + 1) * 128], lhsT=ones_f[:], rhs=Dt[i][:], start=True, stop=True)],
                    [bD[i], bconst], [bb[bk]])
                if c % 4 == 3:
                    O("act", "copy", writes=[bb[bk], brow], out=row[:, (c // 4) * 512:(c // 4 + 1) * 512], in_=banks[bk][:])
            return row, brow

        def make_hT(*a, **k):
            for _ in make_hT_g(*a, **k):
                pass

        def make_hT_g(env, src_t, tile0, nsub, acol, bcol, hT, bhT, keep=None, bkeep=None, src_bufs=None, tbank=7):
            for s in range(nsub):
                i = env["rr"] % 2
                env["rr"] += 1
                if keep is None:
                    xs, bxs = env["xs"][i], env["bxs"][i]
                    xs_ap = xs[:]
                else:
                    xs_ap, bxs = keep[:, s, :], bkeep[s]
                rd = [src_bufs[tile0 + s]] if src_bufs is not None else []
                P.dma("sp", lambda E, xs_ap=xs_ap, s=s: E.dma_start(out=xs_ap, in_=src_t[tile0 + s]), reads=rd, writes=[bxs])
                yield
                ss = env["ss"]; bss = env["bss"]
                jt, bjt = (env["junk"], env["bjunk"]) if env["junk"] is not None else (env["xn"][i], env["bxn"][i])
                O("act", "activation", reads=[bxs], writes=[bjt, bss], out=jt[:], in_=xs_ap, func=AF.Square,
                  accum_out=ss[:, 0:1])
                yield
                O("act", "activation", reads=[bss, bconst], writes=[bss], out=ss[:, 1:2], in_=ss[:, 0:1], func=AF.Ln, scale=1.0 / D, bias=epsT[:, 0:1])
                O("act", "activation", reads=[bss], writes=[bss], out=ss[:, 3:4], in_=ss[:, 1:2], func=AF.Exp, scale=-0.5)
                yield
                xn, bxn = env["xn"][i], env["bxn"][i]
                O("dve", "tensor_scalar", reads=[bxs, bss], writes=[bxn], out=xn[:], in0=xs_ap, scalar1=ss[:, 3:4], scalar2=None,
                  op0=ALU.mult)
                yield
                tp = banks[tbank][:].bitcast(BF16).rearrange("p (c t) -> p c t", t=128)
                for fc in range(8):
                    O("pe", "transpose", reads=[bxn, bconst], writes=[bb[tbank]], signal=(fc == 7), out=tp[:, fc, :],
                      in_=xn[:, fc * 128:(fc + 1) * 128], identity=ident_bf[:])
                yield
                for fc in range(8):
                    O("dve", "tensor_scalar", reads=[bmod], relaxed=[bb[tbank], bhT], out=hT[:, fc, s * 128:(s + 1) * 128], in0=tp[:, fc, :],
                      scalar1=acol[:, fc:fc + 1], scalar2=bcol[:, fc:fc + 1], op0=ALU.mult, op1=ALU.add)

        def rstd_small(ssq, n, width, bufs):
            O("act", "activation", reads=list(bufs) + [bconst], writes=bufs, out=ssq, in_=ssq, func=AF.Ln, scale=1.0 / n, bias=epsT[:, 0:1])
            O("act", "activation", reads=bufs, writes=bufs, out=ssq, in_=ssq, func=AF.Exp, scale=-0.5)

        win_v = win_d.rearrange("(kc p) n -> p kc n", p=128)
        wout_v = wout_d.rearrange("(kc p) n -> p kc n", p=128)

        with ExitStack() as st:
            sb = lambda n, s, d=F32: st.enter_context(nc.sbuf_tensor("ph1_" + n, list(s), d))
            winc = sb("winc", [128, 8, 256], BF16); bwinc = Buf("winc")
            w1 = [sb("w1k", [128, 32, 256], BF16), sb("w1v", [128, 32, 256], BF16)]; bw1 = Buf("w1")
            w2 = [sb("w2k", [128, 2, 64], BF16), sb("w2v", [128, 2, 64], BF16)]; bw2 = Buf("w2")
            posb = sb("posb", [128, 64], BF16); bpos = Buf("pos")
            cbias = sb("cbias", [128, 4]); bcb = Buf("cb")
            X = [sb("xk", [128, 16, 257], BF16), sb("xv", [128, 16, 257], BF16)]; bX = [Buf("xk"), Buf("xv")]
            vcT = sb("vcT", [64, 2, 256]); bvcT = Buf("vcT")
            hT = sb("hT1", [128, 8, TB], BF16); bhT = Buf("hT1")
            env = dict(rr=0, xs=[sb("xs0", [128, D]), sb("xs1", [128, D])], bxs=[Buf("xs0"), Buf("xs1")],
                       xn=[sb("xn0", [128, D], BF16), sb("xn1", [128, D], BF16)], bxn=[Buf("xn0"), Buf("xn1")],
                       junk=sb("junk", [128, D], BF16), bjunk=Buf("junk"), ss=sb("ss", [128, 4]), bss=Buf("ss"))
            xsg = [sb("xsg0", [128, 256]), sb("xsg1", [128, 256])]; x2g = [sb("x2g0", [128, 256]), sb("x2g1", [128, 256])]
            bgl = [Buf("gl0"), Buf("gl1")]
            gl = sb("gl", [128, 2, 2, 2, 256], BF16); bglo = Buf("glo")
            sqk = sb("sqk", [64, 512]); rsk = sb("rsk", [64, 512]); bsqk = Buf("sqk")
            for kv in range(2):
                O("pool", "memset", writes=[bX[kv]], ap=X[kv][:, :, 0:1], constant=0.0)
            P.dma("pool", lambda E: E.dma_start(out=winc[:], in_=win_v[:, :, 512:768]), writes=[bwinc], ndesc=1024)
            wa = [sb(f"wa{i}", [128, 8, 512]) for i in range(2)]
            bwa = [Buf("wa0"), Buf("wa1")]
            sc = sb("siluc", [128, 16]); bsc = Buf("sc")
            O("act", "activation", reads=[bpv], writes=[bsc], out=sc[:], in_=pv[:, PV_CT:PV_CT + 16], func=AF.Silu)
            wada_v = wada_d.rearrange("(kc p) n -> p kc n", p=128)
            mps = banks[0][:, 0:96].rearrange("p (j t) -> p j t", t=2)
            scv = sc[:].rearrange("p (k t) -> p k t", t=2)
            modtok = {}

            def mod_dma(pc):
                i = pc % 2
                modtok[pc] = P.dma("sp", lambda E: E.dma_start(out=wa[i][:], in_=wada_v[:, :, pc * 512:(pc + 1) * 512]), writes=[bwa[i]])

            def mod_mm(pc):
                i = pc % 2
                for cc in range(4):
                    j = pc * 4 + cc
                    mmg([dict(out=mps[:, j, :], lhsT=wa[i][:, kc, cc * 128:(cc + 1) * 128], rhs=scv[:, kc, :],
                              start=(kc == 0), stop=(kc == 7)) for kc in range(8)], [bwa[i], bsc], [bb[0]])
            mod_dma(0); mod_dma(1); mod_mm(0); mod_dma(2); mod_mm(1); mod_dma(3); mod_mm(2); mod_mm(3)
            O("dve", "tensor_tensor", reads=[bpv], writes=[bb[0], bmod], out=mod[:, 0:16], in0=mps[:, 0:16, 0], in1=pv[:, PV_BADA:PV_BADA + 16], op=ALU.add)
            O("dve", "scalar_tensor_tensor", reads=[bmod, bpv], writes=[bmod], out=ab[:, 0:8], in0=mod[:, 8:16], scalar=1.0,
              in1=pv[:, PV_N1G:PV_N1G + 8], op0=ALU.add, op1=ALU.mult)
            P.wait("pool", modtok[3])
            for kv, wd1 in enumerate((w1k_d, w1v_d)):
                src = wd1.rearrange("(l d) n -> d l n", d=64)
                for hf in range(2):
                    P.dma("pool", lambda E, kv=kv, hf=hf, src=src: E.dma_start(out=w1[kv][hf * 64:(hf + 1) * 64, :, :], in_=src), writes=[bw1], ndesc=2048)
            for kv, wd2 in enumerate((w2k_d, w2v_d)):
                P.dma("pool", lambda E, kv=kv, wd2=wd2: E.dma_start(out=w2[kv][:], in_=wd2.rearrange("(h p) n -> p h n", p=128)), writes=[bw2])
            P.dma("pool", lambda E: E.dma_start(out=posb[:], in_=pos_d), writes=[bpos])
            P.dma("pool", lambda E: E.dma_start(out=wA[:], in_=win_v[:, :, 0:512]), writes=[bwin], ndesc=1024)
            for c0 in range(768, NIN, 1036):
                P.dma("pool", lambda E, c0=c0: E.dma_start(out=wB[:, :, c0 - 768:c0 - 768 + 1036], in_=win_v[:, :, c0:c0 + 1036]), writes=[bwin], ndesc=1024)
            P.dma("pool", lambda E: E.dma_start(out=wout[:], in_=wout_v), writes=[bwout], ndesc=1024)
            for kv in range(2):
                for hf in range(2):
                    mmg([dict(out=banks[1][:, kv * 2 + hf:kv * 2 + hf + 1], lhsT=w1[kv][0:64, l, hf * 128:(hf + 1) * 128],
                              rhs=posb[0:64, kv * 32 + l:kv * 32 + l + 1], start=(l == 0), stop=(l == 31)) for l in range(32)],
                        [bw1, bpos], [bb[1]])
            O("dve", "tensor_copy", writes=[bb[1], bcb], out=cbias[:], in_=banks[1][:, 0:4])


            for tb in range(NTB):
                mod_dma(4 + tb)
                make_hT(env, x_t, tb * 4, 4, ab[:, 0:8], mod[:, 0:8], hT, bhT)
                for kv in range(2):
                    bk = 5 + kv
                    mmg([dict(out=banks[bk][:], lhsT=winc[:, kc, kv * 128:(kv + 1) * 128], rhs=hT[:, kc, :], start=(kc == 0), stop=(kc == 7))
                         for kc in range(8)], [bwinc, bhT], [bb[bk]])
                    O("act", "copy", writes=[bb[bk], bX[kv]], out=X[kv][:, :, 1 + tb * 32:1 + (tb + 1) * 32].rearrange("p r m -> p m r"),
                      in_=banks[bk][:].rearrange("p (m r) -> p m r", r=16))
                if tb >= 1:
                    mod_mm(4 + tb - 1)
            mod_mm(11)
            O("dve", "tensor_tensor", reads=[bpv], writes=[bb[0], bmod], out=mod[:, 16:48], in0=mps[:, 16:48, 0], in1=pv[:, PV_BADA + 16:PV_BADA + 48], op=ALU.add)
            O("dve", "scalar_tensor_tensor", reads=[bmod, bpv], writes=[bmod], out=ab[:, 8:16], in0=mod[:, 32:40], scalar=1.0,
              in1=pv[:, PV_N2G:PV_N2G + 8], op0=ALU.add, op1=ALU.mult)
            dbg_out("d_mod", mod[:], bmod, eng="sp")
            hpg = [[banks[2 * g + kv][:].rearrange("p (b n) -> p b n", b=2) for kv in range(2)] for g in range(2)]
            for kv in range(2):
                for g in range(2):
                    mm = []
                    for hf in range(2):
                        for l in range(32):
                            mm.append(dict(out=hpg[g][kv][:, hf, :], lhsT=w1[kv][g * 64:(g + 1) * 64, l, hf * 128:(hf + 1) * 128],
                                           rhs=X[kv][g * 64:(g + 1) * 64, l % 16, l // 16:l // 16 + 256], start=(l == 0), stop=(l == 31)))
                    mmg(mm, [bw1, bX[kv]], [bb[2 * g + kv]])
            rr_ = 0
            for kv in range(2):
                for g in range(2):
                    for hf in range(2):
                        ci = kv * 2 + hf
                        i = rr_ % 2
                        rr_ += 1
                        bk = 2 * g + kv
                        O("act", "activation", reads=[bcb], writes=[bb[bk], bgl[i]], out=xsg[i][:], in_=hpg[g][kv][:, hf, :],
                          func=AF.Identity, bias=cbias[:, ci:ci + 1], scale=1.0)
                        O("dve", "tensor_tensor", writes=[bgl[i]], out=x2g[i][:], in0=xsg[i][:], in1=xsg[i][:], op=ALU.mult)
                        O("dve", "tensor_scalar", writes=[bgl[i]], out=x2g[i][:], in0=x2g[i][:], scalar1=0.044715, scalar2=1.0, op0=ALU.mult, op1=ALU.add)
                        O("dve", "tensor_tensor", writes=[bgl[i]], out=x2g[i][:], in0=x2g[i][:], in1=xsg[i][:], op=ALU.mult)
                        O("act", "activation", writes=[bgl[i]], out=x2g[i][:], in_=x2g[i][:], func=AF.Sigmoid, scale=1.5957691216057308)
                        O("dve", "tensor_tensor", reads=[bgl[i]], writes=[bglo], out=gl[:, kv, hf, g, :], in0=xsg[i][:], in1=x2g[i][:], op=ALU.mult)
            ops_ = [banks[4 + kv][0:64, :].rearrange("p (g n) -> p g n", g=2) for kv in range(2)]
            for kv in range(2):
                mm = []
                for g in range(2):
                    for hf in range(2):
                        mm.append(dict(out=ops_[kv][:, g, :], lhsT=w2[kv][:, hf, :], rhs=gl[:, kv, hf, g, :], start=(hf == 0), stop=(hf == 1)))
                mmg(mm, [bw2, bglo], [bb[4 + kv]])
            O("act", "activation", writes=[bb[4], bsqk], out=sqk[:], in_=banks[4][0:64, :], func=AF.Square)
            mmg([dict(out=banks[6][0:64, :], lhsT=ones_f[0:64, 0:64], rhs=sqk[:], start=True, stop=True)], [bsqk, bconst], [bb[6]])
            O("act", "activation", reads=[bconst], writes=[bb[6], bsqk], out=rsk[:], in_=banks[6][0:64, :], func=AF.Ln, scale=1.0 / 64, bias=epsT[0:64, 0:1])
            O("act", "activation", writes=[bsqk], out=rsk[:], in_=rsk[:], func=AF.Exp, scale=-0.5)
            O("dve", "scalar_tensor_tensor", reads=[bsqk, bpv], writes=[bb[4], bkcT], out=kcT[0:64, :, :],
              in0=ops_[0], scalar=pv[0:64, PV_KCG:PV_KCG + 1], in1=rsk[:].rearrange("p (g n) -> p g n", g=2),
              op0=ALU.mult, op1=ALU.mult)
            O("act", "copy", writes=[bb[5], bvcT], out=vcT[:], in_=ops_[1])
            for tl in range(2):
                for g in range(2):
                    O("pe", "transpose", reads=[bvcT, bconst], writes=[bb[7]], signal=(tl == 1 and g == 1), out=banks[7][:, (tl * 2 + g) * 64:(tl * 2 + g + 1) * 64],
                      in_=vcT[:, g, tl * 128:(tl + 1) * 128], identity=ident_f[0:64, 0:64])
            O("dve", "tensor_copy", writes=[bb[7], bvcx], out=vcx[:, :, :, 0:64],
              in_=banks[7][:, 0:256].rearrange("p (t g d) -> p t g d", t=2, g=2))
            dbg_out("d_kcT", kcT[:].rearrange("p g n -> p (g n)"), bkcT)
            dbg_out("d_vcx", vcx[:].rearrange("p t g n -> p (t g n)"), bvcx)
            P.run()

        with ExitStack() as st:
            sb = lambda n, s, d=F32: st.enter_context(nc.sbuf_tensor("ph2_" + n, list(s), d))
            g1row, bg1 = make_row(st, "g1row", 16)
            ksT = sb("ksT", [68, 2, S], BF16); bks = [Buf(f"ksT{i}") for i in range(NTB)]; bksaug = Buf("ksaug")
            KWR = 1536
            kwT = sb("kwT", [68, 2, KWR], BF16); bkw = [Buf(f"kwT{i}") for i in range(3)]
            vsa = sb("vsa", [128, 32, 2, 65], BF16); bvs = [Buf(f"vsa{i}") for i in range(NTB)]; bvs1 = Buf("vs1")
            vwa = sb("vwa", [128, 12, 2, 65], BF16); bvw = [Buf(f"vwa{i}") for i in range(3)]; bvw1 = Buf("vw1")
            qTs = [sb(f"qT{i}", [68, 8, TB], BF16) for i in range(2)]; bqTs = [Buf("qT0"), Buf("qT1")]
            hT = sb("hT2", [128, 8, TB], BF16); bhT = Buf("hT2")
            mixa = sb("mixa", [128, 4, TB], BF16); bmixa = Buf("mixa")
            mixcs = [sb(f"mixc{i}", [128, 4, TB], BF16) for i in range(2)]; bmixcs = [Buf("mixc0"), Buf("mixc1")]
            xs0_ = sb("xs0", [128, D]); bxs0_ = Buf("xs0")
            env = dict(rr=0, xs=[xs0_, xs0_], bxs=[bxs0_, bxs0_],
                       xn=[sb("xn0", [128, D], BF16), sb("xn1", [128, D], BF16)], bxn=[Buf("xn0"), Buf("xn1")],
                       junk=None, bjunk=None, ss=sb("ss", [128, 4]), bss=Buf("ss"))
            sqt = sb("sqt", [128, 512]); bsqt = Buf("sqt")
            ssq = sb("ssq", [128, 16]); bssq = Buf("ssq")
            ssq2 = sb("ssq2", [128, 16]); bssq2 = Buf("ssq2")
            sqk2 = sb("sqk2", [128, 256]); bsqk2 = Buf("sqk2")
            ssqk = sb("ssqk", [128, 4]); bssqk = Buf("ssqk")
            qnb = sb("qnb", [128, 512], BF16); bqnb = Buf("qnb")
            knb = sb("knb", [128, 256], BF16); bknb = Buf("knb")
            gatess = [sb(f"gates{i}", [128, 4, 24]) for i in range(2)]; bgatess = [Buf("gates0"), Buf("gates1")]
            xin_sb = sb("xin_sb", [128, TB]); u = sb("u", [128, TB + 2]); t1 = sb("t1", [128, TB]); cv = sb("cv", [128, TB])
            sqb = sb("sqb", [128, TB], BF16); uhalo = sb("uhalo", [128, 4, 2])
            bxin, bu, bt1, bcv, bsqb, buh = [Buf(n) for n in ("xin", "u", "t1", "cv", "sqb", "uh")]
            NPT = 3
            PT = [sb(f"PT{i}", [128, TB], BF16) for i in range(NPT)]; bPT = [Buf(f"PT{i}") for i in range(NPT)]
            scr = sb("scr", [128, 1024]); bscr = Buf("scr")
            Uraw = scr[:].rearrange("p (s h j) -> p s h j", s=4, h=4); bUraw = bscr
            acc = sb("acc", [128, 4, 4, 64]); bacc = Buf("acc")
            rw = sb("rw", [128, 16]); brw = Buf("rw")
            imp = sb("imp", [128, 4, 64]); bimp = Buf("imp")
            wk = sb("wk", [128, 64]); m8 = sb("m8", [128, 16]); btk = Buf("tk")
            selb = sb("selb", [128, 4, 64], BF16); bselb = Buf("selb")
            selbT = [sb(f"selbT{i}", [128, TB], BF16) for i in range(2)]; bselbT = [Buf("selbT0"), Buf("selbT1")]
            for i_ in range(2):
                O("pool", "memset", writes=[bselbT[i_]], ap=selbT[i_][:], constant=0.0)
            attnb = sb("attnb", [128, 4, 512], BF16); battnb = Buf("attnb")
            xr = [sb("xr0", [128, D]), sb("xr1", [128, D])]; bxr = [Buf("xr0"), Buf("xr1")]
            OTs = [None, sb("OTs1", [65, TB]), sb("OTs2", [65, TB])]; bOTs = [None, Buf("OTs1"), Buf("OTs2")]
            tmpo = [sqt, xin_sb]; btmpo = [bsqt, bxin]

            for g in range(2):
                P.dma("pool", lambda E, g=g: E.dma_start(out=ksT[64:68, g, :], in_=kaug_d), writes=[bksaug])
            O("pool", "memset", writes=[bvs1], ap=vsa[:, :, :, 64:65], constant=1.0)
            O("pool", "memset", writes=[bvw1], ap=vwa[:, :, :, 64:65], constant=1.0)
            O("pool", "memset", writes=[buh], ap=uhalo[:], constant=0.0)
            CB_KV, CB_GL, CB_GB, CB_GC, CB_XI = 0, 1280 - 768, 1304 - 768, 1816 - 768, 2328 - 768
            rot = dict(s=0, pt=0)
            pend = dict(gen=None)

            def tick(n=1):
                for _ in range(n):
                    if pend["gen"] is not None:
                        try:
                            next(pend["gen"])
                        except StopIteration:
                            pend["gen"] = None

            def drain():
                while pend["gen"] is not None:
                    tick()

            def frontend(tb):
                t0 = tb * TB
                qT, bqT = qTs[tb % 2], bqTs[tb % 2]
                gates, bgates = gatess[tb % 2], bgatess[tb % 2]
                mixc, bmixc = mixcs[tb % 2], bmixcs[tb % 2]
                kwi = tb % 3
                P.dma("pool", lambda E: E.dma_start(out=qT[64:68, :, :], in_=qaug_d[:, :, t0:t0 + TB]), writes=[bqT])
                r0 = kwi * TB
                for g in range(2):
                    P.dma("pool", lambda E, g=g: E.dma_start(out=kwT[64:68, g, r0:r0 + TB], in_=kaug_d[:, t0:t0 + TB]), writes=[bkw[kwi]])
                def q_g(s):
                    ti = tb * 4 + s
                    hs = lambda kc: hT[:, kc, s * 128:(s + 1) * 128]
                    mmg([dict(out=banks[5][:], lhsT=hs(kc), rhs=wA[:, kc, :], start=(kc == 0), stop=(kc == 7)) for kc in range(8)],
                        [bhT, bwin], [bb[5]])
                    yield
                    O("act", "activation", writes=[bb[5], bsqt], out=sqt[:], in_=banks[5][:], func=AF.Square)
                    yield
                    O("dve", "tensor_reduce", reads=[bsqt], writes=[bssq], out=ssq[:, 0:8], in_=sqt[:].rearrange("p (h d) -> p h d", d=64),
                      axis=AX.X, op=ALU.add)
                    yield
                    rstd_small(ssq[:, 0:8], 64, 8, [bssq])
                    yield
                    O("dve", "tensor_tensor", reads=[bssq], writes=[bb[5], bsqt], out=sqt[:].rearrange("p (h d) -> p h d", d=64),
                      in0=banks[5][:].rearrange("p (h d) -> p h d", d=64), in1=bc(ssq[:, 0:8], 64), op=ALU.mult)
                    O("dve", "tensor_tensor", reads=[bsqt, bconst], writes=[bqnb], out=qnb[:].rearrange("p (h d) -> p h d", d=64),
                      in0=sqt[:].rearrange("p (h d) -> p h d", d=64),
                      in1=qg8[:].unsqueeze(1).broadcast_to([128, 8, 64]), op=ALU.mult)
                    yield
                    tp = banks[5][0:64, :].bitcast(BF16).rearrange("p (h t) -> p h t", t=128)
                    for h in range(8):
                        O("pe", "transpose", reads=[bqnb, bconst], writes=[bb[5]], signal=(h == 7), out=tp[:, h, :],
                          in_=qnb[:, h * 64:(h + 1) * 64], identity=ident_bf[:])
                    yield
                    O("dve", "tensor_copy", writes=[bb[5], bqT], out=qT[0:64, :, s * 128:(s + 1) * 128], in_=tp)
                    yield

                def kv_g(s):
                    ti = tb * 4 + s
                    hs = lambda kc: hT[:, kc, s * 128:(s + 1) * 128]
                    mmg([dict(out=banks[6][:], lhsT=hs(kc), rhs=wB[:, kc, CB_KV:CB_KV + 512], start=(kc == 0), stop=(kc == 7)) for kc in range(8)],
                        [bhT, bwin], [bb[6]])
                    yield
                    kview = banks[6][:].rearrange("p (a b c) -> p a b c", a=2, b=2)[:, :, 0, :]
                    O("act", "activation", writes=[bb[6], bsqk2], out=sqk2[:].rearrange("p (a c) -> p a c", a=2), in_=kview, func=AF.Square)
                    O("dve", "tensor_copy", reads=[bvs1], writes=[bb[6]], relaxed=[bvs[tb]], out=vsa[:, ti, :, 0:64],
                      in_=banks[6][:, 128:256].rearrange("p (g d) -> p g d", g=2))
                    O("dve", "tensor_copy", reads=[bvw1], writes=[bb[6]], relaxed=[bvw[kwi]], out=vwa[:, kwi * 4 + s, :, 0:64],
                      in_=banks[6][:, 384:512].rearrange("p (g d) -> p g d", g=2))
                    yield
                    O("dve", "tensor_reduce", reads=[bsqk2], writes=[bssqk], out=ssqk[:, 0:4], in_=sqk2[:].rearrange("p (h d) -> p h d", d=64),
                      axis=AX.X, op=ALU.add)
                    yield
                    rstd_small(ssqk[:, 0:4], 64, 4, [bssqk])
                    yield
                    O("dve", "tensor_tensor", reads=[bssqk], writes=[bb[6], bsqk2], out=sqk2[:].rearrange("p (a g d) -> p a g d", a=2, g=2),
                      in0=kview.rearrange("p a (g d) -> p a g d", g=2),
                      in1=bc(ssqk[:, 0:4].rearrange("p (a g) -> p a g", a=2), 64), op=ALU.mult)
                    O("dve", "tensor_tensor", reads=[bsqk2, brv], writes=[bknb], out=knb[:].rearrange("p (a g d) -> p a g d", a=2, g=2),
                      in0=sqk2[:].rearrange("p (a g d) -> p a g d", a=2, g=2),
                      in1=rv[:, RV_KSG:RV_KSG + 128].rearrange("p (a d) -> p a d", a=2).unsqueeze(2).broadcast_to([128, 2, 2, 64]), op=ALU.mult)
                    yield
                    tp2 = banks[6][0:64, :].bitcast(BF16).rearrange("p (h t) -> p h t", t=128)
                    for j in range(4):
                        O("pe", "transpose", reads=[bknb, bconst], writes=[bb[6]], signal=(j == 3), out=tp2[:, j, :],
                          in_=knb[:, j * 64:(j + 1) * 64], identity=ident_bf[:])
                    yield
                    O("dve", "tensor_copy", writes=[bb[6]], relaxed=[bks[tb]], out=ksT[0:64, :, ti * 128:(ti + 1) * 128], in_=tp2[:, 0:2, :])
                    rr0 = kwi * TB + s * 128
                    O("dve", "tensor_copy", writes=[bb[6]], relaxed=[bkw[kwi]], out=kwT[0:64, :, rr0:rr0 + 128], in_=tp2[:, 2:4, :])
                    yield
                    mmg([dict(out=banks[6][:, 0:24], lhsT=hs(kc), rhs=wB[:, kc, CB_GL:CB_GL + 24], start=(kc == 0), stop=(kc == 7)) for kc in range(8)],
                        [bhT, bwin], [bb[6]])
                    yield
                    O("dve", "tensor_tensor", reads=[brv], writes=[bb[6], bgates], out=gates[:, s, :], in0=banks[6][:, 0:24], in1=rv[:, RV_BG:RV_BG + 24], op=ALU.add)
                    yield
                    O("act", "activation", writes=[bgates], out=gates[:, s, :], in_=gates[:, s, :], func=AF.Exp, scale=-1.0)
                    yield
                    O("dve", "tensor_scalar", writes=[bgates], out=gates[:, s, :], in0=gates[:, s, :], scalar1=1.0, scalar2=None, op0=ALU.add)
                    O("dve", "reciprocal", writes=[bgates], out=gates[:, s, :], in_=gates[:, s, :])
                    yield

                def rr(gens):
                    gens = list(gens)
                    while gens:
                        for g_ in list(gens):
                            try:
                                next(g_)
                            except StopIteration:
                                gens.remove(g_)
                        yield

                mk = lambda s: make_hT_g(env, x_t, tb * 4 + s, 1, ab[:, 0:8], mod[:, 0:8], hT[:, :, s * 128:(s + 1) * 128], bhT)
                yield from mk(0)
                for s in range(4):
                    yield from rr([q_g(s), kv_g(s)] + ([mk(s + 1)] if s < 3 else []))
                for c in range(4):
                    for (bk, cb) in ((5, CB_XI), (6, CB_GC), (7, CB_GB)):
                        mmg([dict(out=banks[bk][:], lhsT=wB[:, kc, cb + c * 128:cb + (c + 1) * 128], rhs=hT[:, kc, :], start=(kc == 0), stop=(kc == 7))
                             for kc in range(8)], [bhT, bwin], [bb[bk]])
                    yield
                    yield
                    O("dve", "tensor_copy", writes=[bb[5], bxin], out=xin_sb[:], in_=banks[5][:])
                    O("pool", "tensor_copy", reads=[buh], writes=[bu], out=u[:, 0:2], in_=uhalo[:, c, :])
                    yield
                    yield
                    O("dve", "tensor_tensor", reads=[bxin], writes=[bb[6], bu], out=u[:, 2:TB + 2], in0=banks[6][:], in1=xin_sb[:], op=ALU.mult)
                    O("pool", "tensor_copy", reads=[bu], writes=[buh], out=uhalo[:, c, :], in_=u[:, TB:TB + 2])
                    cw = lambda j, c=c: pv[:, PV_CMW + c * 3 + j:PV_CMW + c * 3 + j + 1]
                    O("dve", "tensor_scalar", reads=[bu, bpv], writes=[bt1], out=t1[:], in0=u[:, 0:TB], scalar1=cw(0), scalar2=None, op0=ALU.mult)
                    O("dve", "scalar_tensor_tensor", reads=[bu, bpv], writes=[bt1], out=t1[:], in0=u[:, 1:TB + 1], scalar=cw(1), in1=t1[:], op0=ALU.mult, op1=ALU.add)
                    O("dve", "scalar_tensor_tensor", reads=[bu, bpv], writes=[bt1], out=t1[:], in0=u[:, 2:TB + 2], scalar=cw(2), in1=t1[:], op0=ALU.mult, op1=ALU.add)
                    O("dve", "tensor_tensor", reads=[bt1], writes=[bb[7], bcv], out=cv[:], in0=banks[7][:], in1=t1[:], op=ALU.mult)
                    yield
                    yield
                    O("act", "activation", reads=[bcv], writes=[bsqb], out=sqb[:], in_=cv[:], func=AF.Square)
                    yield
                    yield
                    mmg([dict(out=banks[5][:], lhsT=G64[:], rhs=sqb[:], start=True, stop=True)], [bsqb, bconst], [bb[5]])
                    yield
                    yield
                    O("act", "activation", reads=[bconst], writes=[bb[5], bt1], out=t1[:], in_=banks[5][:], func=AF.Ln, scale=1.0 / 64, bias=epsT[:, 0:1])
                    O("act", "activation", writes=[bt1], out=t1[:], in_=t1[:], func=AF.Exp, scale=-0.5)
                    yield
                    yield
                    O("dve", "scalar_tensor_tensor", reads=[bcv, bt1, bpv], writes=[bmixc], out=mixc[:, c, :], in0=cv[:],
                      scalar=pv[:, PV_COG + c:PV_COG + c + 1], in1=t1[:], op0=ALU.mult, op1=ALU.mult)
                    yield
                    yield
                if debug and tb == 0:
                    dbg_out("d_hT", hT[:].rearrange("p c t -> p (c t)"), bhT, eng="pool")
                    dbg_out("d_qT", qT[:].rearrange("p h t -> p (h t)"), bqT, eng="pool")
                    dbg_out("d_gates", gates[:].rearrange("p s c -> p (s c)"), bgates, eng="pool")
                    dbg_out("d_ksT", ksT[:, :, 0:TB], bks[0], eng="pool", dst=dbg["d_ksT"].rearrange("p (g t) -> p g t", g=2))

            def attn_items(items, qT, bqT, h, fin_prev=None):
                n = len(items)
                LOOK = 2
                sb_of = {}
                started = set()
                last_of = {}
                for ix, it in enumerate(items):
                    last_of[it["ob"]] = ix
                if fin_prev is not None:
                    fin_prev()
                    fin_prev = None

                def emit_S(ix):
                    it = items[ix]
                    bk = rot["s"] % 3
                    rot["s"] += 1
                    sb_of[ix] = bk
                    c0, c1 = it["c0"], it["c1"]
                    mm = [dict(out=banks[bk][:, c0:c1], lhsT=it["kT"], rhs=qT[0:68, h, c0:c1], start=True, stop=False, skip=True)]
                    for (lt, rh, e0, e1) in it["extra"]:
                        mm.append(dict(out=banks[bk][:, e0:e1], lhsT=lt, rhs=rh, start=False, stop=False, skip=True))
                    mm[-1]["stop"] = True
                    mmg(mm, it["reads"] + [bqT, bconst], [bb[bk]])

                for ix in range(min(LOOK, n)):
                    emit_S(ix)
                for ix in range(n):
                    if ix + LOOK < n:
                        emit_S(ix + LOOK)
                    it = items[ix]
                    bk = sb_of[ix]
                    ob = it["ob"]
                    c0, c1 = it["c0"], it["c1"]
                    pi = rot["pt"] % NPT
                    rot["pt"] += 1
                    O("act", "activation", writes=[bb[bk], bPT[pi]], out=PT[pi][:, c0:c1], in_=banks[bk][:, c0:c1], func=AF.Exp)
                    mmg([dict(out=banks[ob][0:65, c0:c1], lhsT=it["vrhs"], rhs=PT[pi][:, c0:c1], start=(ob not in started), stop=(ix == last_of[ob]), skip=True)],
                        [bPT[pi]] + it["vreads"], [bb[ob]])
                    started.add(ob)
                    if ix == 3 and fin_prev is not None:
                        fin_prev()
                        fin_prev = None
                    tick(pend.get("k", 1))
                if fin_prev is not None:
                    fin_prev()

            def tail_g(tb):
                mixc, bmixc = mixcs[tb % 2], bmixcs[tb % 2]
                pend["tail"] = True
                for c in range(4):
                    tp4 = banks[7][:].bitcast(BF16)[:, 0:512]
                    for s in range(4):
                        O("pe", "transpose", reads=[battnb, bconst], writes=[bb[7]], signal=(s == 3), out=tp4[:, s * 128:(s + 1) * 128],
                          in_=attnb[:, s, c * 128:(c + 1) * 128], identity=ident_bf[:])
                    yield
                    O("dve", "tensor_copy", writes=[bb[7], bmixa], out=mixa[:, c, :], in_=tp4)
                    yield
                for s in range(4):
                    ti = tb * 4 + s
                    i = ti % 2
                    P.dma("sp", lambda E, i=i, ti=ti: E.dma_start(out=xr[i][:], in_=x_t[ti]), writes=[bxr[i]])
                    for hf in range(2):
                        bk = 5 + hf
                        mm = []
                        for kc in range(8):
                            lt = mixa[:, kc, s * 128:(s + 1) * 128] if kc < 4 else mixc[:, kc - 4, s * 128:(s + 1) * 128]
                            mm.append(dict(out=banks[bk][:], lhsT=lt, rhs=wout[:, kc, hf * 512:(hf + 1) * 512], start=(kc == 0), stop=(kc == 7)))
                        mmg(mm, [bmixa, bmixc, bwout], [bb[bk]])
                    yield
                    for hf in range(2):
                        bk = 5 + hf
                        O("dve", "tensor_tensor", reads=[bg1], writes=[bb[bk], btmpo[hf]], out=tmpo[hf][:], in0=banks[bk][:], in1=g1row[:, hf * 512:(hf + 1) * 512], op=ALU.mult)
                    yield
                    for hf in range(2):
                        O("pool", "tensor_tensor", reads=[btmpo[hf]], writes=[bxr[i]], out=xr[i][:, hf * 512:(hf + 1) * 512], in0=xr[i][:, hf * 512:(hf + 1) * 512],
                          in1=tmpo[hf][:], op=ALU.add)
                    P.dma("sp", lambda E, i=i, ti=ti: E.dma_start(out=out_t[ti], in_=xr[i][:]), reads=[bxr[i]], writes=[obuf[ti]])
                    if s == 3:
                        pend["tail"] = False
                    yield

            def spaced(gen, n):
                for _ in gen:
                    for _k in range(n):
                        yield

            def chain(*gens):
                for g_ in gens:
                    if g_ is not None:
                        yield from g_

            if N2 > 0:
                pend["gen"] = frontend(0)
                drain()
            for tb in range(N2):
                qT, bqT = qTs[tb % 2], bqTs[tb % 2]
                gates, bgates = gatess[tb % 2], bgatess[tb % 2]
                mixc, bmixc = mixcs[tb % 2], bmixcs[tb % 2]
                if tb + 1 < N2 or tb >= 1:
                    pend["gen"] = chain(spaced(tail_g(tb - 1), 2) if (tb >= 1 and not debug) else None, frontend(tb + 1) if tb + 1 < N2 else None)
                    nticks = 8 * ((4 * tb + 4) + (4 if tb == 0 else 8)) + 8
                    pend["k"] = max(1, -(-230 // nticks))
                for g in range(2):
                    ctiles = [0] if tb < 4 else [0, 1]
                    for hi in range(4):
                        h = g * 4 + hi
                        pbase = 1 if hi % 2 == 0 else 3
                        ptl = {}
                        for tl in ctiles:
                            bk = 0
                            mm = [dict(out=banks[bk][:], lhsT=kcT[0:68, g, tl * 128:(tl + 1) * 128], rhs=qT[0:68, h, :], start=True, stop=False, skip=True)]
                            dl = 32 * tb - 128 * tl
                            if 16 * dl < 2048:
                                mm.append(dict(out=banks[bk][:], lhsT=ident_bf[:], rhs=Lc[:, 16 * dl:16 * dl + 512], start=False, stop=False, skip=True))
                            mm[-1]["stop"] = True
                            mmg(mm, [bkcT, bqT, bconst], [bb[bk]])
                            pi = rot["pt"] % NPT
                            rot["pt"] += 1
                            ptl[tl] = pi
                            O("act", "activation", writes=[bb[bk], bPT[pi]], out=PT[pi][:], in_=banks[bk][:], func=AF.Exp)
                        for half in range(2):
                            ob = pbase + half
                            mm = []
                            for s2 in range(2):
                                s = half * 2 + s2
                                for k_, tl in enumerate(ctiles):
                                    mm.append(dict(out=banks[ob][:, s2 * 256:s2 * 256 + 129], lhsT=PT[ptl[tl]][:, s * 128:(s + 1) * 128],
                                                   rhs=vcx[:, tl, g, :], start=(k_ == 0), stop=(k_ == len(ctiles) - 1)))
                            mmg(mm, [bPT[ptl[tl]] for tl in ctiles] + [bvcx], [bb[ob]])
                        for half in range(2):
                            ob = pbase + half
                            ov = banks[ob][:].rearrange("p (s c) -> p s c", s=2)
                            sl = slice(half * 2, half * 2 + 2)
                            r0_ = half * 2
                            if tb == 0:
                                O("dve", "tensor_scalar", writes=[bb[ob], brw], out=rw[:, r0_:r0_ + 2], in0=ov[:, :, 64], scalar1=1e-30, scalar2=None, op0=ALU.max)
                                O("dve", "reciprocal", writes=[brw], out=rw[:, r0_:r0_ + 2], in_=rw[:, r0_:r0_ + 2])
                            else:
                                O("dve", "reciprocal", writes=[bb[ob], brw], out=rw[:, r0_:r0_ + 2], in_=ov[:, :, 64])
                            O("dve", "tensor_tensor", reads=[bgates], writes=[brw], out=rw[:, 12 + r0_:14 + r0_], in0=rw[:, r0_:r0_ + 2], in1=gates[:, sl, h * 3 + 0], op=ALU.mult)
                            O("dve", "tensor_tensor", reads=[brw], writes=[bb[ob], bacc], out=acc[:, sl, hi, :], in0=ov[:, :, 0:64], in1=bc(rw[:, 12 + r0_:14 + r0_], 64), op=ALU.mult)
                            O("dve", "tensor_tensor", reads=[brw], writes=[bb[ob], bUraw], out=Uraw[:, sl, hi, :], in0=ov[:, :, 65:129], in1=bc(rw[:, r0_:r0_ + 2], 64), op=ALU.mult)
                        tick(pend.get("k", 1))
                    O("dve", "tensor_reduce", reads=[bUraw], writes=[bimp], out=imp[:], in_=Uraw.rearrange("p s h j -> p s j h"), axis=AX.X, op=ALU.add)
                    if debug and tb == 1 and g == 0:
                        dbg_out("d_imp", imp[:].rearrange("p s j -> p (s j)"), bimp, eng="pool")
                    for s in range(4):
                        for hf in range(2):
                            cur = tb * 8 + s * 2 + hf
                            rows = slice(hf * 64, hf * 64 + 64)
                            if cur + 1 < 64:
                                O("pool", "memset", writes=[bimp], ap=imp[rows, s, cur + 1:64], constant=-1.0)
                            O("pool", "memset", writes=[bimp], ap=imp[rows, s, max(cur - 1, 0):cur + 1], constant=1.0e4)
                            O("pool", "memset", writes=[bimp], ap=imp[rows, s, 0:1], constant=1.0e4)
                    for s in range(4):
                        O("dve", "max", reads=[bimp], writes=[btk], out=m8[:, 0:8], in_=imp[:, s, :])
                        O("dve", "match_replace", reads=[bimp, btk], writes=[btk], out=wk[:], in_to_replace=m8[:, 0:8], in_values=imp[:, s, :], imm_value=-1.0e30)
                        O("dve", "max", reads=[btk], writes=[btk], out=m8[:, 8:16], in_=wk[:])
                        O("dve", "tensor_scalar", reads=[bimp, btk], writes=[bselb], out=selb[:, s, :], in0=imp[:, s, :], scalar1=m8[:, 15:16], scalar2=NEG,
                          op0=ALU.is_lt, op1=ALU.mult)
                    if debug and tb == 1 and g == 0:
                        dbg_out("d_selb", selb[:].rearrange("p s j -> p (s j)"), bselb, eng="pool")
                    fin = [None]
                    for hi in range(4):
                        h = g * 4 + hi
                        it_sel, it_win = [], []
                        for kt in range(4 * tb + 4):
                            smin = max(0, kt - 4 * tb)
                            c0 = smin * 128
                            extra = [(Eall[:, kt * 128:(kt + 1) * 128], selbT[g][:, c0:TB], c0, TB)]
                            if kt >= 4 * tb:
                                extra.append((ident_bf[:], Ldiag[:], c0, c0 + 128))
                            it_sel.append(dict(ob=3, kT=ksT[0:68, g, kt * 128:(kt + 1) * 128], c0=c0, c1=TB, extra=extra,
                                               vrhs=vsa[:, kt, g, :], reads=[bks[kt // 4], bksaug, bselbT[g]], vreads=[bvs[kt // 4], bvs1]))
                        for m in range(8):
                            kta = 4 * tb - 4 + m
                            if kta < 0:
                                continue
                            slo, shi = max(0, m - 4), min(3, m)
                            extra = []
                            if m >= 4:
                                extra.append((ident_bf[:], Ldiag[:], (m - 4) * 128, (m - 4) * 128 + 128))
                            if m <= 3:
                                extra.append((ident_bf[:], Lfar[:], m * 128, m * 128 + 128))
                            ri = (kta // 4) % 3
                            kr = ri * TB + (kta % 4) * 128
                            it_win.append(dict(ob=4, kT=kwT[0:68, g, kr:kr + 128], c0=slo * 128, c1=(shi + 1) * 128, extra=extra,
                                               vrhs=vwa[:, ri * 4 + kta % 4, g, :], reads=[bkw[ri]], vreads=[bvw[ri], bvw1]))
                        if hi == 0:
                            items = it_win + it_sel
                        else:
                            items = []
                            for k_ in range(max(len(it_sel), len(it_win))):
                                if k_ < len(it_sel):
                                    items.append(it_sel[k_])
                                if k_ < len(it_win):
                                    items.append(it_win[k_])
                        if hi == 0:
                            tbk = rot["s"] % 3
                            rot["s"] += 1
                            tp3 = banks[tbk][0:64, 0:256].bitcast(BF16)
                            for s_ in range(4):
                                O("pe", "transpose", reads=[bselb, bconst], writes=[bb[tbk]], signal=(s_ == 3), out=tp3[:, s_ * 128:(s_ + 1) * 128],
                                  in_=selb[:, s_, :], identity=ident_bf[:])
                            O("dve", "tensor_copy", writes=[bb[tbk], bselbT[g]], out=selbT[g][0:64, :], in_=tp3)
                        attn_items(items, qT, bqT, h, fin_prev=fin[0])
                        for br, ob in ((1, 3), (2, 4)):
                            O("dve", "tensor_copy", writes=[bb[ob], bOTs[br]], out=OTs[br][:], in_=banks[ob][0:65, :])

                        def finalize(hi=hi, h=h):
                            for br, ob in ((1, 3), (2, 4)):
                                tb_ = rot["s"] % 3
                                rot["s"] += 1
                                ovw = banks[tb_][:].rearrange("p (s c) -> p s c", s=4)
                                for s_ in range(4):
                                    O("pe", "transpose", reads=[bOTs[br], bconst], writes=[bb[tb_]], signal=(s_ == 3), out=ovw[:, s_, 0:65],
                                      in_=OTs[br][:, s_ * 128:(s_ + 1) * 128], identity=ident_f[0:65, 0:65])
                                O("dve", "reciprocal", writes=[bb[tb_], brw], out=rw[:, 4:8], in_=ovw[:, :, 64])
                                O("dve", "tensor_tensor", reads=[bgates], writes=[brw], out=rw[:, 8:12], in0=rw[:, 4:8], in1=gates[:, :, h * 3 + br], op=ALU.mult)
                                O("dve", "tensor_tensor", reads=[brw], writes=[bb[tb_], bscr], out=scr[:, 0:256].rearrange("p (s d) -> p s d", s=4),
                                  in0=ovw[:, :, 0:64], in1=bc(rw[:, 8:12], 64), op=ALU.mult)
                                O("pool", "tensor_tensor", reads=[bscr], writes=[bacc], out=acc[:, :, hi, :], in0=acc[:, :, hi, :],
                                  in1=scr[:, 0:256].rearrange("p (s d) -> p s d", s=4), op=ALU.add)
                        fin[0] = finalize
                    fin[0]()
                    if debug and tb == 1 and g == 0:
                        dbg_out("d_acc", acc[:].rearrange("p s h d -> p (s h d)"), bacc, eng="pool")
                    while pend.get("tail"):
                        tick()
                    accf = acc[:].rearrange("p s h d -> p (s h d)")
                    O("dve", "tensor_tensor", reads=[bacc], writes=[bscr], out=scr[:], in0=accf, in1=accf, op=ALU.mult)
                    O("dve", "tensor_reduce", reads=[bscr], writes=[bssq2], out=ssq2[:, 0:16], in_=scr[:].rearrange("p (a d) -> p a d", d=64), axis=AX.X, op=ALU.add)
                    rstd_small(ssq2[:, 0:16], 64, 16, [bssq2])
                    O("dve", "tensor_tensor", reads=[bacc, bssq2], writes=[bscr], out=scr[:].rearrange("p (a d) -> p a d", d=64),
                      in0=acc[:].rearrange("p s h d -> p (s h) d"), in1=bc(ssq2[:, 0:16], 64), op=ALU.mult)
                    O("dve", "tensor_tensor", reads=[bscr, brv], writes=[battnb], out=attnb[:, :, g * 256:(g + 1) * 256],
                      in0=scr[:].rearrange("p (s c) -> p s c", s=4),
                      in1=rv[:, RV_AOG + g * 256:RV_AOG + (g + 1) * 256].unsqueeze(1).broadcast_to([128, 4, 256]), op=ALU.mult)
                drain()
                if debug and tb == 1:
                    dbg_out("d_attnb", attnb[:].rearrange("p s c -> p (s c)"), battnb, eng="pool")
                if debug:
                    pend["gen"] = tail_g(tb)
                    drain()
            if N2 > 0 and not debug:
                pend["gen"] = tail_g(N2 - 1)
                drain()
            P.run()

        stW.close()

        with ExitStack() as st:
            sb = lambda n, s, d=F32: st.enter_context(nc.sbuf_tensor("ph3_" + n, list(s), d))
            g2row, bg2 = make_row(st, "g2row", 40)
            wg = sb("wg", [128, 8, DFF], BF16); wu = sb("wu", [128, 8, DFF], BF16); wd = sb("wd", [128, NFF, D], BF16)
            bwg, bwu, bwd = Buf("wg"), Buf("wu"), Buf("wd")
            wg_v = wg_d.rearrange("(kc p) n -> p kc n", p=128); wu_v = wu_d.rearrange("(kc p) n -> p kc n", p=128)
            wd_v = wd_d.rearrange("(c p) n -> p c n", p=128)
            bwgc = [Buf(f"wg{i}") for i in range(2)]; bwuc = [Buf(f"wu{i}") for i in range(2)]; bwdc = [Buf(f"wd{i}") for i in range(11)]
            def load_w(pc):
                c0 = pc * 1408
                P.dma("pool", lambda E, c0=c0: E.dma_start(out=wg[:, :, c0:c0 + 1408], in_=wg_v[:, :, c0:c0 + 1408]), writes=[bwgc[pc]], ndesc=1024)
                P.dma("pool", lambda E, c0=c0: E.dma_start(out=wu[:, :, c0:c0 + 1408], in_=wu_v[:, :, c0:c0 + 1408]), writes=[bwuc[pc]], ndesc=1024)

            def load_wd(pc):
                P.dma("pool", lambda E, pc=pc: E.dma_start(out=wd[:, 2 * pc:2 * pc + 2, :], in_=wd_v[:, 2 * pc:2 * pc + 2, :]), writes=[bwdc[pc]], ndesc=256)
            load_w(0)
            NS = TBF // 128
            x1ts = [sb(f"x1t{i}", [128, NS, D]) for i in range(2)]; bx1s = [[Buf(f"x1t{i}_{j}") for j in range(NS)] for i in range(2)]
            hTs = [sb(f"hT3_{i}", [128, 8, TBF + 2], BF16) for i in range(2)]; bhTs = [Buf("hT3_0"), Buf("hT3_1")]
            env = dict(rr=0, xs=None, bxs=None,
                       xn=[sb("xn0", [128, D], BF16), sb("xn1", [128, D], BF16)], bxn=[Buf("xn0"), Buf("xn1")],
                       junk=sb("junk", [128, D], BF16), bjunk=Buf("junk"), ss=sb("ss", [128, 4]), bss=Buf("ss"))
            act = sb("act", [128, NFF, TBF], BF16); bact = [Buf(f"act{i}") for i in range(NFF)]
            NSL = 3
            GU = [(0, 1), (2, 3), (4, 5)]
            A0 = [sb(f"A0_{i}", [128, TBF]) for i in range(NSL)]; A1 = [sb(f"A1_{i}", [128, TBF]) for i in range(NSL)]
            bA = [Buf(f"A{i}") for i in range(NSL)]
            t2 = [sb(f"t2_{i}", [128, TBF]) for i in range(NSL)]; bt2 = [Buf(f"t2{i}") for i in range(NSL)]
            tmpo = [sb("tmpf0", [128, 512]), sb("tmpf1", [128, 512])]; btmpo = [Buf("tf0"), Buf("tf1")]
            O("pool", "memset", writes=[bhTs[0]], ap=hTs[0][:, :, 0:2], constant=0.0)
            fw = lambda j, c: pv[:, PV_CFW + c * 3 + j:PV_CFW + c * 3 + j + 1]

            def prep(fb):
                i = fb % 2
                make_hT(env, out_t, fb * NS, NS, ab[:, 8:16], mod[:, 24:32], hTs[i][:, :, 2:TBF + 2], bhTs[i], keep=x1ts[i], bkeep=bx1s[i],
                        src_bufs=obuf, tbank=6)
                if fb > 0:
                    O("pool", "tensor_copy", reads=[bhTs[1 - i]], writes=[bhTs[i]], out=hTs[i][:, :, 0:2], in_=hTs[1 - i][:, :, TBF:TBF + 2])

            def ffn_A(c, hT, bhT):
                sl = c % NSL
                bg_, bu_ = GU[sl]
                pc = (c * 128) // 1408
                pc2 = (c * 128 + 127) // 1408
                rd = [bwgc[pc], bwuc[pc], bhT] + ([bwgc[pc2], bwuc[pc2]] if pc2 != pc else [])
                mmg([dict(out=banks[bg_][:, 0:TBF + 2], lhsT=wg[:, kc, c * 128:(c + 1) * 128], rhs=hT[:, kc, 0:TBF + 2], start=(kc == 0), stop=(kc == 7)) for kc in range(8)],
                    rd, [bb[bg_]])
                mmg([dict(out=banks[bu_][:, 0:TBF], lhsT=wu[:, kc, c * 128:(c + 1) * 128], rhs=hT[:, kc, 2:TBF + 2], start=(kc == 0), stop=(kc == 7)) for kc in range(8)],
                    rd, [bb[bu_]])

            def ffn_B(c):
                sl = c % NSL
                bg_, bu_ = GU[sl]
                G = banks[bg_]
                O("act", "activation", reads=[bpv], writes=[bb[bg_], bA[sl]], out=A0[sl][:], in_=G[:, 0:TBF], func=AF.Identity, scale=fw(0, c))
                O("act", "activation", reads=[bpv], writes=[bb[bg_], bA[sl]], out=A1[sl][:], in_=G[:, 1:TBF + 1], func=AF.Identity, scale=fw(1, c))
                O("dve", "scalar_tensor_tensor", reads=[bA[sl], bpv], writes=[bb[bg_], bt2[sl]], out=t2[sl][:], in0=G[:, 2:TBF + 2], scalar=fw(2, c),
                  in1=A1[sl][:], op0=ALU.mult, op1=ALU.add)
                O("pool", "tensor_tensor", reads=[bA[sl]], writes=[bt2[sl]], out=t2[sl][:], in0=t2[sl][:], in1=A0[sl][:], op=ALU.add)

            def ffn_C(c):
                sl = c % NSL
                bg_, bu_ = GU[sl]
                O("act", "activation", writes=[bt2[sl]], out=t2[sl][:], in_=t2[sl][:], func=AF.Silu)
                O("dve", "tensor_tensor", reads=[bt2[sl]], writes=[bb[bu_], bact[c]], out=act[:, c, :], in0=banks[bu_][:, 0:TBF], in1=t2[sl][:], op=ALU.mult)

            if NB > 0:
                prep(0)
            for fb in range(NB):
                i = fb % 2
                x1t, bx1 = x1ts[i], bx1s[i]
                for c in range(NFF + 2):
                    if c < NFF:
                        ffn_A(c, hTs[i], bhTs[i])
                    if 1 <= c <= NFF:
                        ffn_B(c - 1)
                    if c >= 2:
                        ffn_C(c - 2)
                    if c == 6 and fb + 1 < NB:
                        prep(fb + 1)
                    if fb == 0:
                        if c == 1:
                            load_w(1)
                        if c >= 9 and c - 9 < 11:
                            load_wd(c - 9)
                for s in range(NS):
                    ti = fb * NS + s
                    for hf in range(2):
                        bk = 6 + hf
                        for c in range(NFF):
                            O("pe", "matmul", reads=[bact[c], bwdc[c // 2]], writes=[bb[bk]], signal=(c == NFF - 1), out=banks[bk][:],
                              lhsT=act[:, c, s * 128:(s + 1) * 128], rhs=wd[:, c, hf * 512:(hf + 1) * 512], start=(c == 0), stop=(c == NFF - 1))
                        O("dve", "tensor_tensor", reads=[bg2], writes=[bb[bk], btmpo[hf]], out=tmpo[hf][:], in0=banks[bk][:], in1=g2row[:, hf * 512:(hf + 1) * 512], op=ALU.mult)
                        O("pool", "tensor_tensor", reads=[btmpo[hf]], writes=[bx1[s]], out=x1t[:, s, hf * 512:(hf + 1) * 512], in0=x1t[:, s, hf * 512:(hf + 1) * 512],
                          in1=tmpo[hf][:], op=ALU.add)
                    tk = P.dma("sp", lambda E, s=s, ti=ti, x1t=x1t: E.dma_start(out=out_t[ti], in_=x1t[:, s, :]), reads=[bx1[s]], writes=[obuf[ti]])
                    final_toks.append(tk)
            for tk in final_toks:
                P.wait("sp", tk)
            P.run()
        print("instr counts", P.ninstr, "sems", P.nsem)
    return nc


def _alibi_slopes(n):
    return np.array([2.0 ** (-8.0 * (h + 1) / n) for h in range(n)], dtype=np.float32)


def _const_tables():
    t = np.arange(S)
    sl = _alibi_slopes(8)
    qaug = np.zeros((4, 8, S), np.float32)
    for h in range(8):
        qaug[0, h] = -64.0 * sl[h] * (t // 64)
        qaug[1, h] = -sl[h] * (t % 64)
        qaug[2, h] = 64.0 * sl[h]
        qaug[3, h] = sl[h]
    kaug = np.zeros((4, S), np.float32)
    kaug[0] = 1.0; kaug[1] = 1.0; kaug[2] = t // 64; kaug[3] = t % 64
    npr = np.arange(256)
    cend = 16 * npr + 15
    caug = np.zeros((4, 256), np.float32)
    caug[0] = 1.0; caug[1] = 1.0; caug[2] = cend // 64; caug[3] = cend % 64
    caug[2, 0] = -10000.0
    n_cmp = (S - 32) // 16 + 1
    cs = np.arange(n_cmp) * 16
    bs = np.arange(64) * 64
    overlap = ((cs[None, :] < bs[:, None] + 64) & (cs[None, :] + 32 > bs[:, None])).astype(np.float32)
    ovfull = np.zeros((256, 64), np.float32)
    ovfull[1:1 + n_cmp] = overlap.T
    ovT = np.ascontiguousarray(ovfull.reshape(2, 128, 64).transpose(1, 0, 2))
    return qaug, kaug, caug, ovT


_NC_CACHE = {}


def kernel(**inputs):
    f = lambda k: np.ascontiguousarray(np.asarray(inputs[k], dtype=np.float32))
    x = f("x"); c = f("c")
    B = x.shape[0]
    col = lambda v: np.ascontiguousarray(v.reshape(-1, 128).T)
    rep = lambda v: np.broadcast_to(v.reshape(1, -1), (128, v.size))
    qaug, kaug, caug, ovT = _const_tables()
    b_ada = f("b_ada")[0]
    cmw = f("conv_mix_w")[0]
    cfw = f("conv_ffn_w")[0]
    shared_pv = [col(b_ada), col(f("norm1_g")[0]), col(f("norm2_g")[0]),
                 cmw.reshape(3, 4, 128).transpose(2, 1, 0).reshape(128, 12),
                 col(f("conv_out_g")[0]),
                 np.tile(f("k_norm_cmp_g")[0].reshape(64, 1), (2, 1))]
    cfw_l = cfw.reshape(3, NFF, 128).transpose(2, 1, 0).reshape(128, 66)
    rowvecs = np.ascontiguousarray(np.concatenate([rep(f("b_gate")[0]), rep(f("q_norm_g")[0]), rep(f("k_norm_slc_g")[0]),
                                                   rep(f("k_norm_win_g")[0]), rep(f("attn_out_g")[0])], axis=1), dtype=np.float32)
    posT = np.ascontiguousarray(np.concatenate([np.tile(f("pos_cmp_k")[0].T, (2, 1)), np.tile(f("pos_cmp_v")[0].T, (2, 1))], axis=1), dtype=np.float32)
    common = {
        "w_ada": f("w_ada")[0], "rowvecs": rowvecs, "w_in": f("w_in")[0], "posT": posT,
        "w_cmp_k1": f("w_cmp_k1")[0], "w_cmp_v1": f("w_cmp_v1")[0], "w_cmp_k2": f("w_cmp_k2")[0], "w_cmp_v2": f("w_cmp_v2")[0],
        "w_out": f("w_out")[0], "w_ffn_gate": f("w_ffn_gate")[0], "w_ffn_up": f("w_ffn_up")[0], "w_ffn_down": f("w_ffn_down")[0],
        "qaug": qaug, "kaug": kaug, "caug": caug, "ovT": ovT,
    }
    in_maps = []
    for b in range(B):
        cT = np.repeat(col(c[b]), 2, axis=1)
        pvecs = np.ascontiguousarray(np.concatenate(shared_pv + [cT, cfw_l], axis=1), dtype=np.float32)
        assert pvecs.shape == (128, 163), pvecs.shape
        m = dict(common)
        m["x"] = x[b]
        m["pvecs"] = pvecs
        in_maps.append(m)
    if "nc" not in _NC_CACHE:
        _NC_CACHE["nc"] = build_program(DEBUG)
    nc = _NC_CACHE["nc"]
    res = run_bass_kernel_spmd(nc, in_maps, core_ids=list(range(B)))
    if DEBUG:
        kernel.last = res.results
    return np.stack([np.asarray(r["out"], dtype=np.float32) for r in res.results], axis=0)
```

```python
import numpy as np
from contextlib import ExitStack
import concourse.bass as bass
import concourse.mybir as mybir
from concourse.bass_utils import run_bass_kernel_spmd

F32 = mybir.dt.float32
BF16 = mybir.dt.bfloat16
ALU = mybir.AluOpType
AF = mybir.ActivationFunctionType
AX = mybir.AxisListType

S = 4096
D = 1024
NIN = 2840
DFF = 2816
NFF = 22
EPS = 1e-6
NEG = -30000.0
TB = 512
NTB = S // TB
TBF = 256
ENGINES = ("pe", "act", "dve", "pool", "sp")
SEM_CAP = 30000
DEBUG = False
import os
N1 = int(os.environ.get('K_N1', NTB))
N2 = int(os.environ.get('K_N2', NTB))
NB = int(os.environ.get('K_NB', S // TBF))
CUT = int(os.environ.get('K_CUT', 99))


class Buf:
    __slots__ = ("name", "w", "r")

    def __init__(self, name):
        self.name = name
        self.w = None
        self.r = {}


class Prog:
    def __init__(self, nc, stack, n_dma_sems=24):
        self.nc = nc
        self.stack = stack
        self.q = {e: [] for e in ENGINES}
        self.sem = {}
        self.cnt = {}
        self.nsem = 0
        for e in ENGINES:
            self._new_sem(e)
        self.waited = {}
        self.dma_sems = {e: [stack.enter_context(nc.semaphore(f"dma_{e}{i}")) for i in range(n_dma_sems)] for e in ("sp", "pool")}
        self.dma_cnt = {e: [0] * n_dma_sems for e in ("sp", "pool")}
        self.dma_rr = {"sp": 0, "pool": 0}
        self.ninstr = {e: 0 for e in ENGINES}

    def _new_sem(self, e):
        self.sem[e] = self.stack.enter_context(self.nc.semaphore(f"s_{e}_{self.nsem}"))
        self.nsem += 1
        self.cnt[e] = 0
        self.own = getattr(self, "own", {})
        self.own.setdefault(e, set()).add(id(self.sem[e]))

    def _wait(self, eng, tok):
        if tok is None:
            return
        sem, val = tok
        key = (eng, id(sem))
        if self.waited.get(key, 0) >= val:
            return
        self.waited[key] = val
        self.q[eng].append(lambda E, sem=sem, val=val: E.wait_ge(sem, val))

    def _deps(self, eng, reads, writes):
        for b in reads:
            self._wait(eng, b.w)
        for b in writes:
            self._wait(eng, b.w)
            for t in b.r.values():
                self._wait(eng, t)

    def _mark(self, tok, reads, writes):
        for b in reads:
            b.r[id(tok[0])] = tok
        for b in writes:
            b.w = tok
            b.r = {}

    def op(self, eng, fn, reads=(), writes=(), signal=True, relaxed=()):
        for b in relaxed:
            if b.w is not None and id(b.w[0]) not in self.own[eng]:
                self._wait(eng, b.w)
            for t in b.r.values():
                self._wait(eng, t)
        writes = list(writes) + list(relaxed) if relaxed else writes
        self._deps(eng, reads, [b for b in writes if b not in relaxed] if relaxed else writes)
        self.ninstr[eng] += 1
        if not signal:
            self.q[eng].append(lambda E, fn=fn: fn(E))
            return None
        if self.cnt[eng] >= SEM_CAP:
            self._new_sem(eng)
        self.cnt[eng] += 1
        sem, val = self.sem[eng], self.cnt[eng]
        self.q[eng].append(lambda E, fn=fn, sem=sem: fn(E).then_inc(sem, 1))
        tok = (sem, val)
        self._mark(tok, reads, writes)
        return tok

    def dma(self, eng, fn, reads=(), writes=(), ndesc=256):
        if eng == "pool":
            self.outst = getattr(self, "outst", [])
            while self.outst and sum(n for _, n in self.outst) + ndesc > 5000:
                tok0, _ = self.outst.pop(0)
                self._wait("pool", tok0)
        i = self.dma_rr[eng]
        self.dma_rr[eng] = (i + 1) % len(self.dma_sems[eng])
        sem = self.dma_sems[eng][i]
        if self.dma_cnt[eng][i] > 0:
            self._wait(eng, (sem, self.dma_cnt[eng][i]))
        self._deps(eng, reads, writes)
        self.dma_cnt[eng][i] += 16
        val = self.dma_cnt[eng][i]
        self.q[eng].append(lambda E, fn=fn, sem=sem: fn(E).then_inc(sem, 16))
        tok = (sem, val)
        self._mark(tok, reads, writes)
        self.ninstr[eng] += 1
        if eng == "pool":
            self.outst.append((tok, ndesc))
        return tok

    def wait(self, eng, tok):
        self._wait(eng, tok)

    def run(self):
        nc = self.nc
        for e in ("sp", "pool"):
            for i, sem in enumerate(self.dma_sems[e]):
                if self.dma_cnt[e][i] > 0:
                    self._wait(e, (sem, self.dma_cnt[e][i]))
        q = self.q
        with nc.Block() as block:
            @block.tensor
            def _(E):
                for f in q["pe"]:
                    f(E)

            @block.scalar
            def _(E):
                for f in q["act"]:
                    f(E)

            @block.vector
            def _(E):
                for f in q["dve"]:
                    f(E)

            @block.gpsimd
            def _(E):
                for f in q["pool"]:
                    f(E)

            @block.sync
            def _(E):
                for f in q["sp"]:
                    f(E)
        self.q = {e: [] for e in ENGINES}


def bc(ap, shape):
    return ap.unsqueeze(len(ap.shape)).broadcast_to(list(ap.shape) + [shape])


def build_program(debug=False):
    nc = bass.Bass("TRN2", target_bir_lowering=False)
    dt = lambda n, s, k="ExternalInput": nc.dram_tensor(n, list(s), F32, kind=k).ap()
    x_d = dt("x", [S, D])
    wada_d = dt("w_ada", [D, 6 * D])
    pv_d = dt("pvecs", [128, 163])
    rv_d = dt("rowvecs", [128, 728])
    win_d = dt("w_in", [D, NIN])
    pos_d = dt("posT", [128, 64])
    w1k_d = dt("w_cmp_k1", [2048, 256])
    w1v_d = dt("w_cmp_v1", [2048, 256])
    w2k_d = dt("w_cmp_k2", [256, 64])
    w2v_d = dt("w_cmp_v2", [256, 64])
    wout_d = dt("w_out", [D, D])
    wg_d = dt("w_ffn_gate", [D, DFF])
    wu_d = dt("w_ffn_up", [D, DFF])
    wd_d = dt("w_ffn_down", [DFF, D])
    qaug_d = dt("qaug", [4, 8, S])
    kaug_d = dt("kaug", [4, S])
    caug_d = dt("caug", [4, 256])
    ov_d = dt("ovT", [128, 2, 64])
    out_d = dt("out", [S, D], "ExternalOutput")
    dbg = {}
    if debug:
        for n, s in [("d_mod", [128, 48]), ("d_hT", [128, 8 * TB]), ("d_kcT", [68, 512]), ("d_vcx", [128, 4 * 129]),
                     ("d_qT", [68, 8 * TB]), ("d_gates", [128, 96]), ("d_attnb", [128, 4 * 512]), ("d_mixT", [128, 8 * TB]),
                     ("d_ksT", [68, 2 * TB]), ("d_imp", [128, 256]), ("d_selb", [128, 256]), ("d_acc", [128, 1024])]:
            dbg[n] = nc.dram_tensor(n, s, F32, kind="ExternalOutput").ap()

    x_t = x_d.rearrange("(t p) d -> t p d", p=128)
    out_t = out_d.rearrange("(t p) d -> t p d", p=128)
    obuf = [Buf(f"out{i}") for i in range(32)]
    final_toks = []

    with ExitStack() as st0:
        P = Prog(nc, st0)
        sb0 = lambda n, s, d=F32: st0.enter_context(nc.sbuf_tensor(n, list(s), d))
        banks = [st0.enter_context(nc.psum_tensor(f"bank{i}", [128, 512], F32)) for i in range(8)]
        bb = [Buf(f"bank{i}") for i in range(8)]

        def O(eng, method, reads=(), writes=(), signal=True, relaxed=(), **kw):
            return P.op(eng, lambda E: getattr(E, method)(**kw), reads, writes, signal, relaxed)

        def mmg(mms, reads, writes):
            n = len(mms)
            tok = None
            for i, m in enumerate(mms):
                kw = dict(out=m["out"], lhsT=m["lhsT"], rhs=m["rhs"], start=m["start"], stop=m["stop"])
                if m.get("skip"):
                    kw["skip_group_check"] = True
                tok = O("pe", "matmul", reads, writes, signal=(i == n - 1), **kw)
            return tok

        def dbg_out(name, tile_ap, b, eng="pool", dst=None):
            if debug and name in dbg:
                d_ = dbg[name] if dst is None else dst
                t = P.dma(eng, lambda E: E.dma_start(out=d_, in_=tile_ap), reads=[b])
                final_toks.append(t)

        ident_bf = sb0("ident_bf", [128, 128], BF16); ident_f = sb0("ident_f", [128, 128])
        ones_bf = sb0("ones_bf", [128, 128], BF16); ones_f = sb0("ones_f", [128, 128])
        G64 = sb0("G64", [128, 128], BF16)
        pv = sb0("pv", [128, 163]); rv = sb0("rv", [128, 728])
        mod = sb0("mod", [128, 48]); ab = sb0("ab", [128, 16]); epsT = sb0("epsT", [128, 1])
        qg8 = sb0("qg8", [128, 64])
        stW = ExitStack()
        sbw = lambda n, s, d=F32: stW.enter_context(nc.sbuf_tensor("w_" + n, list(s), d))
        kcT = sbw("kcT", [68, 2, 256], BF16); vcx = sbw("vcx", [128, 2, 2, 129], BF16)
        Lc = sbw("Lc", [128, 2560], BF16); Ldiag = sbw("Ldiag", [128, 128], BF16); Lfar = sbw("Lfar", [128, 128], BF16)
        Eall = sbw("Eall", [128, S], BF16)
        wA = sbw("wA", [128, 8, 512], BF16)
        wB = sbw("wB", [128, 8, NIN - 768], BF16)
        bwin = Buf("win")
        wout = sbw("wout", [128, 8, D], BF16); bwout = Buf("wout")
        bconst = Buf("const"); bpv = Buf("pv"); brv = Buf("rv"); bmod = Buf("mod"); bkcT = Buf("kcT"); bvcx = Buf("vcx")

        PV_BADA, PV_N1G, PV_N2G, PV_CMW, PV_COG, PV_KCG, PV_CT, PV_CFW = 0, 48, 56, 64, 76, 80, 81, 97
        RV_BG, RV_QG, RV_KSG, RV_KWG, RV_AOG = 0, 24, 88, 152, 216

        P.dma("sp", lambda E: E.dma_start(out=pv[:], in_=pv_d), writes=[bpv])
        P.dma("sp", lambda E: E.dma_start(out=rv[:], in_=rv_d), writes=[brv])
        O("pool", "memset", writes=[bconst], ap=ident_bf[:], constant=1.0)
        O("pool", "affine_select", reads=[bconst], writes=[bconst], out=ident_bf[:], in_=ident_bf[:], pattern=[[1, 128]],
          compare_op=ALU.is_equal, fill=0.0, base=0, channel_multiplier=-1)
        O("pool", "tensor_copy", reads=[bconst], writes=[bconst], out=ident_f[:], in_=ident_bf[:])
        O("pool", "memset", writes=[bconst], ap=ones_bf[:], constant=1.0)
        O("pool", "memset", writes=[bconst], ap=ones_f[:], constant=1.0)
        O("pool", "memset", writes=[bconst], ap=epsT[:], constant=EPS)
        O("pool", "memset", writes=[bconst], ap=G64[:], constant=0.0)
        O("pool", "memset", writes=[bconst], ap=G64[0:64, 0:64], constant=1.0)
        O("pool", "memset", writes=[bconst], ap=G64[64:128, 64:128], constant=1.0)
        O("pool", "memset", writes=[bconst], ap=Lc[:], constant=0.0)
        O("pool", "affine_select", reads=[bconst], writes=[bconst], out=Lc[:], in_=Lc[:], pattern=[[1, 2560]],
          compare_op=ALU.is_ge, fill=NEG, base=-15, channel_multiplier=-16)
        O("pool", "memset", writes=[bconst], ap=Ldiag[:], constant=0.0)
        O("pool", "affine_select", reads=[bconst], writes=[bconst], out=Ldiag[:], in_=Ldiag[:], pattern=[[1, 128]],
          compare_op=ALU.is_ge, fill=NEG, base=0, channel_multiplier=-1)
        O("pool", "memset", writes=[bconst], ap=Lfar[:], constant=0.0)
        O("pool", "affine_select", reads=[bconst], writes=[bconst], out=Lfar[:], in_=Lfar[:], pattern=[[-1, 128]],
          compare_op=ALU.is_ge, fill=NEG, base=-1, channel_multiplier=1)
        O("pool", "memset", writes=[bconst], ap=Eall[:], constant=1.0)
        O("pool", "affine_select", reads=[bconst], writes=[bconst], out=Eall[:], in_=Eall[:], pattern=[[1, S]],
          compare_op=ALU.is_ge, fill=0.0, base=0, channel_multiplier=-64)
        O("pool", "affine_select", reads=[bconst], writes=[bconst], out=Eall[:], in_=Eall[:], pattern=[[-1, S]],
          compare_op=ALU.is_ge, fill=0.0, base=63, channel_multiplier=64)
        O("pool", "memset", writes=[bkcT], ap=kcT[0:64, :, :], constant=0.0)
        P.dma("pool", lambda E: E.dma_start(out=kcT[64:68, 0, :], in_=caug_d), writes=[bkcT])
        P.dma("pool", lambda E: E.dma_start(out=kcT[64:68, 1, :], in_=caug_d), writes=[bkcT])
        O("pool", "memset", writes=[bvcx], ap=vcx[:], constant=0.0)
        O("pool", "memset", writes=[bvcx], ap=vcx[:, :, :, 64:65], constant=1.0)
        for g in range(2):
            P.dma("pool", lambda E, g=g: E.dma_start(out=vcx[:, :, g, 65:129], in_=ov_d), writes=[bvcx])
        O("dve", "tensor_scalar", reads=[brv], writes=[bconst], out=qg8[:], in0=rv[:, RV_QG:RV_QG + 64], scalar1=0.125,
          scalar2=None, op0=ALU.mult)

        def make_row(st, name, col0):
            row = st.enter_context(nc.sbuf_tensor(name, [128, D], F32))
            brow = Buf(name)
            Dt = [st.enter_context(nc.sbuf_tensor(f"{name}_D{i}", [128, 128], F32)) for i in range(2)]
            bD = [Buf("D0"), Buf("D1")]
            for c in range(8):
                i = c % 2
                bk = 5 + (c // 4)
                O("dve", "tensor_scalar", reads=[bmod, bconst], writes=[bD[i]], out=Dt[i][:], in0=ident_f[:],
                  scalar1=mod[:, col0 + c:col0 + c + 1], scalar2=None, op0=ALU.mult)
                mmg([dict(out=banks[bk][:, (c % 4) * 128:(c % 4 + 1) * 128], lhsT=ones_f[:], rhs=Dt[i][:], start=True, stop=True)],
                    [bD[i], bconst], [bb[bk]])
                if c % 4 == 3:
                    O("act", "copy", writes=[bb[bk], brow], out=row[:, (c // 4) * 512:(c // 4 + 1) * 512], in_=banks[bk][:])
            return row, brow

        def make_hT(*a, **k):
            for _ in make_hT_g(*a, **k):
                pass

        def make_hT_g(env, src_t, tile0, nsub, acol, bcol, hT, bhT, keep=None, bkeep=None, src_bufs=None, tbank=7):
            for s in range(nsub):
                i = env["rr"] % 2
                env["rr"] += 1
                if keep is None:
                    xs, bxs = env["xs"][i], env["bxs"][i]
                    xs_ap = xs[:]
                else:
                    xs_ap, bxs = keep[:, s, :], bkeep[s]
                rd = [src_bufs[tile0 + s]] if src_bufs is not None else []
                P.dma("sp", lambda E, xs_ap=xs_ap, s=s: E.dma_start(out=xs_ap, in_=src_t[tile0 + s]), reads=rd, writes=[bxs])
                yield
                ss = env["ss"]; bss = env["bss"]
                O("act", "activation", reads=[bxs], writes=[env["bjunk"], bss], out=env["junk"][:], in_=xs_ap, func=AF.Square,
                  accum_out=ss[:, 0:1])
                yield
                O("act", "activation", reads=[bss, bconst], writes=[bss], out=ss[:, 1:2], in_=ss[:, 0:1], func=AF.Ln, scale=1.0 / D, bias=epsT[:, 0:1])
                O("act", "activation", reads=[bss], writes=[bss], out=ss[:, 3:4], in_=ss[:, 1:2], func=AF.Exp, scale=-0.5)
                yield
                xn, bxn = env["xn"][i], env["bxn"][i]
                O("dve", "tensor_scalar", reads=[bxs, bss], writes=[bxn], out=xn[:], in0=xs_ap, scalar1=ss[:, 3:4], scalar2=None,
                  op0=ALU.mult)
                yield
                tp = banks[tbank][:].bitcast(BF16).rearrange("p (c t) -> p c t", t=128)
                for fc in range(8):
                    O("pe", "transpose", reads=[bxn, bconst], writes=[bb[tbank]], signal=(fc == 7), out=tp[:, fc, :],
                      in_=xn[:, fc * 128:(fc + 1) * 128], identity=ident_bf[:])
                yield
                for fc in range(8):
                    O("dve", "tensor_scalar", reads=[bmod], relaxed=[bb[tbank], bhT], out=hT[:, fc, s * 128:(s + 1) * 128], in0=tp[:, fc, :],
                      scalar1=acol[:, fc:fc + 1], scalar2=bcol[:, fc:fc + 1], op0=ALU.mult, op1=ALU.add)

        def rstd_small(ssq, n, width, bufs):
            O("act", "activation", reads=list(bufs) + [bconst], writes=bufs, out=ssq, in_=ssq, func=AF.Ln, scale=1.0 / n, bias=epsT[:, 0:1])
            O("act", "activation", reads=bufs, writes=bufs, out=ssq, in_=ssq, func=AF.Exp, scale=-0.5)

        win_v = win_d.rearrange("(kc p) n -> p kc n", p=128)
        wout_v = wout_d.rearrange("(kc p) n -> p kc n", p=128)

        with ExitStack() as st:
            sb = lambda n, s, d=F32: st.enter_context(nc.sbuf_tensor("ph1_" + n, list(s), d))
            winc = sb("winc", [128, 8, 256], BF16); bwinc = Buf("winc")
            w1 = [sb("w1k", [128, 32, 256], BF16), sb("w1v", [128, 32, 256], BF16)]; bw1 = Buf("w1")
            w2 = [sb("w2k", [128, 2, 64], BF16), sb("w2v", [128, 2, 64], BF16)]; bw2 = Buf("w2")
            posb = sb("posb", [128, 64], BF16); bpos = Buf("pos")
            cbias = sb("cbias", [128, 4]); bcb = Buf("cb")
            X = [sb("xk", [128, 16, 257], BF16), sb("xv", [128, 16, 257], BF16)]; bX = [Buf("xk"), Buf("xv")]
            vcT = sb("vcT", [64, 2, 256]); bvcT = Buf("vcT")
            hT = sb("hT1", [128, 8, TB], BF16); bhT = Buf("hT1")
            env = dict(rr=0, xs=[sb("xs0", [128, D]), sb("xs1", [128, D])], bxs=[Buf("xs0"), Buf("xs1")],
                       xn=[sb("xn0", [128, D], BF16), sb("xn1", [128, D], BF16)], bxn=[Buf("xn0"), Buf("xn1")],
                       junk=sb("junk", [128, D], BF16), bjunk=Buf("junk"), ss=sb("ss", [128, 4]), bss=Buf("ss"))
            xsg = [sb("xsg0", [128, 256]), sb("xsg1", [128, 256])]; x2g = [sb("x2g0", [128, 256]), sb("x2g1", [128, 256])]
            bgl = [Buf("gl0"), Buf("gl1")]
            gl = sb("gl", [128, 2, 2, 2, 256], BF16); bglo = Buf("glo")
            sqk = sb("sqk", [64, 512]); rsk = sb("rsk", [64, 512]); bsqk = Buf("sqk")
            for kv in range(2):
                O("pool", "memset", writes=[bX[kv]], ap=X[kv][:, :, 0:1], constant=0.0)
            P.dma("pool", lambda E: E.dma_start(out=winc[:], in_=win_v[:, :, 512:768]), writes=[bwinc], ndesc=1024)
            wa = [sb(f"wa{i}", [128, 8, 512]) for i in range(2)]
            bwa = [Buf("wa0"), Buf("wa1")]
            sc = sb("siluc", [128, 16]); bsc = Buf("sc")
            O("act", "activation", reads=[bpv], writes=[bsc], out=sc[:], in_=pv[:, PV_CT:PV_CT + 16], func=AF.Silu)
            wada_v = wada_d.rearrange("(kc p) n -> p kc n", p=128)
            mps = banks[0][:, 0:96].rearrange("p (j t) -> p j t", t=2)
            scv = sc[:].rearrange("p (k t) -> p k t", t=2)
            modtok = {}

            def mod_dma(pc):
                i = pc % 2
                modtok[pc] = P.dma("sp", lambda E: E.dma_start(out=wa[i][:], in_=wada_v[:, :, pc * 512:(pc + 1) * 512]), writes=[bwa[i]])

            def mod_mm(pc):
                i = pc % 2
                for cc in range(4):
                    j = pc * 4 + cc
                    mmg([dict(out=mps[:, j, :], lhsT=wa[i][:, kc, cc * 128:(cc + 1) * 128], rhs=scv[:, kc, :],
                              start=(kc == 0), stop=(kc == 7)) for kc in range(8)], [bwa[i], bsc], [bb[0]])
            mod_dma(0); mod_dma(1); mod_mm(0); mod_dma(2); mod_mm(1); mod_dma(3); mod_mm(2); mod_mm(3)
            O("dve", "tensor_tensor", reads=[bpv], writes=[bb[0], bmod], out=mod[:, 0:16], in0=mps[:, 0:16, 0], in1=pv[:, PV_BADA:PV_BADA + 16], op=ALU.add)
            O("dve", "scalar_tensor_tensor", reads=[bmod, bpv], writes=[bmod], out=ab[:, 0:8], in0=mod[:, 8:16], scalar=1.0,
              in1=pv[:, PV_N1G:PV_N1G + 8], op0=ALU.add, op1=ALU.mult)
            P.wait("pool", modtok[3])
            for kv, wd1 in enumerate((w1k_d, w1v_d)):
                src = wd1.rearrange("(l d) n -> d l n", d=64)
                for hf in range(2):
                    P.dma("pool", lambda E, kv=kv, hf=hf, src=src: E.dma_start(out=w1[kv][hf * 64:(hf + 1) * 64, :, :], in_=src), writes=[bw1], ndesc=2048)
            for kv, wd2 in enumerate((w2k_d, w2v_d)):
                P.dma("pool", lambda E, kv=kv, wd2=wd2: E.dma_start(out=w2[kv][:], in_=wd2.rearrange("(h p) n -> p h n", p=128)), writes=[bw2])
            P.dma("pool", lambda E: E.dma_start(out=posb[:], in_=pos_d), writes=[bpos])
            P.dma("pool", lambda E: E.dma_start(out=wA[:], in_=win_v[:, :, 0:512]), writes=[bwin], ndesc=1024)
            for c0 in range(768, NIN, 1036):
                P.dma("pool", lambda E, c0=c0: E.dma_start(out=wB[:, :, c0 - 768:c0 - 768 + 1036], in_=win_v[:, :, c0:c0 + 1036]), writes=[bwin], ndesc=1024)
            P.dma("pool", lambda E: E.dma_start(out=wout[:], in_=wout_v), writes=[bwout], ndesc=1024)
            for kv in range(2):
                for hf in range(2):
                    mmg([dict(out=banks[1][:, kv * 2 + hf:kv * 2 + hf + 1], lhsT=w1[kv][0:64, l, hf * 128:(hf + 1) * 128],
                              rhs=posb[0:64, kv * 32 + l:kv * 32 + l + 1], start=(l == 0), stop=(l == 31)) for l in range(32)],
                        [bw1, bpos], [bb[1]])
            O("dve", "tensor_copy", writes=[bb[1], bcb], out=cbias[:], in_=banks[1][:, 0:4])


            for tb in range(NTB):
                mod_dma(4 + tb)
                make_hT(env, x_t, tb * 4, 4, ab[:, 0:8], mod[:, 0:8], hT, bhT)
                for kv in range(2):
                    bk = 5 + kv
                    mmg([dict(out=banks[bk][:], lhsT=winc[:, kc, kv * 128:(kv + 1) * 128], rhs=hT[:, kc, :], start=(kc == 0), stop=(kc == 7))
                         for kc in range(8)], [bwinc, bhT], [bb[bk]])
                    O("act", "copy", writes=[bb[bk], bX[kv]], out=X[kv][:, :, 1 + tb * 32:1 + (tb + 1) * 32].rearrange("p r m -> p m r"),
                      in_=banks[bk][:].rearrange("p (m r) -> p m r", r=16))
                if tb >= 1:
                    mod_mm(4 + tb - 1)
            mod_mm(11)
            O("dve", "tensor_tensor", reads=[bpv], writes=[bb[0], bmod], out=mod[:, 16:48], in0=mps[:, 16:48, 0], in1=pv[:, PV_BADA + 16:PV_BADA + 48], op=ALU.add)
            O("dve", "scalar_tensor_tensor", reads=[bmod, bpv], writes=[bmod], out=ab[:, 8:16], in0=mod[:, 32:40], scalar=1.0,
              in1=pv[:, PV_N2G:PV_N2G + 8], op0=ALU.add, op1=ALU.mult)
            dbg_out("d_mod", mod[:], bmod, eng="sp")
            hpg = [[banks[2 * g + kv][:].rearrange("p (b n) -> p b n", b=2) for kv in range(2)] for g in range(2)]
            for kv in range(2):
                for g in range(2):
                    mm = []
                    for hf in range(2):
                        for l in range(32):
                            mm.append(dict(out=hpg[g][kv][:, hf, :], lhsT=w1[kv][g * 64:(g + 1) * 64, l, hf * 128:(hf + 1) * 128],
                                           rhs=X[kv][g * 64:(g + 1) * 64, l % 16, l // 16:l // 16 + 256], start=(l == 0), stop=(l == 31)))
                    mmg(mm, [bw1, bX[kv]], [bb[2 * g + kv]])
            rr_ = 0
            for kv in range(2):
                for g in range(2):
                    for hf in range(2):
                        ci = kv * 2 + hf
                        i = rr_ % 2
                        rr_ += 1
                        bk = 2 * g + kv
                        O("act", "activation", reads=[bcb], writes=[bb[bk], bgl[i]], out=xsg[i][:], in_=hpg[g][kv][:, hf, :],
                          func=AF.Identity, bias=cbias[:, ci:ci + 1], scale=1.0)
                        O("dve", "tensor_tensor", writes=[bgl[i]], out=x2g[i][:], in0=xsg[i][:], in1=xsg[i][:], op=ALU.mult)
                        O("dve", "tensor_scalar", writes=[bgl[i]], out=x2g[i][:], in0=x2g[i][:], scalar1=0.044715, scalar2=1.0, op0=ALU.mult, op1=ALU.add)
                        O("dve", "tensor_tensor", writes=[bgl[i]], out=x2g[i][:], in0=x2g[i][:], in1=xsg[i][:], op=ALU.mult)
                        O("act", "activation", writes=[bgl[i]], out=x2g[i][:], in_=x2g[i][:], func=AF.Sigmoid, scale=1.5957691216057308)
                        O("dve", "tensor_tensor", reads=[bgl[i]], writes=[bglo], out=gl[:, kv, hf, g, :], in0=xsg[i][:], in1=x2g[i][:], op=ALU.mult)
            ops_ = [banks[4 + kv][0:64, :].rearrange("p (g n) -> p g n", g=2) for kv in range(2)]
            for kv in range(2):
                mm = []
                for g in range(2):
                    for hf in range(2):
                        mm.append(dict(out=ops_[kv][:, g, :], lhsT=w2[kv][:, hf, :], rhs=gl[:, kv, hf, g, :], start=(hf == 0), stop=(hf == 1)))
                mmg(mm, [bw2, bglo], [bb[4 + kv]])
            O("act", "activation", writes=[bb[4], bsqk], out=sqk[:], in_=banks[4][0:64, :], func=AF.Square)
            mmg([dict(out=banks[6][0:64, :], lhsT=ones_f[0:64, 0:64], rhs=sqk[:], start=True, stop=True)], [bsqk, bconst], [bb[6]])
            O("act", "activation", reads=[bconst], writes=[bb[6], bsqk], out=rsk[:], in_=banks[6][0:64, :], func=AF.Ln, scale=1.0 / 64, bias=epsT[0:64, 0:1])
            O("act", "activation", writes=[bsqk], out=rsk[:], in_=rsk[:], func=AF.Exp, scale=-0.5)
            O("dve", "scalar_tensor_tensor", reads=[bsqk, bpv], writes=[bb[4], bkcT], out=kcT[0:64, :, :],
              in0=ops_[0], scalar=pv[0:64, PV_KCG:PV_KCG + 1], in1=rsk[:].rearrange("p (g n) -> p g n", g=2),
              op0=ALU.mult, op1=ALU.mult)
            O("act", "copy", writes=[bb[5], bvcT], out=vcT[:], in_=ops_[1])
            for tl in range(2):
                for g in range(2):
                    O("pe", "transpose", reads=[bvcT, bconst], writes=[bb[7]], signal=(tl == 1 and g == 1), out=banks[7][:, (tl * 2 + g) * 64:(tl * 2 + g + 1) * 64],
                      in_=vcT[:, g, tl * 128:(tl + 1) * 128], identity=ident_f[0:64, 0:64])
            O("dve", "tensor_copy", writes=[bb[7], bvcx], out=vcx[:, :, :, 0:64],
              in_=banks[7][:, 0:256].rearrange("p (t g d) -> p t g d", t=2, g=2))
            dbg_out("d_kcT", kcT[:].rearrange("p g n -> p (g n)"), bkcT)
            dbg_out("d_vcx", vcx[:].rearrange("p t g n -> p (t g n)"), bvcx)
            P.run()

        with ExitStack() as st:
            sb = lambda n, s, d=F32: st.enter_context(nc.sbuf_tensor("ph2_" + n, list(s), d))
            g1row, bg1 = make_row(st, "g1row", 16)
            ksT = sb("ksT", [68, 2, S], BF16); bks = [Buf(f"ksT{i}") for i in range(NTB)]; bksaug = Buf("ksaug")
            KWR = 1536
            kwT = sb("kwT", [68, 2, KWR], BF16); bkw = [Buf(f"kwT{i}") for i in range(3)]
            vsa = sb("vsa", [128, 32, 2, 65], BF16); bvs = [Buf(f"vsa{i}") for i in range(NTB)]; bvs1 = Buf("vs1")
            vwa = sb("vwa", [128, 12, 2, 65], BF16); bvw = [Buf(f"vwa{i}") for i in range(3)]; bvw1 = Buf("vw1")
            qTs = [sb(f"qT{i}", [68, 8, TB], BF16) for i in range(2)]; bqTs = [Buf("qT0"), Buf("qT1")]
            hT = sb("hT2", [128, 8, TB], BF16); bhT = Buf("hT2")
            mixa = sb("mixa", [128, 4, TB], BF16); bmixa = Buf("mixa")
            mixcs = [sb(f"mixc{i}", [128, 4, TB], BF16) for i in range(2)]; bmixcs = [Buf("mixc0"), Buf("mixc1")]
            env = dict(rr=0, xs=[sb("xs0", [128, D]), sb("xs1", [128, D])], bxs=[Buf("xs0"), Buf("xs1")],
                       xn=[sb("xn0", [128, D], BF16), sb("xn1", [128, D], BF16)], bxn=[Buf("xn0"), Buf("xn1")],
                       junk=sb("junk", [128, D], BF16), bjunk=Buf("junk"), ss=sb("ss", [128, 4]), bss=Buf("ss"))
            sqt = sb("sqt", [128, 512]); bsqt = Buf("sqt")
            ssq = sb("ssq", [128, 16]); bssq = Buf("ssq")
            ssq2 = sb("ssq2", [128, 16]); bssq2 = Buf("ssq2")
            sqk2 = sb("sqk2", [128, 256]); bsqk2 = Buf("sqk2")
            ssqk = sb("ssqk", [128, 4]); bssqk = Buf("ssqk")
            qnb = sb("qnb", [128, 512], BF16); bqnb = Buf("qnb")
            knb = sb("knb", [128, 256], BF16); bknb = Buf("knb")
            gatess = [sb(f"gates{i}", [128, 4, 24]) for i in range(2)]; bgatess = [Buf("gates0"), Buf("gates1")]
            xin_sb = sb("xin_sb", [128, TB]); u = sb("u", [128, TB + 2]); t1 = sb("t1", [128, TB]); cv = sb("cv", [128, TB])
            sqb = sb("sqb", [128, TB], BF16); uhalo = sb("uhalo", [128, 4, 2])
            bxin, bu, bt1, bcv, bsqb, buh = [Buf(n) for n in ("xin", "u", "t1", "cv", "sqb", "uh")]
            NPT = 3
            PT = [sb(f"PT{i}", [128, TB], BF16) for i in range(NPT)]; bPT = [Buf(f"PT{i}") for i in range(NPT)]
            scr = sb("scr", [128, 1024]); bscr = Buf("scr")
            Uraw = scr[:].rearrange("p (s h j) -> p s h j", s=4, h=4); bUraw = bscr
            acc = sb("acc", [128, 4, 4, 64]); bacc = Buf("acc")
            rw = sb("rw", [128, 16]); brw = Buf("rw")
            imp = sb("imp", [128, 4, 64]); bimp = Buf("imp")
            wk = sb("wk", [128, 64]); m8 = sb("m8", [128, 16]); btk = Buf("tk")
            selb = sb("selb", [128, 4, 64], BF16); bselb = Buf("selb")
            selbT = [sb(f"selbT{i}", [128, TB], BF16) for i in range(2)]; bselbT = [Buf("selbT0"), Buf("selbT1")]
            for i_ in range(2):
                O("pool", "memset", writes=[bselbT[i_]], ap=selbT[i_][:], constant=0.0)
            attnb = sb("attnb", [128, 4, 512], BF16); battnb = Buf("attnb")
            xr = [sb("xr0", [128, D]), sb("xr1", [128, D])]; bxr = [Buf("xr0"), Buf("xr1")]
            tmpo = [sqt, xin_sb]; btmpo = [bsqt, bxin]

            for g in range(2):
                P.dma("pool", lambda E, g=g: E.dma_start(out=ksT[64:68, g, :], in_=kaug_d), writes=[bksaug])
            O("pool", "memset", writes=[bvs1], ap=vsa[:, :, :, 64:65], constant=1.0)
            O("pool", "memset", writes=[bvw1], ap=vwa[:, :, :, 64:65], constant=1.0)
            O("pool", "memset", writes=[buh], ap=uhalo[:], constant=0.0)
            CB_KV, CB_GL, CB_GB, CB_GC, CB_XI = 0, 1280 - 768, 1304 - 768, 1816 - 768, 2328 - 768
            rot = dict(s=0, pt=0)
            pend = dict(gen=None)

            def tick(n=1):
                for _ in range(n):
                    if pend["gen"] is not None:
                        try:
                            next(pend["gen"])
                        except StopIteration:
                            pend["gen"] = None

            def drain():
                while pend["gen"] is not None:
                    tick()

            def frontend(tb):
                t0 = tb * TB
                qT, bqT = qTs[tb % 2], bqTs[tb % 2]
                gates, bgates = gatess[tb % 2], bgatess[tb % 2]
                mixc, bmixc = mixcs[tb % 2], bmixcs[tb % 2]
                kwi = tb % 3
                P.dma("pool", lambda E: E.dma_start(out=qT[64:68, :, :], in_=qaug_d[:, :, t0:t0 + TB]), writes=[bqT])
                r0 = kwi * TB
                for g in range(2):
                    P.dma("pool", lambda E, g=g: E.dma_start(out=kwT[64:68, g, r0:r0 + TB], in_=kaug_d[:, t0:t0 + TB]), writes=[bkw[kwi]])
                def q_g(s):
                    ti = tb * 4 + s
                    hs = lambda kc: hT[:, kc, s * 128:(s + 1) * 128]
                    mmg([dict(out=banks[5][:], lhsT=hs(kc), rhs=wA[:, kc, :], start=(kc == 0), stop=(kc == 7)) for kc in range(8)],
                        [bhT, bwin], [bb[5]])
                    yield
                    O("act", "activation", writes=[bb[5], bsqt], out=sqt[:], in_=banks[5][:], func=AF.Square)
                    yield
                    O("dve", "tensor_reduce", reads=[bsqt], writes=[bssq], out=ssq[:, 0:8], in_=sqt[:].rearrange("p (h d) -> p h d", d=64),
                      axis=AX.X, op=ALU.add)
                    yield
                    rstd_small(ssq[:, 0:8], 64, 8, [bssq])
                    yield
                    O("dve", "tensor_tensor", reads=[bssq], writes=[bb[5], bsqt], out=sqt[:].rearrange("p (h d) -> p h d", d=64),
                      in0=banks[5][:].rearrange("p (h d) -> p h d", d=64), in1=bc(ssq[:, 0:8], 64), op=ALU.mult)
                    O("dve", "tensor_tensor", reads=[bsqt, bconst], writes=[bqnb], out=qnb[:].rearrange("p (h d) -> p h d", d=64),
                      in0=sqt[:].rearrange("p (h d) -> p h d", d=64),
                      in1=qg8[:].unsqueeze(1).broadcast_to([128, 8, 64]), op=ALU.mult)
                    yield
                    tp = banks[5][0:64, :].bitcast(BF16).rearrange("p (h t) -> p h t", t=128)
                    for h in range(8):
                        O("pe", "transpose", reads=[bqnb, bconst], writes=[bb[5]], signal=(h == 7), out=tp[:, h, :],
                          in_=qnb[:, h * 64:(h + 1) * 64], identity=ident_bf[:])
                    yield
                    O("act", "copy", writes=[bb[5], bqT], out=qT[0:64, :, s * 128:(s + 1) * 128], in_=tp)
                    yield

                def kv_g(s):
                    ti = tb * 4 + s
                    hs = lambda kc: hT[:, kc, s * 128:(s + 1) * 128]
                    mmg([dict(out=banks[6][:], lhsT=hs(kc), rhs=wB[:, kc, CB_KV:CB_KV + 512], start=(kc == 0), stop=(kc == 7)) for kc in range(8)],
                        [bhT, bwin], [bb[6]])
                    yield
                    kview = banks[6][:].rearrange("p (a b c) -> p a b c", a=2, b=2)[:, :, 0, :]
                    O("act", "activation", writes=[bb[6], bsqk2], out=sqk2[:].rearrange("p (a c) -> p a c", a=2), in_=kview, func=AF.Square)
                    O("dve", "tensor_copy", reads=[bvs1], writes=[bb[6]], relaxed=[bvs[tb]], out=vsa[:, ti, :, 0:64],
                      in_=banks[6][:, 128:256].rearrange("p (g d) -> p g d", g=2))
                    O("dve", "tensor_copy", reads=[bvw1], writes=[bb[6]], relaxed=[bvw[kwi]], out=vwa[:, kwi * 4 + s, :, 0:64],
                      in_=banks[6][:, 384:512].rearrange("p (g d) -> p g d", g=2))
                    yield
                    O("dve", "tensor_reduce", reads=[bsqk2], writes=[bssqk], out=ssqk[:, 0:4], in_=sqk2[:].rearrange("p (h d) -> p h d", d=64),
                      axis=AX.X, op=ALU.add)
                    yield
                    rstd_small(ssqk[:, 0:4], 64, 4, [bssqk])
                    yield
                    O("dve", "tensor_tensor", reads=[bssqk], writes=[bb[6], bsqk2], out=sqk2[:].rearrange("p (a g d) -> p a g d", a=2, g=2),
                      in0=kview.rearrange("p a (g d) -> p a g d", g=2),
                      in1=bc(ssqk[:, 0:4].rearrange("p (a g) -> p a g", a=2), 64), op=ALU.mult)
                    O("dve", "tensor_tensor", reads=[bsqk2, brv], writes=[bknb], out=knb[:].rearrange("p (a g d) -> p a g d", a=2, g=2),
                      in0=sqk2[:].rearrange("p (a g d) -> p a g d", a=2, g=2),
                      in1=rv[:, RV_KSG:RV_KSG + 128].rearrange("p (a d) -> p a d", a=2).unsqueeze(2).broadcast_to([128, 2, 2, 64]), op=ALU.mult)
                    yield
                    tp2 = banks[6][0:64, :].bitcast(BF16).rearrange("p (h t) -> p h t", t=128)
                    for j in range(4):
                        O("pe", "transpose", reads=[bknb, bconst], writes=[bb[6]], signal=(j == 3), out=tp2[:, j, :],
                          in_=knb[:, j * 64:(j + 1) * 64], identity=ident_bf[:])
                    yield
                    O("act", "copy", writes=[bb[6]], relaxed=[bks[tb]], out=ksT[0:64, :, ti * 128:(ti + 1) * 128], in_=tp2[:, 0:2, :])
                    rr0 = kwi * TB + s * 128
                    O("act", "copy", writes=[bb[6]], relaxed=[bkw[kwi]], out=kwT[0:64, :, rr0:rr0 + 128], in_=tp2[:, 2:4, :])
                    yield
                    mmg([dict(out=banks[6][:, 0:24], lhsT=hs(kc), rhs=wB[:, kc, CB_GL:CB_GL + 24], start=(kc == 0), stop=(kc == 7)) for kc in range(8)],
                        [bhT, bwin], [bb[6]])
                    yield
                    O("dve", "tensor_tensor", reads=[brv], writes=[bb[6], bgates], out=gates[:, s, :], in0=banks[6][:, 0:24], in1=rv[:, RV_BG:RV_BG + 24], op=ALU.add)
                    yield
                    O("act", "activation", writes=[bgates], out=gates[:, s, :], in_=gates[:, s, :], func=AF.Exp, scale=-1.0)
                    yield
                    O("dve", "tensor_scalar", writes=[bgates], out=gates[:, s, :], in0=gates[:, s, :], scalar1=1.0, scalar2=None, op0=ALU.add)
                    O("dve", "reciprocal", writes=[bgates], out=gates[:, s, :], in_=gates[:, s, :])
                    yield

                def rr(gens):
                    gens = list(gens)
                    while gens:
                        for g_ in list(gens):
                            try:
                                next(g_)
                            except StopIteration:
                                gens.remove(g_)
                        yield

                mk = lambda s: make_hT_g(env, x_t, tb * 4 + s, 1, ab[:, 0:8], mod[:, 0:8], hT[:, :, s * 128:(s + 1) * 128], bhT)
                yield from mk(0)
                for s in range(4):
                    yield from rr([q_g(s), kv_g(s)] + ([mk(s + 1)] if s < 3 else []))
                for c in range(4):
                    for (bk, cb) in ((5, CB_XI), (6, CB_GC), (7, CB_GB)):
                        mmg([dict(out=banks[bk][:], lhsT=wB[:, kc, cb + c * 128:cb + (c + 1) * 128], rhs=hT[:, kc, :], start=(kc == 0), stop=(kc == 7))
                             for kc in range(8)], [bhT, bwin], [bb[bk]])
                    yield
                    O("act", "copy", writes=[bb[5], bxin], out=xin_sb[:], in_=banks[5][:])
                    O("pool", "tensor_copy", reads=[buh], writes=[bu], out=u[:, 0:2], in_=uhalo[:, c, :])
                    yield
                    O("dve", "tensor_tensor", reads=[bxin], writes=[bb[6], bu], out=u[:, 2:TB + 2], in0=banks[6][:], in1=xin_sb[:], op=ALU.mult)
                    O("pool", "tensor_copy", reads=[bu], writes=[buh], out=uhalo[:, c, :], in_=u[:, TB:TB + 2])
                    cw = lambda j, c=c: pv[:, PV_CMW + c * 3 + j:PV_CMW + c * 3 + j + 1]
                    O("dve", "tensor_scalar", reads=[bu, bpv], writes=[bt1], out=t1[:], in0=u[:, 0:TB], scalar1=cw(0), scalar2=None, op0=ALU.mult)
                    O("dve", "scalar_tensor_tensor", reads=[bu, bpv], writes=[bt1], out=t1[:], in0=u[:, 1:TB + 1], scalar=cw(1), in1=t1[:], op0=ALU.mult, op1=ALU.add)
                    O("dve", "scalar_tensor_tensor", reads=[bu, bpv], writes=[bt1], out=t1[:], in0=u[:, 2:TB + 2], scalar=cw(2), in1=t1[:], op0=ALU.mult, op1=ALU.add)
                    O("dve", "tensor_tensor", reads=[bt1], writes=[bb[7], bcv], out=cv[:], in0=banks[7][:], in1=t1[:], op=ALU.mult)
                    yield
                    O("act", "activation", reads=[bcv], writes=[bsqb], out=sqb[:], in_=cv[:], func=AF.Square)
                    yield
                    mmg([dict(out=banks[5][:], lhsT=G64[:], rhs=sqb[:], start=True, stop=True)], [bsqb, bconst], [bb[5]])
                    yield
                    O("act", "activation", reads=[bconst], writes=[bb[5], bt1], out=t1[:], in_=banks[5][:], func=AF.Ln, scale=1.0 / 64, bias=epsT[:, 0:1])
                    O("act", "activation", writes=[bt1], out=t1[:], in_=t1[:], func=AF.Exp, scale=-0.5)
                    yield
                    O("dve", "scalar_tensor_tensor", reads=[bcv, bt1, bpv], writes=[bmixc], out=mixc[:, c, :], in0=cv[:],
                      scalar=pv[:, PV_COG + c:PV_COG + c + 1], in1=t1[:], op0=ALU.mult, op1=ALU.mult)
                    yield
                if debug and tb == 0:
                    dbg_out("d_hT", hT[:].rearrange("p c t -> p (c t)"), bhT, eng="pool")
                    dbg_out("d_qT", qT[:].rearrange("p h t -> p (h t)"), bqT, eng="pool")
                    dbg_out("d_gates", gates[:].rearrange("p s c -> p (s c)"), bgates, eng="pool")
                    dbg_out("d_ksT", ksT[:, :, 0:TB], bks[0], eng="pool", dst=dbg["d_ksT"].rearrange("p (g t) -> p g t", g=2))

            def attn_items(items, obank, qT, bqT, h):
                first = [True]
                n = len(items)
                LOOK = 2
                sb_of = {}

                def emit_S(ix):
                    it = items[ix]
                    bk = rot["s"] % 3
                    rot["s"] += 1
                    sb_of[ix] = bk
                    c0, c1 = it["c0"], it["c1"]
                    mm = [dict(out=banks[bk][:, c0:c1], lhsT=it["kT"], rhs=qT[0:68, h, c0:c1], start=True, stop=False, skip=True)]
                    for (lt, rh, e0, e1) in it["extra"]:
                        mm.append(dict(out=banks[bk][:, e0:e1], lhsT=lt, rhs=rh, start=False, stop=False, skip=True))
                    mm[-1]["stop"] = True
                    mmg(mm, it["reads"] + [bqT, bconst], [bb[bk]])

                for ix in range(min(LOOK, n)):
                    emit_S(ix)
                for ix in range(n):
                    if ix + LOOK < n:
                        emit_S(ix + LOOK)
                    it = items[ix]
                    bk = sb_of[ix]
                    c0, c1 = it["c0"], it["c1"]
                    pi = rot["pt"] % NPT
                    rot["pt"] += 1
                    O("act", "activation", writes=[bb[bk], bPT[pi]], out=PT[pi][:, c0:c1], in_=banks[bk][:, c0:c1], func=AF.Exp)
                    mm = []
                    for s in range(c0 // 128, c1 // 128):
                        mm.append(dict(out=it["oview"](s), lhsT=PT[pi][:, s * 128:(s + 1) * 128], rhs=it["vrhs"],
                                       start=first[0], stop=(ix == n - 1), skip=True))
                        first[0] = False
                    mmg(mm, [bPT[pi]] + it["vreads"], [bb[obank]])
                    tick(pend.get("k", 1))

            def tail_g(tb):
                mixc, bmixc = mixcs[tb % 2], bmixcs[tb % 2]
                pend["tail"] = True
                for c in range(4):
                    tp4 = banks[7][:].bitcast(BF16)[:, 0:512]
                    for s in range(4):
                        O("pe", "transpose", reads=[battnb, bconst], writes=[bb[7]], signal=(s == 3), out=tp4[:, s * 128:(s + 1) * 128],
                          in_=attnb[:, s, c * 128:(c + 1) * 128], identity=ident_bf[:])
                    yield
                    O("act", "copy", writes=[bb[7], bmixa], out=mixa[:, c, :], in_=tp4)
                    yield
                for s in range(4):
                    ti = tb * 4 + s
                    i = ti % 2
                    P.dma("sp", lambda E, i=i, ti=ti: E.dma_start(out=xr[i][:], in_=x_t[ti]), writes=[bxr[i]])
                    for hf in range(2):
                        bk = 5 + hf
                        mm = []
                        for kc in range(8):
                            lt = mixa[:, kc, s * 128:(s + 1) * 128] if kc < 4 else mixc[:, kc - 4, s * 128:(s + 1) * 128]
                            mm.append(dict(out=banks[bk][:], lhsT=lt, rhs=wout[:, kc, hf * 512:(hf + 1) * 512], start=(kc == 0), stop=(kc == 7)))
                        mmg(mm, [bmixa, bmixc, bwout], [bb[bk]])
                    yield
                    for hf in range(2):
                        bk = 5 + hf
                        O("dve", "tensor_tensor", reads=[bg1], writes=[bb[bk], btmpo[hf]], out=tmpo[hf][:], in0=banks[bk][:], in1=g1row[:, hf * 512:(hf + 1) * 512], op=ALU.mult)
                    yield
                    for hf in range(2):
                        O("pool", "tensor_tensor", reads=[btmpo[hf]], writes=[bxr[i]], out=xr[i][:, hf * 512:(hf + 1) * 512], in0=xr[i][:, hf * 512:(hf + 1) * 512],
                          in1=tmpo[hf][:], op=ALU.add)
                    P.dma("sp", lambda E, i=i, ti=ti: E.dma_start(out=out_t[ti], in_=xr[i][:]), reads=[bxr[i]], writes=[obuf[ti]])
                    if s == 3:
                        pend["tail"] = False
                    yield

            def chain(*gens):
                for g_ in gens:
                    if g_ is not None:
                        yield from g_

            if N2 > 0:
                pend["gen"] = frontend(0)
                drain()
            for tb in range(N2):
                qT, bqT = qTs[tb % 2], bqTs[tb % 2]
                gates, bgates = gatess[tb % 2], bgatess[tb % 2]
                mixc, bmixc = mixcs[tb % 2], bmixcs[tb % 2]
                if tb + 1 < N2 or tb >= 1:
                    pend["gen"] = chain(tail_g(tb - 1) if (tb >= 1 and not debug) else None, frontend(tb + 1) if tb + 1 < N2 else None)
                    nticks = 8 * ((4 * tb + 4) + (4 if tb == 0 else 8)) + 8
                    pend["k"] = max(1, -(-165 // nticks))
                for g in range(2):
                    ctiles = [0] if tb < 4 else [0, 1]
                    for hi in range(4):
                        h = g * 4 + hi
                        pbase = 1 if hi % 2 == 0 else 3
                        ptl = {}
                        for tl in ctiles:
                            bk = 0
                            mm = [dict(out=banks[bk][:], lhsT=kcT[0:68, g, tl * 128:(tl + 1) * 128], rhs=qT[0:68, h, :], start=True, stop=False, skip=True)]
                            dl = 32 * tb - 128 * tl
                            if 16 * dl < 2048:
                                mm.append(dict(out=banks[bk][:], lhsT=ident_bf[:], rhs=Lc[:, 16 * dl:16 * dl + 512], start=False, stop=False, skip=True))
                            mm[-1]["stop"] = True
                            mmg(mm, [bkcT, bqT, bconst], [bb[bk]])
                            pi = rot["pt"] % NPT
                            rot["pt"] += 1
                            ptl[tl] = pi
                            O("act", "activation", writes=[bb[bk], bPT[pi]], out=PT[pi][:], in_=banks[bk][:], func=AF.Exp)
                        for half in range(2):
                            ob = pbase + half
                            mm = []
                            for s2 in range(2):
                                s = half * 2 + s2
                                for k_, tl in enumerate(ctiles):
                                    mm.append(dict(out=banks[ob][:, s2 * 256:s2 * 256 + 129], lhsT=PT[ptl[tl]][:, s * 128:(s + 1) * 128],
                                                   rhs=vcx[:, tl, g, :], start=(k_ == 0), stop=(k_ == len(ctiles) - 1)))
                            mmg(mm, [bPT[ptl[tl]] for tl in ctiles] + [bvcx], [bb[ob]])
                        for half in range(2):
                            ob = pbase + half
                            ov = banks[ob][:].rearrange("p (s c) -> p s c", s=2)
                            sl = slice(half * 2, half * 2 + 2)
                            r0_ = half * 2
                            if tb == 0:
                                O("dve", "tensor_scalar", writes=[bb[ob], brw], out=rw[:, r0_:r0_ + 2], in0=ov[:, :, 64], scalar1=1e-30, scalar2=None, op0=ALU.max)
                                O("dve", "reciprocal", writes=[brw], out=rw[:, r0_:r0_ + 2], in_=rw[:, r0_:r0_ + 2])
                            else:
                                O("dve", "reciprocal", writes=[bb[ob], brw], out=rw[:, r0_:r0_ + 2], in_=ov[:, :, 64])
                            O("dve", "tensor_tensor", reads=[bgates], writes=[brw], out=rw[:, 12 + r0_:14 + r0_], in0=rw[:, r0_:r0_ + 2], in1=gates[:, sl, h * 3 + 0], op=ALU.mult)
                            O("dve", "tensor_tensor", reads=[brw], writes=[bb[ob], bacc], out=acc[:, sl, hi, :], in0=ov[:, :, 0:64], in1=bc(rw[:, 12 + r0_:14 + r0_], 64), op=ALU.mult)
                            O("dve", "tensor_tensor", reads=[brw], writes=[bb[ob], bUraw], out=Uraw[:, sl, hi, :], in0=ov[:, :, 65:129], in1=bc(rw[:, r0_:r0_ + 2], 64), op=ALU.mult)
                        tick(pend.get("k", 1))
                    O("dve", "tensor_reduce", reads=[bUraw], writes=[bimp], out=imp[:], in_=Uraw.rearrange("p s h j -> p s j h"), axis=AX.X, op=ALU.add)
                    if debug and tb == 1 and g == 0:
                        dbg_out("d_imp", imp[:].rearrange("p s j -> p (s j)"), bimp, eng="pool")
                    for s in range(4):
                        for hf in range(2):
                            cur = tb * 8 + s * 2 + hf
                            rows = slice(hf * 64, hf * 64 + 64)
                            if cur + 1 < 64:
                                O("pool", "memset", writes=[bimp], ap=imp[rows, s, cur + 1:64], constant=-1.0)
                            O("pool", "memset", writes=[bimp], ap=imp[rows, s, max(cur - 1, 0):cur + 1], constant=1.0e4)
                            O("pool", "memset", writes=[bimp], ap=imp[rows, s, 0:1], constant=1.0e4)
                    for s in range(4):
                        O("dve", "max", reads=[bimp], writes=[btk], out=m8[:, 0:8], in_=imp[:, s, :])
                        O("dve", "match_replace", reads=[bimp, btk], writes=[btk], out=wk[:], in_to_replace=m8[:, 0:8], in_values=imp[:, s, :], imm_value=-1.0e30)
                        O("dve", "max", reads=[btk], writes=[btk], out=m8[:, 8:16], in_=wk[:])
                        O("dve", "tensor_scalar", reads=[bimp, btk], writes=[bselb], out=selb[:, s, :], in0=imp[:, s, :], scalar1=m8[:, 15:16], scalar2=NEG,
                          op0=ALU.is_lt, op1=ALU.mult)
                    if debug and tb == 1 and g == 0:
                        dbg_out("d_selb", selb[:].rearrange("p s j -> p (s j)"), bselb, eng="pool")
                    for hi in range(4):
                        h = g * 4 + hi
                        for br in (2, 1):
                            ob = 3 if br == 1 else 4
                            ovw = banks[ob][:].rearrange("p (s c) -> p s c", s=4)
                            oview = lambda s, ovw=ovw: ovw[:, s, 0:65]
                            items = []
                            if br == 1:
                                for kt in range(4 * tb + 4):
                                    smin = max(0, kt - 4 * tb)
                                    c0 = smin * 128
                                    extra = [(Eall[:, kt * 128:(kt + 1) * 128], selbT[g][:, c0:TB], c0, TB)]
                                    if kt >= 4 * tb:
                                        extra.append((ident_bf[:], Ldiag[:], c0, c0 + 128))
                                    items.append(dict(kT=ksT[0:68, g, kt * 128:(kt + 1) * 128], c0=c0, c1=TB, extra=extra, oview=oview,
                                                      vrhs=vsa[:, kt, g, :], reads=[bks[kt // 4], bksaug, bselbT[g]], vreads=[bvs[kt // 4], bvs1]))
                            else:
                                for m in range(8):
                                    kta = 4 * tb - 4 + m
                                    if kta < 0:
                                        continue
                                    slo, shi = max(0, m - 4), min(3, m)
                                    extra = []
                                    if m >= 4:
                                        extra.append((ident_bf[:], Ldiag[:], (m - 4) * 128, (m - 4) * 128 + 128))
                                    if m <= 3:
                                        extra.append((ident_bf[:], Lfar[:], m * 128, m * 128 + 128))
                                    ri = (kta // 4) % 3
                                    kr = ri * TB + (kta % 4) * 128
                                    items.append(dict(kT=kwT[0:68, g, kr:kr + 128], c0=slo * 128, c1=(shi + 1) * 128, extra=extra, oview=oview,
                                                      vrhs=vwa[:, ri * 4 + kta % 4, g, :], reads=[bkw[ri]], vreads=[bvw[ri], bvw1]))
                            if br == 1 and hi == 0:
                                tbk = rot["s"] % 3
                                rot["s"] += 1
                                tp3 = banks[tbk][0:64, 0:256].bitcast(BF16)
                                for s_ in range(4):
                                    O("pe", "transpose", reads=[bselb, bconst], writes=[bb[tbk]], signal=(s_ == 3), out=tp3[:, s_ * 128:(s_ + 1) * 128],
                                      in_=selb[:, s_, :], identity=ident_bf[:])
                                O("act", "copy", writes=[bb[tbk], bselbT[g]], out=selbT[g][0:64, :], in_=tp3)
                            attn_items(items, ob, qT, bqT, h)
                            O("dve", "reciprocal", writes=[bb[ob], brw], out=rw[:, 4:8], in_=ovw[:, :, 64])
                            O("dve", "tensor_tensor", reads=[bgates], writes=[brw], out=rw[:, 8:12], in0=rw[:, 4:8], in1=gates[:, :, h * 3 + br], op=ALU.mult)
                            O("dve", "tensor_tensor", reads=[brw], writes=[bb[ob], bscr], out=scr[:, 0:256].rearrange("p (s d) -> p s d", s=4),
                              in0=ovw[:, :, 0:64], in1=bc(rw[:, 8:12], 64), op=ALU.mult)
                            O("pool", "tensor_tensor", reads=[bscr], writes=[bacc], out=acc[:, :, hi, :], in0=acc[:, :, hi, :],
                              in1=scr[:, 0:256].rearrange("p (s d) -> p s d", s=4), op=ALU.add)
                    if debug and tb == 1 and g == 0:
                        dbg_out("d_acc", acc[:].rearrange("p s h d -> p (s h d)"), bacc, eng="pool")
                    while pend.get("tail"):
                        tick()
                    accf = acc[:].rearrange("p s h d -> p (s h d)")
                    O("dve", "tensor_tensor", reads=[bacc], writes=[bscr], out=scr[:], in0=accf, in1=accf, op=ALU.mult)
                    O("dve", "tensor_reduce", reads=[bscr], writes=[bssq2], out=ssq2[:, 0:16], in_=scr[:].rearrange("p (a d) -> p a d", d=64), axis=AX.X, op=ALU.add)
                    rstd_small(ssq2[:, 0:16], 64, 16, [bssq2])
                    O("dve", "tensor_tensor", reads=[bacc, bssq2], writes=[bscr], out=scr[:].rearrange("p (a d) -> p a d", d=64),
                      in0=acc[:].rearrange("p s h d -> p (s h) d"), in1=bc(ssq2[:, 0:16], 64), op=ALU.mult)
                    O("dve", "tensor_tensor", reads=[bscr, brv], writes=[battnb], out=attnb[:, :, g * 256:(g + 1) * 256],
                      in0=scr[:].rearrange("p (s c) -> p s c", s=4),
                      in1=rv[:, RV_AOG + g * 256:RV_AOG + (g + 1) * 256].unsqueeze(1).broadcast_to([128, 4, 256]), op=ALU.mult)
                drain()
                if debug and tb == 1:
                    dbg_out("d_attnb", attnb[:].rearrange("p s c -> p (s c)"), battnb, eng="pool")
                if debug:
                    pend["gen"] = tail_g(tb)
                    drain()
            if N2 > 0 and not debug:
                pend["gen"] = tail_g(N2 - 1)
                drain()
            P.run()

        stW.close()

        with ExitStack() as st:
            sb = lambda n, s, d=F32: st.enter_context(nc.sbuf_tensor("ph3_" + n, list(s), d))
            g2row, bg2 = make_row(st, "g2row", 40)
            wg = sb("wg", [128, 8, DFF], BF16); wu = sb("wu", [128, 8, DFF], BF16); wd = sb("wd", [128, NFF, D], BF16)
            bwg, bwu, bwd = Buf("wg"), Buf("wu"), Buf("wd")
            wg_v = wg_d.rearrange("(kc p) n -> p kc n", p=128); wu_v = wu_d.rearrange("(kc p) n -> p kc n", p=128)
            wd_v = wd_d.rearrange("(c p) n -> p c n", p=128)
            bwgc = [Buf(f"wg{i}") for i in range(2)]; bwuc = [Buf(f"wu{i}") for i in range(2)]; bwdc = [Buf(f"wd{i}") for i in range(11)]
            def load_w(pc):
                c0 = pc * 1408
                P.dma("pool", lambda E, c0=c0: E.dma_start(out=wg[:, :, c0:c0 + 1408], in_=wg_v[:, :, c0:c0 + 1408]), writes=[bwgc[pc]], ndesc=1024)
                P.dma("pool", lambda E, c0=c0: E.dma_start(out=wu[:, :, c0:c0 + 1408], in_=wu_v[:, :, c0:c0 + 1408]), writes=[bwuc[pc]], ndesc=1024)

            def load_wd(pc):
                P.dma("pool", lambda E, pc=pc: E.dma_start(out=wd[:, 2 * pc:2 * pc + 2, :], in_=wd_v[:, 2 * pc:2 * pc + 2, :]), writes=[bwdc[pc]], ndesc=256)
            load_w(0)
            NS = TBF // 128
            x1ts = [sb(f"x1t{i}", [128, NS, D]) for i in range(2)]; bx1s = [[Buf(f"x1t{i}_{j}") for j in range(NS)] for i in range(2)]
            hTs = [sb(f"hT3_{i}", [128, 8, TBF + 2], BF16) for i in range(2)]; bhTs = [Buf("hT3_0"), Buf("hT3_1")]
            env = dict(rr=0, xs=None, bxs=None,
                       xn=[sb("xn0", [128, D], BF16), sb("xn1", [128, D], BF16)], bxn=[Buf("xn0"), Buf("xn1")],
                       junk=sb("junk", [128, D], BF16), bjunk=Buf("junk"), ss=sb("ss", [128, 4]), bss=Buf("ss"))
            act = sb("act", [128, NFF, TBF], BF16); bact = [Buf(f"act{i}") for i in range(NFF)]
            NSL = 3
            GU = [(0, 1), (2, 3), (4, 5)]
            A0 = [sb(f"A0_{i}", [128, TBF]) for i in range(NSL)]; A1 = [sb(f"A1_{i}", [128, TBF]) for i in range(NSL)]
            bA = [Buf(f"A{i}") for i in range(NSL)]
            t2 = [sb(f"t2_{i}", [128, TBF]) for i in range(NSL)]; bt2 = [Buf(f"t2{i}") for i in range(NSL)]
            tmpo = [sb("tmpf0", [128, 512]), sb("tmpf1", [128, 512])]; btmpo = [Buf("tf0"), Buf("tf1")]
            O("pool", "memset", writes=[bhTs[0]], ap=hTs[0][:, :, 0:2], constant=0.0)
            fw = lambda j, c: pv[:, PV_CFW + c * 3 + j:PV_CFW + c * 3 + j + 1]

            def prep_gens(fb):
                i = fb % 2
                return [make_hT_g(env, out_t, fb * NS + s_, 1, ab[:, 8:16], mod[:, 24:32], hTs[i][:, :, 2 + s_ * 128:2 + (s_ + 1) * 128], bhTs[i],
                                  keep=x1ts[i][:, s_:s_ + 1, :], bkeep=bx1s[i][s_:s_ + 1], src_bufs=obuf, tbank=6) for s_ in range(NS)]

            def adv(g_, n):
                for _ in range(n):
                    try:
                        next(g_)
                    except StopIteration:
                        return

            def prep_halo(fb):
                i = fb % 2
                if fb > 0:
                    O("pool", "tensor_copy", reads=[bhTs[1 - i]], writes=[bhTs[i]], out=hTs[i][:, :, 0:2], in_=hTs[1 - i][:, :, TBF:TBF + 2])

            def prep(fb):
                for g_ in prep_gens(fb):
                    adv(g_, 100)
                prep_halo(fb)

            def ffn_A(c, hT, bhT):
                sl = c % NSL
                bg_, bu_ = GU[sl]
                pc = (c * 128) // 1408
                pc2 = (c * 128 + 127) // 1408
                rd = [bwgc[pc], bwuc[pc], bhT] + ([bwgc[pc2], bwuc[pc2]] if pc2 != pc else [])
                mmg([dict(out=banks[bg_][:, 0:TBF + 2], lhsT=wg[:, kc, c * 128:(c + 1) * 128], rhs=hT[:, kc, 0:TBF + 2], start=(kc == 0), stop=(kc == 7)) for kc in range(8)],
                    rd, [bb[bg_]])
                mmg([dict(out=banks[bu_][:, 0:TBF], lhsT=wu[:, kc, c * 128:(c + 1) * 128], rhs=hT[:, kc, 2:TBF + 2], start=(kc == 0), stop=(kc == 7)) for kc in range(8)],
                    rd, [bb[bu_]])

            def ffn_B(c):
                sl = c % NSL
                bg_, bu_ = GU[sl]
                G = banks[bg_]
                O("act", "activation", reads=[bpv], writes=[bb[bg_], bA[sl]], out=A0[sl][:], in_=G[:, 0:TBF], func=AF.Identity, scale=fw(0, c))
                O("act", "activation", reads=[bpv], writes=[bb[bg_], bA[sl]], out=A1[sl][:], in_=G[:, 1:TBF + 1], func=AF.Identity, scale=fw(1, c))
                O("dve", "scalar_tensor_tensor", reads=[bA[sl], bpv], writes=[bb[bg_], bt2[sl]], out=t2[sl][:], in0=G[:, 2:TBF + 2], scalar=fw(2, c),
                  in1=A1[sl][:], op0=ALU.mult, op1=ALU.add)
                O("pool", "tensor_tensor", reads=[bA[sl]], writes=[bt2[sl]], out=t2[sl][:], in0=t2[sl][:], in1=A0[sl][:], op=ALU.add)

            def ffn_C(c):
                sl = c % NSL
                bg_, bu_ = GU[sl]
                O("act", "activation", writes=[bt2[sl]], out=t2[sl][:], in_=t2[sl][:], func=AF.Silu)
                O("dve", "tensor_tensor", reads=[bt2[sl]], writes=[bb[bu_], bact[c]], out=act[:, c, :], in0=banks[bu_][:, 0:TBF], in1=t2[sl][:], op=ALU.mult)

            if NB > 0:
                prep(0)
            for fb in range(NB):
                i = fb % 2
                x1t, bx1 = x1ts[i], bx1s[i]
                for c in range(NFF + 2):
                    if c < NFF:
                        ffn_A(c, hTs[i], bhTs[i])
                    if 1 <= c <= NFF:
                        ffn_B(c - 1)
                    if c >= 2:
                        ffn_C(c - 2)
                    if fb + 1 < NB:
                        if c == 2:
                            pg = prep_gens(fb + 1)
                            adv(pg[0], 4)
                        if c == 4:
                            adv(pg[1], 4)
                        if c == 10:
                            adv(pg[0], 100)
                        if c == 15:
                            adv(pg[1], 100)
                            prep_halo(fb + 1)
                    if fb == 0:
                        if c == 1:
                            load_w(1)
                        if c >= 9 and c - 9 < 11:
                            load_wd(c - 9)
                for s in range(NS):
                    ti = fb * NS + s
                    for hf in range(2):
                        bk = 6 + hf
                        for c in range(NFF):
                            O("pe", "matmul", reads=[bact[c], bwdc[c // 2]], writes=[bb[bk]], signal=(c == NFF - 1), out=banks[bk][:],
                              lhsT=act[:, c, s * 128:(s + 1) * 128], rhs=wd[:, c, hf * 512:(hf + 1) * 512], start=(c == 0), stop=(c == NFF - 1))
                        O("dve", "tensor_tensor", reads=[bg2], writes=[bb[bk], btmpo[hf]], out=tmpo[hf][:], in0=banks[bk][:], in1=g2row[:, hf * 512:(hf + 1) * 512], op=ALU.mult)
                        O("pool", "tensor_tensor", reads=[btmpo[hf]], writes=[bx1[s]], out=x1t[:, s, hf * 512:(hf + 1) * 512], in0=x1t[:, s, hf * 512:(hf + 1) * 512],
                          in1=tmpo[hf][:], op=ALU.add)
                    tk = P.dma("sp", lambda E, s=s, ti=ti, x1t=x1t: E.dma_start(out=out_t[ti], in_=x1t[:, s, :]), reads=[bx1[s]], writes=[obuf[ti]])
                    final_toks.append(tk)
            for tk in final_toks:
                P.wait("sp", tk)
            P.run()
        print("instr counts", P.ninstr, "sems", P.nsem)
    return nc


def _alibi_slopes(n):
    return np.array([2.0 ** (-8.0 * (h + 1) / n) for h in range(n)], dtype=np.float32)


def _const_tables():
    t = np.arange(S)
    sl = _alibi_slopes(8)
    qaug = np.zeros((4, 8, S), np.float32)
    for h in range(8):
        qaug[0, h] = -64.0 * sl[h] * (t // 64)
        qaug[1, h] = -sl[h] * (t % 64)
        qaug[2, h] = 64.0 * sl[h]
        qaug[3, h] = sl[h]
    kaug = np.zeros((4, S), np.float32)
    kaug[0] = 1.0; kaug[1] = 1.0; kaug[2] = t // 64; kaug[3] = t % 64
    npr = np.arange(256)
    cend = 16 * npr + 15
    caug = np.zeros((4, 256), np.float32)
    caug[0] = 1.0; caug[1] = 1.0; caug[2] = cend // 64; caug[3] = cend % 64
    caug[2, 0] = -10000.0
    n_cmp = (S - 32) // 16 + 1
    cs = np.arange(n_cmp) * 16
    bs = np.arange(64) * 64
    overlap = ((cs[None, :] < bs[:, None] + 64) & (cs[None, :] + 32 > bs[:, None])).astype(np.float32)
    ovfull = np.zeros((256, 64), np.float32)
    ovfull[1:1 + n_cmp] = overlap.T
    ovT = np.ascontiguousarray(ovfull.reshape(2, 128, 64).transpose(1, 0, 2))
    return qaug, kaug, caug, ovT


_NC_CACHE = {}


def kernel(**inputs):
    f = lambda k: np.ascontiguousarray(np.asarray(inputs[k], dtype=np.float32))
    x = f("x"); c = f("c")
    B = x.shape[0]
    col = lambda v: np.ascontiguousarray(v.reshape(-1, 128).T)
    rep = lambda v: np.broadcast_to(v.reshape(1, -1), (128, v.size))
    qaug, kaug, caug, ovT = _const_tables()
    b_ada = f("b_ada")[0]
    cmw = f("conv_mix_w")[0]
    cfw = f("conv_ffn_w")[0]
    shared_pv = [col(b_ada), col(f("norm1_g")[0]), col(f("norm2_g")[0]),
                 cmw.reshape(3, 4, 128).transpose(2, 1, 0).reshape(128, 12),
                 col(f("conv_out_g")[0]),
                 np.tile(f("k_norm_cmp_g")[0].reshape(64, 1), (2, 1))]
    cfw_l = cfw.reshape(3, NFF, 128).transpose(2, 1, 0).reshape(128, 66)
    rowvecs = np.ascontiguousarray(np.concatenate([rep(f("b_gate")[0]), rep(f("q_norm_g")[0]), rep(f("k_norm_slc_g")[0]),
                                                   rep(f("k_norm_win_g")[0]), rep(f("attn_out_g")[0])], axis=1), dtype=np.float32)
    posT = np.ascontiguousarray(np.concatenate([np.tile(f("pos_cmp_k")[0].T, (2, 1)), np.tile(f("pos_cmp_v")[0].T, (2, 1))], axis=1), dtype=np.float32)
    common = {
        "w_ada": f("w_ada")[0], "rowvecs": rowvecs, "w_in": f("w_in")[0], "posT": posT,
        "w_cmp_k1": f("w_cmp_k1")[0], "w_cmp_v1": f("w_cmp_v1")[0], "w_cmp_k2": f("w_cmp_k2")[0], "w_cmp_v2": f("w_cmp_v2")[0],
        "w_out": f("w_out")[0], "w_ffn_gate": f("w_ffn_gate")[0], "w_ffn_up": f("w_ffn_up")[0], "w_ffn_down": f("w_ffn_down")[0],
        "qaug": qaug, "kaug": kaug, "caug": caug, "ovT": ovT,
    }
    in_maps = []
    for b in range(B):
        cT = np.repeat(col(c[b]), 2, axis=1)
        pvecs = np.ascontiguousarray(np.concatenate(shared_pv + [cT, cfw_l], axis=1), dtype=np.float32)
        assert pvecs.shape == (128, 163), pvecs.shape
        m = dict(common)
        m["x"] = x[b]
        m["pvecs"] = pvecs
        in_maps.append(m)
    if "nc" not in _NC_CACHE:
        _NC_CACHE["nc"] = build_program(DEBUG)
    nc = _NC_CACHE["nc"]
    res = run_bass_kernel_spmd(nc, in_maps, core_ids=list(range(B)))
    if DEBUG:
        kernel.last = res.results
    return np.stack([np.asarray(r["out"], dtype=np.float32) for r in res.results], axis=0)
```

```python
import numpy as np
from contextlib import ExitStack
import concourse.bass as bass
import concourse.mybir as mybir
from concourse.bass_utils import run_bass_kernel_spmd

F32 = mybir.dt.float32
BF16 = mybir.dt.bfloat16
ALU = mybir.AluOpType
AF = mybir.ActivationFunctionType
AX = mybir.AxisListType

S = 4096
D = 1024
NIN = 2840
DFF = 2816
NFF = 22
EPS = 1e-6
NEG = -30000.0
TB = 512
NTB = S // TB
TBF = 256
ENGINES = ("pe", "act", "dve", "pool", "sp")
SEM_CAP = 30000
DEBUG = False
import os
N1 = int(os.environ.get('K_N1', NTB))
N2 = int(os.environ.get('K_N2', NTB))
NB = int(os.environ.get('K_NB', S // TBF))
CUT = int(os.environ.get('K_CUT', 99))


class Buf:
    __slots__ = ("name", "w", "r")

    def __init__(self, name):
        self.name = name
        self.w = None
        self.r = {}


class Prog:
    def __init__(self, nc, stack, n_dma_sems=24):
        self.nc = nc
        self.stack = stack
        self.q = {e: [] for e in ENGINES}
        self.sem = {}
        self.cnt = {}
        self.nsem = 0
        for e in ENGINES:
            self._new_sem(e)
        self.waited = {}
        self.dma_sems = {e: [stack.enter_context(nc.semaphore(f"dma_{e}{i}")) for i in range(n_dma_sems)] for e in ("sp", "pool")}
        self.dma_cnt = {e: [0] * n_dma_sems for e in ("sp", "pool")}
        self.dma_rr = {"sp": 0, "pool": 0}
        self.ninstr = {e: 0 for e in ENGINES}

    def _new_sem(self, e):
        self.sem[e] = self.stack.enter_context(self.nc.semaphore(f"s_{e}_{self.nsem}"))
        self.nsem += 1
        self.cnt[e] = 0
        self.own = getattr(self, "own", {})
        self.own.setdefault(e, set()).add(id(self.sem[e]))

    def _wait(self, eng, tok):
        if tok is None:
            return
        sem, val = tok
        key = (eng, id(sem))
        if self.waited.get(key, 0) >= val:
            return
        self.waited[key] = val
        self.q[eng].append(lambda E, sem=sem, val=val: E.wait_ge(sem, val))

    def _deps(self, eng, reads, writes):
        for b in reads:
            self._wait(eng, b.w)
        for b in writes:
            self._wait(eng, b.w)
            for t in b.r.values():
                self._wait(eng, t)

    def _mark(self, tok, reads, writes):
        for b in reads:
            b.r[id(tok[0])] = tok
        for b in writes:
            b.w = tok
            b.r = {}

    def op(self, eng, fn, reads=(), writes=(), signal=True, relaxed=()):
        for b in relaxed:
            if b.w is not None and id(b.w[0]) not in self.own[eng]:
                self._wait(eng, b.w)
            for t in b.r.values():
                self._wait(eng, t)
        writes = list(writes) + list(relaxed) if relaxed else writes
        self._deps(eng, reads, [b for b in writes if b not in relaxed] if relaxed else writes)
        self.ninstr[eng] += 1
        if not signal:
            self.q[eng].append(lambda E, fn=fn: fn(E))
            return None
        if self.cnt[eng] >= SEM_CAP:
            self._new_sem(eng)
        self.cnt[eng] += 1
        sem, val = self.sem[eng], self.cnt[eng]
        self.q[eng].append(lambda E, fn=fn, sem=sem: fn(E).then_inc(sem, 1))
        tok = (sem, val)
        self._mark(tok, reads, writes)
        return tok

    def dma(self, eng, fn, reads=(), writes=(), ndesc=256):
        if eng == "pool":
            self.outst = getattr(self, "outst", [])
            while self.outst and sum(n for _, n in self.outst) + ndesc > 5000:
                tok0, _ = self.outst.pop(0)
                self._wait("pool", tok0)
        i = self.dma_rr[eng]
        self.dma_rr[eng] = (i + 1) % len(self.dma_sems[eng])
        sem = self.dma_sems[eng][i]
        if self.dma_cnt[eng][i] > 0:
            self._wait(eng, (sem, self.dma_cnt[eng][i]))
        self._deps(eng, reads, writes)
        self.dma_cnt[eng][i] += 16
        val = self.dma_cnt[eng][i]
        self.q[eng].append(lambda E, fn=fn, sem=sem: fn(E).then_inc(sem, 16))
        tok = (sem, val)
        self._mark(tok, reads, writes)
        self.ninstr[eng] += 1
        if eng == "pool":
            self.outst.append((tok, ndesc))
        return tok

    def wait(self, eng, tok):
        self._wait(eng, tok)

    def run(self):
        nc = self.nc
        for e in ("sp", "pool"):
            for i, sem in enumerate(self.dma_sems[e]):
                if self.dma_cnt[e][i] > 0:
                    self._wait(e, (sem, self.dma_cnt[e][i]))
        q = self.q
        with nc.Block() as block:
            @block.tensor
            def _(E):
                for f in q["pe"]:
                    f(E)

            @block.scalar
            def _(E):
                for f in q["act"]:
                    f(E)

            @block.vector
            def _(E):
                for f in q["dve"]:
                    f(E)

            @block.gpsimd
            def _(E):
                for f in q["pool"]:
                    f(E)

            @block.sync
            def _(E):
                for f in q["sp"]:
                    f(E)
        self.q = {e: [] for e in ENGINES}


def bc(ap, shape):
    return ap.unsqueeze(len(ap.shape)).broadcast_to(list(ap.shape) + [shape])


def build_program(debug=False):
    nc = bass.Bass("TRN2", target_bir_lowering=False)
    dt = lambda n, s, k="ExternalInput": nc.dram_tensor(n, list(s), F32, kind=k).ap()
    x_d = dt("x", [S, D])
    wada_d = dt("w_ada", [D, 6 * D])
    pv_d = dt("pvecs", [128, 163])
    rv_d = dt("rowvecs", [128, 728])
    win_d = dt("w_in", [D, NIN])
    pos_d = dt("posT", [128, 64])
    w1k_d = dt("w_cmp_k1", [2048, 256])
    w1v_d = dt("w_cmp_v1", [2048, 256])
    w2k_d = dt("w_cmp_k2", [256, 64])
    w2v_d = dt("w_cmp_v2", [256, 64])
    wout_d = dt("w_out", [D, D])
    wg_d = dt("w_ffn_gate", [D, DFF])
    wu_d = dt("w_ffn_up", [D, DFF])
    wd_d = dt("w_ffn_down", [DFF, D])
    qaug_d = dt("qaug", [4, 8, S])
    kaug_d = dt("kaug", [4, S])
    caug_d = dt("caug", [4, 256])
    ov_d = dt("ovT", [128, 2, 64])
    out_d = dt("out", [S, D], "ExternalOutput")
    dbg = {}
    if debug:
        for n, s in [("d_mod", [128, 48]), ("d_hT", [128, 8 * TB]), ("d_kcT", [68, 512]), ("d_vcx", [128, 4 * 129]),
                     ("d_qT", [68, 8 * TB]), ("d_gates", [128, 96]), ("d_attnb", [128, 4 * 512]), ("d_mixT", [128, 8 * TB]),
                     ("d_ksT", [68, 2 * TB]), ("d_imp", [128, 256]), ("d_selb", [128, 256]), ("d_acc", [128, 1024])]:
            dbg[n] = nc.dram_tensor(n, s, F32, kind="ExternalOutput").ap()

    x_t = x_d.rearrange("(t p) d -> t p d", p=128)
    out_t = out_d.rearrange("(t p) d -> t p d", p=128)
    obuf = [Buf(f"out{i}") for i in range(32)]
    final_toks = []

    with ExitStack() as st0:
        P = Prog(nc, st0)
        sb0 = lambda n, s, d=F32: st0.enter_context(nc.sbuf_tensor(n, list(s), d))
        banks = [st0.enter_context(nc.psum_tensor(f"bank{i}", [128, 512], F32)) for i in range(8)]
        bb = [Buf(f"bank{i}") for i in range(8)]

        def O(eng, method, reads=(), writes=(), signal=True, relaxed=(), **kw):
            return P.op(eng, lambda E: getattr(E, method)(**kw), reads, writes, signal, relaxed)

        def mmg(mms, reads, writes):
            n = len(mms)
            tok = None
            for i, m in enumerate(mms):
                kw = dict(out=m["out"], lhsT=m["lhsT"], rhs=m["rhs"], start=m["start"], stop=m["stop"])
                if m.get("skip"):
                    kw["skip_group_check"] = True
                tok = O("pe", "matmul", reads, writes, signal=(i == n - 1), **kw)
            return tok

        def dbg_out(name, tile_ap, b, eng="pool", dst=None):
            if debug and name in dbg:
                d_ = dbg[name] if dst is None else dst
                t = P.dma(eng, lambda E: E.dma_start(out=d_, in_=tile_ap), reads=[b])
                final_toks.append(t)

        ident_bf = sb0("ident_bf", [128, 128], BF16); ident_f = sb0("ident_f", [128, 128])
        ones_bf = sb0("ones_bf", [128, 128], BF16); ones_f = sb0("ones_f", [128, 128])
        G64 = sb0("G64", [128, 128], BF16)
        pv = sb0("pv", [128, 163]); rv = sb0("rv", [128, 728])
        mod = sb0("mod", [128, 48]); ab = sb0("ab", [128, 16]); epsT = sb0("epsT", [128, 1])
        qg8 = sb0("qg8", [128, 64])
        stW = ExitStack()
        sbw = lambda n, s, d=F32: stW.enter_context(nc.sbuf_tensor("w_" + n, list(s), d))
        kcT = sbw("kcT", [68, 2, 256], BF16); vcx = sbw("vcx", [128, 2, 2, 129], BF16)
        Lc = sbw("Lc", [128, 2560], BF16); Ldiag = sbw("Ldiag", [128, 128], BF16); Lfar = sbw("Lfar", [128, 128], BF16)
        Eall = sbw("Eall", [128, S], BF16)
        wA = sbw("wA", [128, 8, 512], BF16)
        wB = sbw("wB", [128, 8, NIN - 768], BF16)
        bwin = Buf("win")
        wout = sbw("wout", [128, 8, D], BF16); bwout = Buf("wout")
        bconst = Buf("const"); bpv = Buf("pv"); brv = Buf("rv"); bmod = Buf("mod"); bkcT = Buf("kcT"); bvcx = Buf("vcx")

        PV_BADA, PV_N1G, PV_N2G, PV_CMW, PV_COG, PV_KCG, PV_CT, PV_CFW = 0, 48, 56, 64, 76, 80, 81, 97
        RV_BG, RV_QG, RV_KSG, RV_KWG, RV_AOG = 0, 24, 88, 152, 216

        P.dma("sp", lambda E: E.dma_start(out=pv[:], in_=pv_d), writes=[bpv])
        P.dma("sp", lambda E: E.dma_start(out=rv[:], in_=rv_d), writes=[brv])
        O("pool", "memset", writes=[bconst], ap=ident_bf[:], constant=1.0)
        O("pool", "affine_select", reads=[bconst], writes=[bconst], out=ident_bf[:], in_=ident_bf[:], pattern=[[1, 128]],
          compare_op=ALU.is_equal, fill=0.0, base=0, channel_multiplier=-1)
        O("pool", "tensor_copy", reads=[bconst], writes=[bconst], out=ident_f[:], in_=ident_bf[:])
        O("pool", "memset", writes=[bconst], ap=ones_bf[:], constant=1.0)
        O("pool", "memset", writes=[bconst], ap=ones_f[:], constant=1.0)
        O("pool", "memset", writes=[bconst], ap=epsT[:], constant=EPS)
        O("pool", "memset", writes=[bconst], ap=G64[:], constant=0.0)
        O("pool", "memset", writes=[bconst], ap=G64[0:64, 0:64], constant=1.0)
        O("pool", "memset", writes=[bconst], ap=G64[64:128, 64:128], constant=1.0)
        O("pool", "memset", writes=[bconst], ap=Lc[:], constant=0.0)
        O("pool", "affine_select", reads=[bconst], writes=[bconst], out=Lc[:], in_=Lc[:], pattern=[[1, 2560]],
          compare_op=ALU.is_ge, fill=NEG, base=-15, channel_multiplier=-16)
        O("pool", "memset", writes=[bconst], ap=Ldiag[:], constant=0.0)
        O("pool", "affine_select", reads=[bconst], writes=[bconst], out=Ldiag[:], in_=Ldiag[:], pattern=[[1, 128]],
          compare_op=ALU.is_ge, fill=NEG, base=0, channel_multiplier=-1)
        O("pool", "memset", writes=[bconst], ap=Lfar[:], constant=0.0)
        O("pool", "affine_select", reads=[bconst], writes=[bconst], out=Lfar[:], in_=Lfar[:], pattern=[[-1, 128]],
          compare_op=ALU.is_ge, fill=NEG, base=-1, channel_multiplier=1)
        O("pool", "memset", writes=[bconst], ap=Eall[:], constant=1.0)
        O("pool", "affine_select", reads=[bconst], writes=[bconst], out=Eall[:], in_=Eall[:], pattern=[[1, S]],
          compare_op=ALU.is_ge, fill=0.0, base=0, channel_multiplier=-64)
        O("pool", "affine_select", reads=[bconst], writes=[bconst], out=Eall[:], in_=Eall[:], pattern=[[-1, S]],
          compare_op=ALU.is_ge, fill=0.0, base=63, channel_multiplier=64)
        O("pool", "memset", writes=[bkcT], ap=kcT[0:64, :, :], constant=0.0)
        P.dma("pool", lambda E: E.dma_start(out=kcT[64:68, 0, :], in_=caug_d), writes=[bkcT])
        P.dma("pool", lambda E: E.dma_start(out=kcT[64:68, 1, :], in_=caug_d), writes=[bkcT])
        O("pool", "memset", writes=[bvcx], ap=vcx[:], constant=0.0)
        O("pool", "memset", writes=[bvcx], ap=vcx[:, :, :, 64:65], constant=1.0)
        for g in range(2):
            P.dma("pool", lambda E, g=g: E.dma_start(out=vcx[:, :, g, 65:129], in_=ov_d), writes=[bvcx])
        O("dve", "tensor_scalar", reads=[brv], writes=[bconst], out=qg8[:], in0=rv[:, RV_QG:RV_QG + 64], scalar1=0.125,
          scalar2=None, op0=ALU.mult)

        def make_row(st, name, col0):
            row = st.enter_context(nc.sbuf_tensor(name, [128, D], F32))
            brow = Buf(name)
            Dt = [st.enter_context(nc.sbuf_tensor(f"{name}_D{i}", [128, 128], F32)) for i in range(2)]
            bD = [Buf("D0"), Buf("D1")]
            for c in range(8):
                i = c % 2
                bk = 5 + (c // 4)
                O("dve", "tensor_scalar", reads=[bmod, bconst], writes=[bD[i]], out=Dt[i][:], in0=ident_f[:],
                  scalar1=mod[:, col0 + c:col0 + c + 1], scalar2=None, op0=ALU.mult)
                mmg([dict(out=banks[bk][:, (c % 4) * 128:(c % 4 + 1) * 128], lhsT=ones_f[:], rhs=Dt[i][:], start=True, stop=True)],
                    [bD[i], bconst], [bb[bk]])
                if c % 4 == 3:
                    O("act", "copy", writes=[bb[bk], brow], out=row[:, (c // 4) * 512:(c // 4 + 1) * 512], in_=banks[bk][:])
            return row, brow

        def make_hT(*a, **k):
            for _ in make_hT_g(*a, **k):
                pass

        def make_hT_g(env, src_t, tile0, nsub, acol, bcol, hT, bhT, keep=None, bkeep=None, src_bufs=None, tbank=7):
            for s in range(nsub):
                i = env["rr"] % 2
                env["rr"] += 1
                if keep is None:
                    xs, bxs = env["xs"][i], env["bxs"][i]
                    xs_ap = xs[:]
                else:
                    xs_ap, bxs = keep[:, s, :], bkeep[s]
                rd = [src_bufs[tile0 + s]] if src_bufs is not None else []
                P.dma("sp", lambda E, xs_ap=xs_ap, s=s: E.dma_start(out=xs_ap, in_=src_t[tile0 + s]), reads=rd, writes=[bxs])
                yield
                ss = env["ss"]; bss = env["bss"]
                O("act", "activation", reads=[bxs], writes=[env["bjunk"], bss], out=env["junk"][:], in_=xs_ap, func=AF.Square,
                  accum_out=ss[:, 0:1])
                yield
                O("act", "activation", reads=[bss, bconst], writes=[bss], out=ss[:, 1:2], in_=ss[:, 0:1], func=AF.Ln, scale=1.0 / D, bias=epsT[:, 0:1])
                O("act", "activation", reads=[bss], writes=[bss], out=ss[:, 3:4], in_=ss[:, 1:2], func=AF.Exp, scale=-0.5)
                yield
                xn, bxn = env["xn"][i], env["bxn"][i]
                O("dve", "tensor_scalar", reads=[bxs, bss], writes=[bxn], out=xn[:], in0=xs_ap, scalar1=ss[:, 3:4], scalar2=None,
                  op0=ALU.mult)
                yield
                tp = banks[tbank][:].bitcast(BF16).rearrange("p (c t) -> p c t", t=128)
                for fc in range(8):
                    O("pe", "transpose", reads=[bxn, bconst], writes=[bb[tbank]], signal=(fc == 7), out=tp[:, fc, :],
                      in_=xn[:, fc * 128:(fc + 1) * 128], identity=ident_bf[:])
                yield
                for fc in range(8):
                    O("dve", "tensor_scalar", reads=[bmod], relaxed=[bb[tbank], bhT], out=hT[:, fc, s * 128:(s + 1) * 128], in0=tp[:, fc, :],
                      scalar1=acol[:, fc:fc + 1], scalar2=bcol[:, fc:fc + 1], op0=ALU.mult, op1=ALU.add)

        def rstd_small(ssq, n, width, bufs):
            O("act", "activation", reads=list(bufs) + [bconst], writes=bufs, out=ssq, in_=ssq, func=AF.Ln, scale=1.0 / n, bias=epsT[:, 0:1])
            O("act", "activation", reads=bufs, writes=bufs, out=ssq, in_=ssq, func=AF.Exp, scale=-0.5)

        win_v = win_d.rearrange("(kc p) n -> p kc n", p=128)
        wout_v = wout_d.rearrange("(kc p) n -> p kc n", p=128)

        with ExitStack() as st:
            sb = lambda n, s, d=F32: st.enter_context(nc.sbuf_tensor("ph1_" + n, list(s), d))
            winc = sb("winc", [128, 8, 256], BF16); bwinc = Buf("winc")
            w1 = [sb("w1k", [128, 32, 256], BF16), sb("w1v", [128, 32, 256], BF16)]; bw1 = Buf("w1")
            w2 = [sb("w2k", [128, 2, 64], BF16), sb("w2v", [128, 2, 64], BF16)]; bw2 = Buf("w2")
            posb = sb("posb", [128, 64], BF16); bpos = Buf("pos")
            cbias = sb("cbias", [128, 4]); bcb = Buf("cb")
            X = [sb("xk", [128, 16, 257], BF16), sb("xv", [128, 16, 257], BF16)]; bX = [Buf("xk"), Buf("xv")]
            vcT = sb("vcT", [64, 2, 256]); bvcT = Buf("vcT")
            hT = sb("hT1", [128, 8, TB], BF16); bhT = Buf("hT1")
            env = dict(rr=0, xs=[sb("xs0", [128, D]), sb("xs1", [128, D])], bxs=[Buf("xs0"), Buf("xs1")],
                       xn=[sb("xn0", [128, D], BF16), sb("xn1", [128, D], BF16)], bxn=[Buf("xn0"), Buf("xn1")],
                       junk=sb("junk", [128, D], BF16), bjunk=Buf("junk"), ss=sb("ss", [128, 4]), bss=Buf("ss"))
            xsg = [sb("xsg0", [128, 256]), sb("xsg1", [128, 256])]; x2g = [sb("x2g0", [128, 256]), sb("x2g1", [128, 256])]
            bgl = [Buf("gl0"), Buf("gl1")]
            gl = sb("gl", [128, 2, 2, 2, 256], BF16); bglo = Buf("glo")
            sqk = sb("sqk", [64, 512]); rsk = sb("rsk", [64, 512]); bsqk = Buf("sqk")
            for kv in range(2):
                O("pool", "memset", writes=[bX[kv]], ap=X[kv][:, :, 0:1], constant=0.0)
            P.dma("pool", lambda E: E.dma_start(out=winc[:], in_=win_v[:, :, 512:768]), writes=[bwinc], ndesc=1024)
            wa = [sb(f"wa{i}", [128, 8, 512]) for i in range(2)]
            bwa = [Buf("wa0"), Buf("wa1")]
            sc = sb("siluc", [128, 16]); bsc = Buf("sc")
            O("act", "activation", reads=[bpv], writes=[bsc], out=sc[:], in_=pv[:, PV_CT:PV_CT + 16], func=AF.Silu)
            wada_v = wada_d.rearrange("(kc p) n -> p kc n", p=128)
            mps = banks[0][:, 0:96].rearrange("p (j t) -> p j t", t=2)
            scv = sc[:].rearrange("p (k t) -> p k t", t=2)
            modtok = {}

            def mod_dma(pc):
                i = pc % 2
                modtok[pc] = P.dma("sp", lambda E: E.dma_start(out=wa[i][:], in_=wada_v[:, :, pc * 512:(pc + 1) * 512]), writes=[bwa[i]])

            def mod_mm(pc):
                i = pc % 2
                for cc in range(4):
                    j = pc * 4 + cc
                    mmg([dict(out=mps[:, j, :], lhsT=wa[i][:, kc, cc * 128:(cc + 1) * 128], rhs=scv[:, kc, :],
                              start=(kc == 0), stop=(kc == 7)) for kc in range(8)], [bwa[i], bsc], [bb[0]])
            mod_dma(0); mod_dma(1); mod_mm(0); mod_dma(2); mod_mm(1); mod_dma(3); mod_mm(2); mod_mm(3)
            O("dve", "tensor_tensor", reads=[bpv], writes=[bb[0], bmod], out=mod[:, 0:16], in0=mps[:, 0:16, 0], in1=pv[:, PV_BADA:PV_BADA + 16], op=ALU.add)
            O("dve", "scalar_tensor_tensor", reads=[bmod, bpv], writes=[bmod], out=ab[:, 0:8], in0=mod[:, 8:16], scalar=1.0,
              in1=pv[:, PV_N1G:PV_N1G + 8], op0=ALU.add, op1=ALU.mult)
            P.wait("pool", modtok[3])
            for kv, wd1 in enumerate((w1k_d, w1v_d)):
                src = wd1.rearrange("(l d) n -> d l n", d=64)
                for hf in range(2):
                    P.dma("pool", lambda E, kv=kv, hf=hf, src=src: E.dma_start(out=w1[kv][hf * 64:(hf + 1) * 64, :, :], in_=src), writes=[bw1], ndesc=2048)
            for kv, wd2 in enumerate((w2k_d, w2v_d)):
                P.dma("pool", lambda E, kv=kv, wd2=wd2: E.dma_start(out=w2[kv][:], in_=wd2.rearrange("(h p) n -> p h n", p=128)), writes=[bw2])
            P.dma("pool", lambda E: E.dma_start(out=posb[:], in_=pos_d), writes=[bpos])
            for kv in range(2):
                for hf in range(2):
                    mmg([dict(out=banks[1][:, kv * 2 + hf:kv * 2 + hf + 1], lhsT=w1[kv][0:64, l, hf * 128:(hf + 1) * 128],
                              rhs=posb[0:64, kv * 32 + l:kv * 32 + l + 1], start=(l == 0), stop=(l == 31)) for l in range(32)],
                        [bw1, bpos], [bb[1]])
            O("dve", "tensor_copy", writes=[bb[1], bcb], out=cbias[:], in_=banks[1][:, 0:4])


            for tb in range(NTB):
                mod_dma(4 + tb)
                make_hT(env, x_t, tb * 4, 4, ab[:, 0:8], mod[:, 0:8], hT, bhT)
                for kv in range(2):
                    bk = 5 + kv
                    mmg([dict(out=banks[bk][:], lhsT=winc[:, kc, kv * 128:(kv + 1) * 128], rhs=hT[:, kc, :], start=(kc == 0), stop=(kc == 7))
                         for kc in range(8)], [bwinc, bhT], [bb[bk]])
                    O("act", "copy", writes=[bb[bk], bX[kv]], out=X[kv][:, :, 1 + tb * 32:1 + (tb + 1) * 32].rearrange("p r m -> p m r"),
                      in_=banks[bk][:].rearrange("p (m r) -> p m r", r=16))
                if tb >= 1:
                    mod_mm(4 + tb - 1)
            mod_mm(11)
            P.dma("pool", lambda E: E.dma_start(out=wA[:], in_=win_v[:, :, 0:512]), writes=[bwin], ndesc=1024)
            for c0 in range(768, NIN, 1036):
                P.dma("pool", lambda E, c0=c0: E.dma_start(out=wB[:, :, c0 - 768:c0 - 768 + 1036], in_=win_v[:, :, c0:c0 + 1036]), writes=[bwin], ndesc=1024)
            O("dve", "tensor_tensor", reads=[bpv], writes=[bb[0], bmod], out=mod[:, 16:48], in0=mps[:, 16:48, 0], in1=pv[:, PV_BADA + 16:PV_BADA + 48], op=ALU.add)
            O("dve", "scalar_tensor_tensor", reads=[bmod, bpv], writes=[bmod], out=ab[:, 8:16], in0=mod[:, 32:40], scalar=1.0,
              in1=pv[:, PV_N2G:PV_N2G + 8], op0=ALU.add, op1=ALU.mult)
            dbg_out("d_mod", mod[:], bmod, eng="sp")
            hpg = [[banks[2 * g + kv][:].rearrange("p (b n) -> p b n", b=2) for kv in range(2)] for g in range(2)]
            for kv in range(2):
                for g in range(2):
                    mm = []
                    for hf in range(2):
                        for l in range(32):
                            mm.append(dict(out=hpg[g][kv][:, hf, :], lhsT=w1[kv][g * 64:(g + 1) * 64, l, hf * 128:(hf + 1) * 128],
                                           rhs=X[kv][g * 64:(g + 1) * 64, l % 16, l // 16:l // 16 + 256], start=(l == 0), stop=(l == 31)))
                    mmg(mm, [bw1, bX[kv]], [bb[2 * g + kv]])
            rr_ = 0
            for kv in range(2):
                for g in range(2):
                    for hf in range(2):
                        ci = kv * 2 + hf
                        i = rr_ % 2
                        rr_ += 1
                        bk = 2 * g + kv
                        O("act", "activation", reads=[bcb], writes=[bb[bk], bgl[i]], out=xsg[i][:], in_=hpg[g][kv][:, hf, :],
                          func=AF.Identity, bias=cbias[:, ci:ci + 1], scale=1.0)
                        O("dve", "tensor_tensor", writes=[bgl[i]], out=x2g[i][:], in0=xsg[i][:], in1=xsg[i][:], op=ALU.mult)
                        O("dve", "tensor_scalar", writes=[bgl[i]], out=x2g[i][:], in0=x2g[i][:], scalar1=0.044715, scalar2=1.0, op0=ALU.mult, op1=ALU.add)
                        O("dve", "tensor_tensor", writes=[bgl[i]], out=x2g[i][:], in0=x2g[i][:], in1=xsg[i][:], op=ALU.mult)
                        O("act", "activation", writes=[bgl[i]], out=x2g[i][:], in_=x2g[i][:], func=AF.Sigmoid, scale=1.5957691216057308)
                        O("dve", "tensor_tensor", reads=[bgl[i]], writes=[bglo], out=gl[:, kv, hf, g, :], in0=xsg[i][:], in1=x2g[i][:], op=ALU.mult)
            ops_ = [banks[4 + kv][0:64, :].rearrange("p (g n) -> p g n", g=2) for kv in range(2)]
            for kv in range(2):
                mm = []
                for g in range(2):
                    for hf in range(2):
                        mm.append(dict(out=ops_[kv][:, g, :], lhsT=w2[kv][:, hf, :], rhs=gl[:, kv, hf, g, :], start=(hf == 0), stop=(hf == 1)))
                mmg(mm, [bw2, bglo], [bb[4 + kv]])
            O("act", "activation", writes=[bb[4], bsqk], out=sqk[:], in_=banks[4][0:64, :], func=AF.Square)
            mmg([dict(out=banks[6][0:64, :], lhsT=ones_f[0:64, 0:64], rhs=sqk[:], start=True, stop=True)], [bsqk, bconst], [bb[6]])
            O("act", "activation", reads=[bconst], writes=[bb[6], bsqk], out=rsk[:], in_=banks[6][0:64, :], func=AF.Ln, scale=1.0 / 64, bias=epsT[0:64, 0:1])
            O("act", "activation", writes=[bsqk], out=rsk[:], in_=rsk[:], func=AF.Exp, scale=-0.5)
            O("dve", "scalar_tensor_tensor", reads=[bsqk, bpv], writes=[bb[4], bkcT], out=kcT[0:64, :, :],
              in0=ops_[0], scalar=pv[0:64, PV_KCG:PV_KCG + 1], in1=rsk[:].rearrange("p (g n) -> p g n", g=2),
              op0=ALU.mult, op1=ALU.mult)
            O("act", "copy", writes=[bb[5], bvcT], out=vcT[:], in_=ops_[1])
            for tl in range(2):
                for g in range(2):
                    O("pe", "transpose", reads=[bvcT, bconst], writes=[bb[7]], signal=(tl == 1 and g == 1), out=banks[7][:, (tl * 2 + g) * 64:(tl * 2 + g + 1) * 64],
                      in_=vcT[:, g, tl * 128:(tl + 1) * 128], identity=ident_f[0:64, 0:64])
            O("dve", "tensor_copy", writes=[bb[7], bvcx], out=vcx[:, :, :, 0:64],
              in_=banks[7][:, 0:256].rearrange("p (t g d) -> p t g d", t=2, g=2))
            dbg_out("d_kcT", kcT[:].rearrange("p g n -> p (g n)"), bkcT)
            dbg_out("d_vcx", vcx[:].rearrange("p t g n -> p (t g n)"), bvcx)
            P.run()

        with ExitStack() as st:
            sb = lambda n, s, d=F32: st.enter_context(nc.sbuf_tensor("ph2_" + n, list(s), d))
            g1row, bg1 = make_row(st, "g1row", 16)
            P.dma("pool", lambda E: E.dma_start(out=wout[:], in_=wout_v), writes=[bwout], ndesc=1024)
            ksT = sb("ksT", [68, 2, S], BF16); bks = [Buf(f"ksT{i}") for i in range(NTB)]; bksaug = Buf("ksaug")
            KWR = 1536
            kwT = sb("kwT", [68, 2, KWR], BF16); bkw = [Buf(f"kwT{i}") for i in range(3)]
            vsa = sb("vsa", [128, 32, 2, 65], BF16); bvs = [Buf(f"vsa{i}") for i in range(NTB)]; bvs1 = Buf("vs1")
            vwa = sb("vwa", [128, 12, 2, 65], BF16); bvw = [Buf(f"vwa{i}") for i in range(3)]; bvw1 = Buf("vw1")
            qTs = [sb(f"qT{i}", [68, 8, TB], BF16) for i in range(2)]; bqTs = [Buf("qT0"), Buf("qT1")]
            hT = sb("hT2", [128, 8, TB], BF16); bhT = Buf("hT2")
            mixa = sb("mixa", [128, 4, TB], BF16); bmixa = Buf("mixa")
            mixcs = [sb(f"mixc{i}", [128, 4, TB], BF16) for i in range(2)]; bmixcs = [Buf("mixc0"), Buf("mixc1")]
            env = dict(rr=0, xs=[sb("xs0", [128, D]), sb("xs1", [128, D])], bxs=[Buf("xs0"), Buf("xs1")],
                       xn=[sb("xn0", [128, D], BF16), sb("xn1", [128, D], BF16)], bxn=[Buf("xn0"), Buf("xn1")],
                       junk=sb("junk", [128, D], BF16), bjunk=Buf("junk"), ss=sb("ss", [128, 4]), bss=Buf("ss"))
            sqt = sb("sqt", [128, 512]); bsqt = Buf("sqt")
            ssq = sb("ssq", [128, 16]); bssq = Buf("ssq")
            ssq2 = sb("ssq2", [128, 16]); bssq2 = Buf("ssq2")
            sqk2 = sb("sqk2", [128, 256]); bsqk2 = Buf("sqk2")
            ssqk = sb("ssqk", [128, 4]); bssqk = Buf("ssqk")
            qnb = sb("qnb", [128, 512], BF16); bqnb = Buf("qnb")
            knb = sb("knb", [128, 256], BF16); bknb = Buf("knb")
            gatess = [sb(f"gates{i}", [128, 4, 24]) for i in range(2)]; bgatess = [Buf("gates0"), Buf("gates1")]
            xin_sb = sb("xin_sb", [128, TB]); u = sb("u", [128, TB + 2]); t1 = sb("t1", [128, TB]); cv = sb("cv", [128, TB])
            sqb = sb("sqb", [128, TB], BF16); uhalo = sb("uhalo", [128, 4, 2])
            bxin, bu, bt1, bcv, bsqb, buh = [Buf(n) for n in ("xin", "u", "t1", "cv", "sqb", "uh")]
            NPT = 3
            PT = [sb(f"PT{i}", [128, TB], BF16) for i in range(NPT)]; bPT = [Buf(f"PT{i}") for i in range(NPT)]
            scr = sb("scr", [128, 1024]); bscr = Buf("scr")
            Uraw = scr[:].rearrange("p (s h j) -> p s h j", s=4, h=4); bUraw = bscr
            acc = sb("acc", [128, 4, 4, 64]); bacc = Buf("acc")
            rw = sb("rw", [128, 16]); brw = Buf("rw")
            imp = sb("imp", [128, 4, 64]); bimp = Buf("imp")
            wk = sb("wk", [128, 64]); m8 = sb("m8", [128, 16]); btk = Buf("tk")
            selb = sb("selb", [128, 4, 64], BF16); bselb = Buf("selb")
            selbT = [sb(f"selbT{i}", [128, TB], BF16) for i in range(2)]; bselbT = [Buf("selbT0"), Buf("selbT1")]
            for i_ in range(2):
                O("pool", "memset", writes=[bselbT[i_]], ap=selbT[i_][:], constant=0.0)
            attnb = sb("attnb", [128, 4, 512], BF16); battnb = Buf("attnb")
            xr = [sb("xr0", [128, D]), sb("xr1", [128, D])]; bxr = [Buf("xr0"), Buf("xr1")]
            tmpo = [sqt, xin_sb]; btmpo = [bsqt, bxin]

            for g in range(2):
                P.dma("pool", lambda E, g=g: E.dma_start(out=ksT[64:68, g, :], in_=kaug_d), writes=[bksaug])
            O("pool", "memset", writes=[bvs1], ap=vsa[:, :, :, 64:65], constant=1.0)
            O("pool", "memset", writes=[bvw1], ap=vwa[:, :, :, 64:65], constant=1.0)
            O("pool", "memset", writes=[buh], ap=uhalo[:], constant=0.0)
            CB_KV, CB_GL, CB_GB, CB_GC, CB_XI = 0, 1280 - 768, 1304 - 768, 1816 - 768, 2328 - 768
            rot = dict(s=0, pt=0)
            pend = dict(gen=None)

            def tick(n=1):
                for _ in range(n):
                    if pend["gen"] is not None:
                        try:
                            next(pend["gen"])
                        except StopIteration:
                            pend["gen"] = None

            def drain():
                while pend["gen"] is not None:
                    tick()

            def frontend(tb):
                t0 = tb * TB
                qT, bqT = qTs[tb % 2], bqTs[tb % 2]
                gates, bgates = gatess[tb % 2], bgatess[tb % 2]
                mixc, bmixc = mixcs[tb % 2], bmixcs[tb % 2]
                kwi = tb % 3
                P.dma("pool", lambda E: E.dma_start(out=qT[64:68, :, :], in_=qaug_d[:, :, t0:t0 + TB]), writes=[bqT])
                r0 = kwi * TB
                for g in range(2):
                    P.dma("pool", lambda E, g=g: E.dma_start(out=kwT[64:68, g, r0:r0 + TB], in_=kaug_d[:, t0:t0 + TB]), writes=[bkw[kwi]])
                def q_g(s):
                    ti = tb * 4 + s
                    hs = lambda kc: hT[:, kc, s * 128:(s + 1) * 128]
                    mmg([dict(out=banks[5][:], lhsT=hs(kc), rhs=wA[:, kc, :], start=(kc == 0), stop=(kc == 7)) for kc in range(8)],
                        [bhT, bwin], [bb[5]])
                    yield
                    O("act", "activation", writes=[bb[5], bsqt], out=sqt[:], in_=banks[5][:], func=AF.Square)
                    yield
                    O("dve", "tensor_reduce", reads=[bsqt], writes=[bssq], out=ssq[:, 0:8], in_=sqt[:].rearrange("p (h d) -> p h d", d=64),
                      axis=AX.X, op=ALU.add)
                    yield
                    rstd_small(ssq[:, 0:8], 64, 8, [bssq])
                    yield
                    O("dve", "tensor_tensor", reads=[bssq], writes=[bb[5], bsqt], out=sqt[:].rearrange("p (h d) -> p h d", d=64),
                      in0=banks[5][:].rearrange("p (h d) -> p h d", d=64), in1=bc(ssq[:, 0:8], 64), op=ALU.mult)
                    O("dve", "tensor_tensor", reads=[bsqt, bconst], writes=[bqnb], out=qnb[:].rearrange("p (h d) -> p h d", d=64),
                      in0=sqt[:].rearrange("p (h d) -> p h d", d=64),
                      in1=qg8[:].unsqueeze(1).broadcast_to([128, 8, 64]), op=ALU.mult)
                    yield
                    tp = banks[5][0:64, :].bitcast(BF16).rearrange("p (h t) -> p h t", t=128)
                    for h in range(8):
                        O("pe", "transpose", reads=[bqnb, bconst], writes=[bb[5]], signal=(h == 7), out=tp[:, h, :],
                          in_=qnb[:, h * 64:(h + 1) * 64], identity=ident_bf[:])
                    yield
                    O("act", "copy", writes=[bb[5], bqT], out=qT[0:64, :, s * 128:(s + 1) * 128], in_=tp)
                    yield

                def kv_g(s):
                    ti = tb * 4 + s
                    hs = lambda kc: hT[:, kc, s * 128:(s + 1) * 128]
                    mmg([dict(out=banks[6][:], lhsT=hs(kc), rhs=wB[:, kc, CB_KV:CB_KV + 512], start=(kc == 0), stop=(kc == 7)) for kc in range(8)],
                        [bhT, bwin], [bb[6]])
                    yield
                    kview = banks[6][:].rearrange("p (a b c) -> p a b c", a=2, b=2)[:, :, 0, :]
                    O("act", "activation", writes=[bb[6], bsqk2], out=sqk2[:].rearrange("p (a c) -> p a c", a=2), in_=kview, func=AF.Square)
                    O("dve", "tensor_copy", reads=[bvs1], writes=[bb[6]], relaxed=[bvs[tb]], out=vsa[:, ti, :, 0:64],
                      in_=banks[6][:, 128:256].rearrange("p (g d) -> p g d", g=2))
                    O("dve", "tensor_copy", reads=[bvw1], writes=[bb[6]], relaxed=[bvw[kwi]], out=vwa[:, kwi * 4 + s, :, 0:64],
                      in_=banks[6][:, 384:512].rearrange("p (g d) -> p g d", g=2))
                    yield
                    O("dve", "tensor_reduce", reads=[bsqk2], writes=[bssqk], out=ssqk[:, 0:4], in_=sqk2[:].rearrange("p (h d) -> p h d", d=64),
                      axis=AX.X, op=ALU.add)
                    yield
                    rstd_small(ssqk[:, 0:4], 64, 4, [bssqk])
                    yield
                    O("dve", "tensor_tensor", reads=[bssqk], writes=[bb[6], bsqk2], out=sqk2[:].rearrange("p (a g d) -> p a g d", a=2, g=2),
                      in0=kview.rearrange("p a (g d) -> p a g d", g=2),
                      in1=bc(ssqk[:, 0:4].rearrange("p (a g) -> p a g", a=2), 64), op=ALU.mult)
                    O("dve", "tensor_tensor", reads=[bsqk2, brv], writes=[bknb], out=knb[:].rearrange("p (a g d) -> p a g d", a=2, g=2),
                      in0=sqk2[:].rearrange("p (a g d) -> p a g d", a=2, g=2),
                      in1=rv[:, RV_KSG:RV_KSG + 128].rearrange("p (a d) -> p a d", a=2).unsqueeze(2).broadcast_to([128, 2, 2, 64]), op=ALU.mult)
                    yield
                    tp2 = banks[6][0:64, :].bitcast(BF16).rearrange("p (h t) -> p h t", t=128)
                    for j in range(4):
                        O("pe", "transpose", reads=[bknb, bconst], writes=[bb[6]], signal=(j == 3), out=tp2[:, j, :],
                          in_=knb[:, j * 64:(j + 1) * 64], identity=ident_bf[:])
                    yield
                    O("act", "copy", writes=[bb[6]], relaxed=[bks[tb]], out=ksT[0:64, :, ti * 128:(ti + 1) * 128], in_=tp2[:, 0:2, :])
                    rr0 = kwi * TB + s * 128
                    O("act", "copy", writes=[bb[6]], relaxed=[bkw[kwi]], out=kwT[0:64, :, rr0:rr0 + 128], in_=tp2[:, 2:4, :])
                    yield
                    mmg([dict(out=banks[6][:, 0:24], lhsT=hs(kc), rhs=wB[:, kc, CB_GL:CB_GL + 24], start=(kc == 0), stop=(kc == 7)) for kc in range(8)],
                        [bhT, bwin], [bb[6]])
                    yield
                    O("dve", "tensor_tensor", reads=[brv], writes=[bb[6], bgates], out=gates[:, s, :], in0=banks[6][:, 0:24], in1=rv[:, RV_BG:RV_BG + 24], op=ALU.add)
                    yield
                    O("act", "activation", writes=[bgates], out=gates[:, s, :], in_=gates[:, s, :], func=AF.Exp, scale=-1.0)
                    yield
                    O("dve", "tensor_scalar", writes=[bgates], out=gates[:, s, :], in0=gates[:, s, :], scalar1=1.0, scalar2=None, op0=ALU.add)
                    O("dve", "reciprocal", writes=[bgates], out=gates[:, s, :], in_=gates[:, s, :])
                    yield

                def rr(gens):
                    gens = list(gens)
                    while gens:
                        for g_ in list(gens):
                            try:
                                next(g_)
                            except StopIteration:
                                gens.remove(g_)
                        yield

                mk = lambda s: make_hT_g(env, x_t, tb * 4 + s, 1, ab[:, 0:8], mod[:, 0:8], hT[:, :, s * 128:(s + 1) * 128], bhT)
                yield from mk(0)
                for s in range(4):
                    yield from rr([q_g(s), kv_g(s)] + ([mk(s + 1)] if s < 3 else []))
                for c in range(4):
                    for (bk, cb) in ((5, CB_XI), (6, CB_GC), (7, CB_GB)):
                        mmg([dict(out=banks[bk][:], lhsT=wB[:, kc, cb + c * 128:cb + (c + 1) * 128], rhs=hT[:, kc, :], start=(kc == 0), stop=(kc == 7))
                             for kc in range(8)], [bhT, bwin], [bb[bk]])
                    yield
                    O("act", "copy", writes=[bb[5], bxin], out=xin_sb[:], in_=banks[5][:])
                    O("pool", "tensor_copy", reads=[buh], writes=[bu], out=u[:, 0:2], in_=uhalo[:, c, :])
                    yield
                    O("dve", "tensor_tensor", reads=[bxin], writes=[bb[6], bu], out=u[:, 2:TB + 2], in0=banks[6][:], in1=xin_sb[:], op=ALU.mult)
                    O("pool", "tensor_copy", reads=[bu], writes=[buh], out=uhalo[:, c, :], in_=u[:, TB:TB + 2])
                    cw = lambda j, c=c: pv[:, PV_CMW + c * 3 + j:PV_CMW + c * 3 + j + 1]
                    O("dve", "tensor_scalar", reads=[bu, bpv], writes=[bt1], out=t1[:], in0=u[:, 0:TB], scalar1=cw(0), scalar2=None, op0=ALU.mult)
                    O("dve", "scalar_tensor_tensor", reads=[bu, bpv], writes=[bt1], out=t1[:], in0=u[:, 1:TB + 1], scalar=cw(1), in1=t1[:], op0=ALU.mult, op1=ALU.add)
                    O("dve", "scalar_tensor_tensor", reads=[bu, bpv], writes=[bt1], out=t1[:], in0=u[:, 2:TB + 2], scalar=cw(2), in1=t1[:], op0=ALU.mult, op1=ALU.add)
                    O("dve", "tensor_tensor", reads=[bt1], writes=[bb[7], bcv], out=cv[:], in0=banks[7][:], in1=t1[:], op=ALU.mult)
                    yield
                    O("act", "activation", reads=[bcv], writes=[bsqb], out=sqb[:], in_=cv[:], func=AF.Square)
                    yield
                    mmg([dict(out=banks[5][:], lhsT=G64[:], rhs=sqb[:], start=True, stop=True)], [bsqb, bconst], [bb[5]])
                    yield
                    O("act", "activation", reads=[bconst], writes=[bb[5], bt1], out=t1[:], in_=banks[5][:], func=AF.Ln, scale=1.0 / 64, bias=epsT[:, 0:1])
                    O("act", "activation", writes=[bt1], out=t1[:], in_=t1[:], func=AF.Exp, scale=-0.5)
                    yield
                    O("dve", "scalar_tensor_tensor", reads=[bcv, bt1, bpv], writes=[bmixc], out=mixc[:, c, :], in0=cv[:],
                      scalar=pv[:, PV_COG + c:PV_COG + c + 1], in1=t1[:], op0=ALU.mult, op1=ALU.mult)
                    yield
                if debug and tb == 0:
                    dbg_out("d_hT", hT[:].rearrange("p c t -> p (c t)"), bhT, eng="pool")
                    dbg_out("d_qT", qT[:].rearrange("p h t -> p (h t)"), bqT, eng="pool")
                    dbg_out("d_gates", gates[:].rearrange("p s c -> p (s c)"), bgates, eng="pool")
                    dbg_out("d_ksT", ksT[:, :, 0:TB], bks[0], eng="pool", dst=dbg["d_ksT"].rearrange("p (g t) -> p g t", g=2))

            def attn_items(items, obank, qT, bqT, h):
                first = [True]
                n = len(items)
                LOOK = 2
                sb_of = {}

                def emit_S(ix):
                    it = items[ix]
                    bk = rot["s"] % 3
                    rot["s"] += 1
                    sb_of[ix] = bk
                    c0, c1 = it["c0"], it["c1"]
                    mm = [dict(out=banks[bk][:, c0:c1], lhsT=it["kT"], rhs=qT[0:68, h, c0:c1], start=True, stop=False, skip=True)]
                    for (lt, rh, e0, e1) in it["extra"]:
                        mm.append(dict(out=banks[bk][:, e0:e1], lhsT=lt, rhs=rh, start=False, stop=False, skip=True))
                    mm[-1]["stop"] = True
                    mmg(mm, it["reads"] + [bqT, bconst], [bb[bk]])

                for ix in range(min(LOOK, n)):
                    emit_S(ix)
                for ix in range(n):
                    if ix + LOOK < n:
                        emit_S(ix + LOOK)
                    it = items[ix]
                    bk = sb_of[ix]
                    c0, c1 = it["c0"], it["c1"]
                    pi = rot["pt"] % NPT
                    rot["pt"] += 1
                    O("act", "activation", writes=[bb[bk], bPT[pi]], out=PT[pi][:, c0:c1], in_=banks[bk][:, c0:c1], func=AF.Exp)
                    mm = []
                    for s in range(c0 // 128, c1 // 128):
                        mm.append(dict(out=it["oview"](s), lhsT=PT[pi][:, s * 128:(s + 1) * 128], rhs=it["vrhs"],
                                       start=first[0], stop=(ix == n - 1), skip=True))
                        first[0] = False
                    mmg(mm, [bPT[pi]] + it["vreads"], [bb[obank]])
                    tick(pend.get("k", 1))

            def tail_g(tb):
                mixc, bmixc = mixcs[tb % 2], bmixcs[tb % 2]
                pend["tail"] = True
                for c in range(4):
                    tp4 = banks[7][:].bitcast(BF16)[:, 0:512]
                    for s in range(4):
                        O("pe", "transpose", reads=[battnb, bconst], writes=[bb[7]], signal=(s == 3), out=tp4[:, s * 128:(s + 1) * 128],
                          in_=attnb[:, s, c * 128:(c + 1) * 128], identity=ident_bf[:])
                    yield
                    O("act", "copy", writes=[bb[7], bmixa], out=mixa[:, c, :], in_=tp4)
                    yield
                for s in range(4):
                    ti = tb * 4 + s
                    i = ti % 2
                    P.dma("sp", lambda E, i=i, ti=ti: E.dma_start(out=xr[i][:], in_=x_t[ti]), writes=[bxr[i]])
                    for hf in range(2):
                        bk = 5 + hf
                        mm = []
                        for kc in range(8):
                            lt = mixa[:, kc, s * 128:(s + 1) * 128] if kc < 4 else mixc[:, kc - 4, s * 128:(s + 1) * 128]
                            mm.append(dict(out=banks[bk][:], lhsT=lt, rhs=wout[:, kc, hf * 512:(hf + 1) * 512], start=(kc == 0), stop=(kc == 7)))
                        mmg(mm, [bmixa, bmixc, bwout], [bb[bk]])
                    yield
                    for hf in range(2):
                        bk = 5 + hf
                        O("dve", "tensor_tensor", reads=[bg1], writes=[bb[bk], btmpo[hf]], out=tmpo[hf][:], in0=banks[bk][:], in1=g1row[:, hf * 512:(hf + 1) * 512], op=ALU.mult)
                    yield
                    for hf in range(2):
                        O("pool", "tensor_tensor", reads=[btmpo[hf]], writes=[bxr[i]], out=xr[i][:, hf * 512:(hf + 1) * 512], in0=xr[i][:, hf * 512:(hf + 1) * 512],
                          in1=tmpo[hf][:], op=ALU.add)
                    P.dma("sp", lambda E, i=i, ti=ti: E.dma_start(out=out_t[ti], in_=xr[i][:]), reads=[bxr[i]], writes=[obuf[ti]])
                    if s == 3:
                        pend["tail"] = False
                    yield

            def chain(*gens):
                for g_ in gens:
                    if g_ is not None:
                        yield from g_

            if N2 > 0:
                pend["gen"] = frontend(0)
                drain()
            for tb in range(N2):
                qT, bqT = qTs[tb % 2], bqTs[tb % 2]
                gates, bgates = gatess[tb % 2], bgatess[tb % 2]
                mixc, bmixc = mixcs[tb % 2], bmixcs[tb % 2]
                if tb + 1 < N2 or tb >= 1:
                    pend["gen"] = chain(tail_g(tb - 1) if (tb >= 1 and not debug) else None, frontend(tb + 1) if tb + 1 < N2 else None)
                    nticks = 8 * ((4 * tb + 4) + (4 if tb == 0 else 8)) + 8
                    pend["k"] = max(1, -(-165 // nticks))
                for g in range(2):
                    ctiles = [0] if tb < 4 else [0, 1]
                    for hi in range(4):
                        h = g * 4 + hi
                        pbase = 1 if hi % 2 == 0 else 3
                        ptl = {}
                        for tl in ctiles:
                            bk = 0
                            mm = [dict(out=banks[bk][:], lhsT=kcT[0:68, g, tl * 128:(tl + 1) * 128], rhs=qT[0:68, h, :], start=True, stop=False, skip=True)]
                            dl = 32 * tb - 128 * tl
                            if 16 * dl < 2048:
                                mm.append(dict(out=banks[bk][:], lhsT=ident_bf[:], rhs=Lc[:, 16 * dl:16 * dl + 512], start=False, stop=False, skip=True))
                            mm[-1]["stop"] = True
                            mmg(mm, [bkcT, bqT, bconst], [bb[bk]])
                            pi = rot["pt"] % NPT
                            rot["pt"] += 1
                            ptl[tl] = pi
                            O("act", "activation", writes=[bb[bk], bPT[pi]], out=PT[pi][:], in_=banks[bk][:], func=AF.Exp)
                        for half in range(2):
                            ob = pbase + half
                            mm = []
                            for s2 in range(2):
                                s = half * 2 + s2
                                for k_, tl in enumerate(ctiles):
                                    mm.append(dict(out=banks[ob][:, s2 * 256:s2 * 256 + 129], lhsT=PT[ptl[tl]][:, s * 128:(s + 1) * 128],
                                                   rhs=vcx[:, tl, g, :], start=(k_ == 0), stop=(k_ == len(ctiles) - 1)))
                            mmg(mm, [bPT[ptl[tl]] for tl in ctiles] + [bvcx], [bb[ob]])
                        for half in range(2):
                            ob = pbase + half
                            ov = banks[ob][:].rearrange("p (s c) -> p s c", s=2)
                            sl = slice(half * 2, half * 2 + 2)
                            r0_ = half * 2
                            if tb == 0:
                                O("dve", "tensor_scalar", writes=[bb[ob], brw], out=rw[:, r0_:r0_ + 2], in0=ov[:, :, 64], scalar1=1e-30, scalar2=None, op0=ALU.max)
                                O("dve", "reciprocal", writes=[brw], out=rw[:, r0_:r0_ + 2], in_=rw[:, r0_:r0_ + 2])
                            else:
                                O("dve", "reciprocal", writes=[bb[ob], brw], out=rw[:, r0_:r0_ + 2], in_=ov[:, :, 64])
                            O("dve", "tensor_tensor", reads=[bgates], writes=[brw], out=rw[:, 12 + r0_:14 + r0_], in0=rw[:, r0_:r0_ + 2], in1=gates[:, sl, h * 3 + 0], op=ALU.mult)
                            O("dve", "tensor_tensor", reads=[brw], writes=[bb[ob], bacc], out=acc[:, sl, hi, :], in0=ov[:, :, 0:64], in1=bc(rw[:, 12 + r0_:14 + r0_], 64), op=ALU.mult)
                            O("dve", "tensor_tensor", reads=[brw], writes=[bb[ob], bUraw], out=Uraw[:, sl, hi, :], in0=ov[:, :, 65:129], in1=bc(rw[:, r0_:r0_ + 2], 64), op=ALU.mult)
                        tick(pend.get("k", 1))
                    O("dve", "tensor_reduce", reads=[bUraw], writes=[bimp], out=imp[:], in_=Uraw.rearrange("p s h j -> p s j h"), axis=AX.X, op=ALU.add)
                    if debug and tb == 1 and g == 0:
                        dbg_out("d_imp", imp[:].rearrange("p s j -> p (s j)"), bimp, eng="pool")
                    for s in range(4):
                        for hf in range(2):
                            cur = tb * 8 + s * 2 + hf
                            rows = slice(hf * 64, hf * 64 + 64)
                            if cur + 1 < 64:
                                O("pool", "memset", writes=[bimp], ap=imp[rows, s, cur + 1:64], constant=-1.0)
                            O("pool", "memset", writes=[bimp], ap=imp[rows, s, max(cur - 1, 0):cur + 1], constant=1.0e4)
                            O("pool", "memset", writes=[bimp], ap=imp[rows, s, 0:1], constant=1.0e4)
                    for s in range(4):
                        O("dve", "max", reads=[bimp], writes=[btk], out=m8[:, 0:8], in_=imp[:, s, :])
                        O("dve", "match_replace", reads=[bimp, btk], writes=[btk], out=wk[:], in_to_replace=m8[:, 0:8], in_values=imp[:, s, :], imm_value=-1.0e30)
                        O("dve", "max", reads=[btk], writes=[btk], out=m8[:, 8:16], in_=wk[:])
                        O("dve", "tensor_scalar", reads=[bimp, btk], writes=[bselb], out=selb[:, s, :], in0=imp[:, s, :], scalar1=m8[:, 15:16], scalar2=NEG,
                          op0=ALU.is_lt, op1=ALU.mult)
                    if debug and tb == 1 and g == 0:
                        dbg_out("d_selb", selb[:].rearrange("p s j -> p (s j)"), bselb, eng="pool")
                    for hi in range(4):
                        h = g * 4 + hi
                        for br in (2, 1):
                            ob = 3 if br == 1 else 4
                            ovw = banks[ob][:].rearrange("p (s c) -> p s c", s=4)
                            oview = lambda s, ovw=ovw: ovw[:, s, 0:65]
                            items = []
                            if br == 1:
                                for kt in range(4 * tb + 4):
                                    smin = max(0, kt - 4 * tb)
                                    c0 = smin * 128
                                    extra = [(Eall[:, kt * 128:(kt + 1) * 128], selbT[g][:, c0:TB], c0, TB)]
                                    if kt >= 4 * tb:
                                        extra.append((ident_bf[:], Ldiag[:], c0, c0 + 128))
                                    items.append(dict(kT=ksT[0:68, g, kt * 128:(kt + 1) * 128], c0=c0, c1=TB, extra=extra, oview=oview,
                                                      vrhs=vsa[:, kt, g, :], reads=[bks[kt // 4], bksaug, bselbT[g]], vreads=[bvs[kt // 4], bvs1]))
                            else:
                                for m in range(8):
                                    kta = 4 * tb - 4 + m
                                    if kta < 0:
                                        continue
                                    slo, shi = max(0, m - 4), min(3, m)
                                    extra = []
                                    if m >= 4:
                                        extra.append((ident_bf[:], Ldiag[:], (m - 4) * 128, (m - 4) * 128 + 128))
                                    if m <= 3:
                                        extra.append((ident_bf[:], Lfar[:], m * 128, m * 128 + 128))
                                    ri = (kta // 4) % 3
                                    kr = ri * TB + (kta % 4) * 128
                                    items.append(dict(kT=kwT[0:68, g, kr:kr + 128], c0=slo * 128, c1=(shi + 1) * 128, extra=extra, oview=oview,
                                                      vrhs=vwa[:, ri * 4 + kta % 4, g, :], reads=[bkw[ri]], vreads=[bvw[ri], bvw1]))
                            if br == 1 and hi == 0:
                                tbk = rot["s"] % 3
                                rot["s"] += 1
                                tp3 = banks[tbk][0:64, 0:256].bitcast(BF16)
                                for s_ in range(4):
                                    O("pe", "transpose", reads=[bselb, bconst], writes=[bb[tbk]], signal=(s_ == 3), out=tp3[:, s_ * 128:(s_ + 1) * 128],
                                      in_=selb[:, s_, :], identity=ident_bf[:])
                                O("act", "copy", writes=[bb[tbk], bselbT[g]], out=selbT[g][0:64, :], in_=tp3)
                            attn_items(items, ob, qT, bqT, h)
                            O("dve", "reciprocal", writes=[bb[ob], brw], out=rw[:, 4:8], in_=ovw[:, :, 64])
                            O("dve", "tensor_tensor", reads=[bgates], writes=[brw], out=rw[:, 8:12], in0=rw[:, 4:8], in1=gates[:, :, h * 3 + br], op=ALU.mult)
                            O("dve", "tensor_tensor", reads=[brw], writes=[bb[ob], bscr], out=scr[:, 0:256].rearrange("p (s d) -> p s d", s=4),
                              in0=ovw[:, :, 0:64], in1=bc(rw[:, 8:12], 64), op=ALU.mult)
                            O("pool", "tensor_tensor", reads=[bscr], writes=[bacc], out=acc[:, :, hi, :], in0=acc[:, :, hi, :],
                              in1=scr[:, 0:256].rearrange("p (s d) -> p s d", s=4), op=ALU.add)
                    if debug and tb == 1 and g == 0:
                        dbg_out("d_acc", acc[:].rearrange("p s h d -> p (s h d)"), bacc, eng="pool")
                    while pend.get("tail"):
                        tick()
                    accf = acc[:].rearrange("p s h d -> p (s h d)")
                    O("dve", "tensor_tensor", reads=[bacc], writes=[bscr], out=scr[:], in0=accf, in1=accf, op=ALU.mult)
                    O("dve", "tensor_reduce", reads=[bscr], writes=[bssq2], out=ssq2[:, 0:16], in_=scr[:].rearrange("p (a d) -> p a d", d=64), axis=AX.X, op=ALU.add)
                    rstd_small(ssq2[:, 0:16], 64, 16, [bssq2])
                    O("dve", "tensor_tensor", reads=[bacc, bssq2], writes=[bscr], out=scr[:].rearrange("p (a d) -> p a d", d=64),
                      in0=acc[:].rearrange("p s h d -> p (s h) d"), in1=bc(ssq2[:, 0:16], 64), op=ALU.mult)
                    O("dve", "tensor_tensor", reads=[bscr, brv], writes=[battnb], out=attnb[:, :, g * 256:(g + 1) * 256],
                      in0=scr[:].rearrange("p (s c) -> p s c", s=4),
                      in1=rv[:, RV_AOG + g * 256:RV_AOG + (g + 1) * 256].unsqueeze(1).broadcast_to([128, 4, 256]), op=ALU.mult)
                drain()
                if debug and tb == 1:
                    dbg_out("d_attnb", attnb[:].rearrange("p s c -> p (s c)"), battnb, eng="pool")
                if debug:
                    pend["gen"] = tail_g(tb)
                    drain()
            if N2 > 0 and not debug:
                pend["gen"] = tail_g(N2 - 1)
                drain()
            P.run()

        stW.close()

        with ExitStack() as st:
            sb = lambda n, s, d=F32: st.enter_context(nc.sbuf_tensor("ph3_" + n, list(s), d))
            g2row, bg2 = make_row(st, "g2row", 40)
            wg = sb("wg", [128, 8, DFF], BF16); wu = sb("wu", [128, 8, DFF], BF16); wd = sb("wd", [128, NFF, D], BF16)
            bwg, bwu, bwd = Buf("wg"), Buf("wu"), Buf("wd")
            wg_v = wg_d.rearrange("(kc p) n -> p kc n", p=128); wu_v = wu_d.rearrange("(kc p) n -> p kc n", p=128)
            wd_v = wd_d.rearrange("(c p) n -> p c n", p=128)
            bwgc = [Buf(f"wg{i}") for i in range(2)]; bwuc = [Buf(f"wu{i}") for i in range(2)]; bwdc = [Buf(f"wd{i}") for i in range(11)]
            def load_w(pc):
                c0 = pc * 1408
                P.dma("pool", lambda E, c0=c0: E.dma_start(out=wg[:, :, c0:c0 + 1408], in_=wg_v[:, :, c0:c0 + 1408]), writes=[bwgc[pc]], ndesc=1024)
                P.dma("pool", lambda E, c0=c0: E.dma_start(out=wu[:, :, c0:c0 + 1408], in_=wu_v[:, :, c0:c0 + 1408]), writes=[bwuc[pc]], ndesc=1024)

            def load_wd(pc):
                P.dma("pool", lambda E, pc=pc: E.dma_start(out=wd[:, 2 * pc:2 * pc + 2, :], in_=wd_v[:, 2 * pc:2 * pc + 2, :]), writes=[bwdc[pc]], ndesc=256)
            load_w(0)
            NS = TBF // 128
            x1ts = [sb(f"x1t{i}", [128, NS, D]) for i in range(2)]; bx1s = [[Buf(f"x1t{i}_{j}") for j in range(NS)] for i in range(2)]
            hTs = [sb(f"hT3_{i}", [128, 8, TBF + 2], BF16) for i in range(2)]; bhTs = [Buf("hT3_0"), Buf("hT3_1")]
            env = dict(rr=0, xs=None, bxs=None,
                       xn=[sb("xn0", [128, D], BF16), sb("xn1", [128, D], BF16)], bxn=[Buf("xn0"), Buf("xn1")],
                       junk=sb("junk", [128, D], BF16), bjunk=Buf("junk"), ss=sb("ss", [128, 4]), bss=Buf("ss"))
            act = sb("act", [128, NFF, TBF], BF16); bact = [Buf(f"act{i}") for i in range(NFF)]
            NSL = 3
            GU = [(0, 1), (2, 3), (4, 5)]
            A0 = [sb(f"A0_{i}", [128, TBF]) for i in range(NSL)]; A1 = [sb(f"A1_{i}", [128, TBF]) for i in range(NSL)]
            bA = [Buf(f"A{i}") for i in range(NSL)]
            t2 = [sb(f"t2_{i}", [128, TBF]) for i in range(NSL)]; bt2 = [Buf(f"t2{i}") for i in range(NSL)]
            tmpo = [sb("tmpf0", [128, 512]), sb("tmpf1", [128, 512])]; btmpo = [Buf("tf0"), Buf("tf1")]
            O("pool", "memset", writes=[bhTs[0]], ap=hTs[0][:, :, 0:2], constant=0.0)
            fw = lambda j, c: pv[:, PV_CFW + c * 3 + j:PV_CFW + c * 3 + j + 1]

            def prep(fb):
                i = fb % 2
                make_hT(env, out_t, fb * NS, NS, ab[:, 8:16], mod[:, 24:32], hTs[i][:, :, 2:TBF + 2], bhTs[i], keep=x1ts[i], bkeep=bx1s[i],
                        src_bufs=obuf, tbank=6)
                if fb > 0:
                    O("pool", "tensor_copy", reads=[bhTs[1 - i]], writes=[bhTs[i]], out=hTs[i][:, :, 0:2], in_=hTs[1 - i][:, :, TBF:TBF + 2])

            def ffn_A(c, hT, bhT):
                sl = c % NSL
                bg_, bu_ = GU[sl]
                pc = (c * 128) // 1408
                pc2 = (c * 128 + 127) // 1408
                rd = [bwgc[pc], bwuc[pc], bhT] + ([bwgc[pc2], bwuc[pc2]] if pc2 != pc else [])
                mmg([dict(out=banks[bg_][:, 0:TBF + 2], lhsT=wg[:, kc, c * 128:(c + 1) * 128], rhs=hT[:, kc, 0:TBF + 2], start=(kc == 0), stop=(kc == 7)) for kc in range(8)],
                    rd, [bb[bg_]])
                mmg([dict(out=banks[bu_][:, 0:TBF], lhsT=wu[:, kc, c * 128:(c + 1) * 128], rhs=hT[:, kc, 2:TBF + 2], start=(kc == 0), stop=(kc == 7)) for kc in range(8)],
                    rd, [bb[bu_]])

            def ffn_B(c):
                sl = c % NSL
                bg_, bu_ = GU[sl]
                G = banks[bg_]
                O("act", "activation", reads=[bpv], writes=[bb[bg_], bA[sl]], out=A0[sl][:], in_=G[:, 0:TBF], func=AF.Identity, scale=fw(0, c))
                O("act", "activation", reads=[bpv], writes=[bb[bg_], bA[sl]], out=A1[sl][:], in_=G[:, 1:TBF + 1], func=AF.Identity, scale=fw(1, c))
                O("dve", "scalar_tensor_tensor", reads=[bA[sl], bpv], writes=[bb[bg_], bt2[sl]], out=t2[sl][:], in0=G[:, 2:TBF + 2], scalar=fw(2, c),
                  in1=A1[sl][:], op0=ALU.mult, op1=ALU.add)
                O("pool", "tensor_tensor", reads=[bA[sl]], writes=[bt2[sl]], out=t2[sl][:], in0=t2[sl][:], in1=A0[sl][:], op=ALU.add)

            def ffn_C(c):
                sl = c % NSL
                bg_, bu_ = GU[sl]
                O("act", "activation", writes=[bt2[sl]], out=t2[sl][:], in_=t2[sl][:], func=AF.Silu)
                O("dve", "tensor_tensor", reads=[bt2[sl]], writes=[bb[bu_], bact[c]], out=act[:, c, :], in0=banks[bu_][:, 0:TBF], in1=t2[sl][:], op=ALU.mult)

            if NB > 0:
                prep(0)
            for fb in range(NB):
                i = fb % 2
                x1t, bx1 = x1ts[i], bx1s[i]
                for c in range(NFF + 2):
                    if c < NFF:
                        ffn_A(c, hTs[i], bhTs[i])
                    if 1 <= c <= NFF:
                        ffn_B(c - 1)
                    if c >= 2:
                        ffn_C(c - 2)
                    if c == 6 and fb + 1 < NB:
                        prep(fb + 1)
                    if fb == 0:
                        if c == 1:
                            load_w(1)
                        if c >= 9 and c - 9 < 11:
                            load_wd(c - 9)
                for s in range(NS):
                    ti = fb * NS + s
                    for hf in range(2):
                        bk = 6 + hf
                        for c in range(NFF):
                            O("pe", "matmul", reads=[bact[c], bwdc[c // 2]], writes=[bb[bk]], signal=(c == NFF - 1), out=banks[bk][:],
                              lhsT=act[:, c, s * 128:(s + 1) * 128], rhs=wd[:, c, hf * 512:(hf + 1) * 512], start=(c == 0), stop=(c == NFF - 1))
                        O("dve", "tensor_tensor", reads=[bg2], writes=[bb[bk], btmpo[hf]], out=tmpo[hf][:], in0=banks[bk][:], in1=g2row[:, hf * 512:(hf + 1) * 512], op=ALU.mult)
                        O("pool", "tensor_tensor", reads=[btmpo[hf]], writes=[bx1[s]], out=x1t[:, s, hf * 512:(hf + 1) * 512], in0=x1t[:, s, hf * 512:(hf + 1) * 512],
                          in1=tmpo[hf][:], op=ALU.add)
                    tk = P.dma("sp", lambda E, s=s, ti=ti, x1t=x1t: E.dma_start(out=out_t[ti], in_=x1t[:, s, :]), reads=[bx1[s]], writes=[obuf[ti]])
                    final_toks.append(tk)
            for tk in final_toks:
                P.wait("sp", tk)
            P.run()
        print("instr counts", P.ninstr, "sems", P.nsem)
    return nc


def _alibi_slopes(n):
    return np.array([2.0 ** (-8.0 * (h + 1) / n) for h in range(n)], dtype=np.float32)


def _const_tables():
    t = np.arange(S)
    sl = _alibi_slopes(8)
    qaug = np.zeros((4, 8, S), np.float32)
    for h in range(8):
        qaug[0, h] = -64.0 * sl[h] * (t // 64)
        qaug[1, h] = -sl[h] * (t % 64)
        qaug[2, h] = 64.0 * sl[h]
        qaug[3, h] = sl[h]
    kaug = np.zeros((4, S), np.float32)
    kaug[0] = 1.0; kaug[1] = 1.0; kaug[2] = t // 64; kaug[3] = t % 64
    npr = np.arange(256)
    cend = 16 * npr + 15
    caug = np.zeros((4, 256), np.float32)
    caug[0] = 1.0; caug[1] = 1.0; caug[2] = cend // 64; caug[3] = cend % 64
    caug[2, 0] = -10000.0
    n_cmp = (S - 32) // 16 + 1
    cs = np.arange(n_cmp) * 16
    bs = np.arange(64) * 64
    overlap = ((cs[None, :] < bs[:, None] + 64) & (cs[None, :] + 32 > bs[:, None])).astype(np.float32)
    ovfull = np.zeros((256, 64), np.float32)
    ovfull[1:1 + n_cmp] = overlap.T
    ovT = np.ascontiguousarray(ovfull.reshape(2, 128, 64).transpose(1, 0, 2))
    return qaug, kaug, caug, ovT


_NC_CACHE = {}


def kernel(**inputs):
    f = lambda k: np.ascontiguousarray(np.asarray(inputs[k], dtype=np.float32))
    x = f("x"); c = f("c")
    B = x.shape[0]
    col = lambda v: np.ascontiguousarray(v.reshape(-1, 128).T)
    rep = lambda v: np.broadcast_to(v.reshape(1, -1), (128, v.size))
    qaug, kaug, caug, ovT = _const_tables()
    b_ada = f("b_ada")[0]
    cmw = f("conv_mix_w")[0]
    cfw = f("conv_ffn_w")[0]
    shared_pv = [col(b_ada), col(f("norm1_g")[0]), col(f("norm2_g")[0]),
                 cmw.reshape(3, 4, 128).transpose(2, 1, 0).reshape(128, 12),
                 col(f("conv_out_g")[0]),
                 np.tile(f("k_norm_cmp_g")[0].reshape(64, 1), (2, 1))]
    cfw_l = cfw.reshape(3, NFF, 128).transpose(2, 1, 0).reshape(128, 66)
    rowvecs = np.ascontiguousarray(np.concatenate([rep(f("b_gate")[0]), rep(f("q_norm_g")[0]), rep(f("k_norm_slc_g")[0]),
                                                   rep(f("k_norm_win_g")[0]), rep(f("attn_out_g")[0])], axis=1), dtype=np.float32)
    posT = np.ascontiguousarray(np.concatenate([np.tile(f("pos_cmp_k")[0].T, (2, 1)), np.tile(f("pos_cmp_v")[0].T, (2, 1))], axis=1), dtype=np.float32)
    common = {
        "w_ada": f("w_ada")[0], "rowvecs": rowvecs, "w_in": f("w_in")[0], "posT": posT,
        "w_cmp_k1": f("w_cmp_k1")[0], "w_cmp_v1": f("w_cmp_v1")[0], "w_cmp_k2": f("w_cmp_k2")[0], "w_cmp_v2": f("w_cmp_v2")[0],
        "w_out": f("w_out")[0], "w_ffn_gate": f("w_ffn_gate")[0], "w_ffn_up": f("w_ffn_up")[0], "w_ffn_down": f("w_ffn_down")[0],
        "qaug": qaug, "kaug": kaug, "caug": caug, "ovT": ovT,
    }
    in_maps = []
    for b in range(B):
        cT = np.repeat(col(c[b]), 2, axis=1)
        pvecs = np.ascontiguousarray(np.concatenate(shared_pv + [cT, cfw_l], axis=1), dtype=np.float32)
        assert pvecs.shape == (128, 163), pvecs.shape
        m = dict(common)
        m["x"] = x[b]
        m["pvecs"] = pvecs
        in_maps.append(m)
    if "nc" not in _NC_CACHE:
        _NC_CACHE["nc"] = build_program(DEBUG)
    nc = _NC_CACHE["nc"]
    res = run_bass_kernel_spmd(nc, in_maps, core_ids=list(range(B)))
    if DEBUG:
        kernel.last = res.results
    return np.stack([np.asarray(r["out"], dtype=np.float32) for r in res.results], axis=0)
```
